# Optimizing a Trainium2 kernel written in Bass

```python
import jax
import jax.numpy as jnp
from jax import lax
import numpy as np


D_MODEL = 4096
BATCH = 2
SEQ = 8192
DEPTH = 2

CTX_LEN = 256
GRID_W = 64

A_HEAD_DIM = 128
A_HEADS = (D_MODEL // 2) // A_HEAD_DIM
A_KV_HEADS = 4
A_WINDOW = 128
A_BLOCK = 128
ROPE_THETA = 10000.0
B_WIDTH = D_MODEL // 2
B_HEAD_DIM = 64
B_HEADS = B_WIDTH // B_HEAD_DIM
DECAY_LORA = 96
AAA_LORA = 96
GN_EPS = 64e-5
C_WIDTH = 3 * D_MODEL
C_CHUNK = 128
C_GROUPS = 16
LN_EPS = 1e-5
NORM_EPS = 1e-6

A_Q = A_HEADS * A_HEAD_DIM
A_KV = A_KV_HEADS * A_HEAD_DIM
AB_WIDTHS = (A_Q, A_KV, A_KV, A_Q, B_WIDTH, B_WIDTH, B_WIDTH, B_WIDTH, 2 * DECAY_LORA, 2 * AAA_LORA)
AB_IN = sum(AB_WIDTHS)
AB_SPLITS = tuple(int(s) for s in np.cumsum(AB_WIDTHS)[:-1])
AB_MIX = A_Q + B_WIDTH
N_EVEN = (DEPTH + 1) // 2
N_ODD = DEPTH // 2

kernel_name = 'hybrid_swa_rwkv7_gmlp_dit_block'


def rms_norm(x, g):
    xf = x.astype(jnp.float32)
    y = xf * lax.rsqrt(jnp.mean(xf * xf, axis=-1, keepdims=True) + NORM_EPS)
    return (y * g.astype(jnp.float32)).astype(x.dtype)


def layer_norm(x, g, b, eps):
    xf = x.astype(jnp.float32)
    xc = xf - jnp.mean(xf, axis=-1, keepdims=True)
    var = jnp.mean(xc * xc, axis=-1, keepdims=True)
    return xc * lax.rsqrt(var + eps) * g.astype(jnp.float32) + b.astype(jnp.float32)


def centred_conv3(x, w):
    xp = jnp.pad(x, ((0, 0), (1, 1), (0, 0)))
    return xp[:, :-2] * w[0] + xp[:, 1:-1] * w[1] + xp[:, 2:] * w[2]


def axial_rope_angles(n_tok):
    rows = n_tok // GRID_W
    row = jnp.repeat(jnp.arange(rows, dtype=jnp.float32), GRID_W)
    col = jnp.tile(jnp.arange(GRID_W, dtype=jnp.float32), rows)
    n_freq = A_HEAD_DIM // 4
    inv_freq = ROPE_THETA ** (-jnp.arange(n_freq, dtype=jnp.float32) / n_freq)
    return row[:, None] * inv_freq, col[:, None] * inv_freq


def rotate_half_pairs(x, ang):
    cos = jnp.cos(ang)[None, :, None, :].astype(x.dtype)
    sin = jnp.sin(ang)[None, :, None, :].astype(x.dtype)
    x1, x2 = jnp.split(x, 2, axis=-1)
    return jnp.concatenate([x1 * cos - x2 * sin, x2 * cos + x1 * sin], axis=-1)


def axial_rope(x, ang_row, ang_col):
    x_row, x_col = jnp.split(x, 2, axis=-1)
    return jnp.concatenate([rotate_half_pairs(x_row, ang_row), rotate_half_pairs(x_col, ang_col)], axis=-1)


def band_context_attention(q, k, v, kc, vc, sink):
    B, T, HQ, Dh = q.shape
    HK = k.shape[2]
    G = HQ // HK
    nb = T // A_BLOCK
    f32 = jnp.float32
    qb = q.reshape(B, nb, A_BLOCK, HK, G, Dh).astype(f32) * (Dh ** -0.5)

    def windows(t):
        tp = jnp.pad(t, ((0, 0), (A_BLOCK, A_BLOCK), (0, 0), (0, 0))).reshape(B, nb + 2, A_BLOCK, HK, Dh)
        return jnp.concatenate([tp[:, i:i + nb] for i in range(3)], axis=2).astype(f32)

    kw, vw = windows(k), windows(v)
    qpos = jnp.arange(nb)[:, None, None] * A_BLOCK + jnp.arange(A_BLOCK)[None, :, None]
    kpos = (jnp.arange(nb)[:, None, None] - 1) * A_BLOCK + jnp.arange(3 * A_BLOCK)[None, None, :]
    valid = (jnp.abs(qpos - kpos) <= A_WINDOW) & (kpos >= 0) & (kpos < T)
    s_band = jnp.einsum('bnqhgd,bnkhd->bhgnqk', qb, kw)
    s_band = jnp.where(valid, s_band, -jnp.inf)
    s_ctx = jnp.einsum('bnqhgd,blhd->bhgnql', qb, kc.astype(f32))
    s_sink = sink.astype(f32).reshape(1, HK, G, 1, 1, 1)
    m = jnp.maximum(jnp.maximum(s_band.max(-1, keepdims=True), s_ctx.max(-1, keepdims=True)), s_sink)
    p_band = jnp.exp(s_band - m)
    p_ctx = jnp.exp(s_ctx - m)
    den = jnp.exp(s_sink - m) + p_band.sum(-1, keepdims=True) + p_ctx.sum(-1, keepdims=True)
    o = (jnp.einsum('bhgnqk,bnkhd->bnqhgd', p_band / den, vw)
         + jnp.einsum('bhgnql,blhd->bnqhgd', p_ctx / den, vc.astype(f32)))
    return o.reshape(B, T, HQ * Dh).astype(q.dtype)


def context_attention(q, k, v, sink):
    B, L, HQ, Dh = q.shape
    HK = k.shape[2]
    G = HQ // HK
    f32 = jnp.float32
    qf = q.reshape(B, L, HK, G, Dh).astype(f32) * (Dh ** -0.5)
    s = jnp.einsum('blhgd,bmhd->bhglm', qf, k.astype(f32))
    s_sink = jnp.broadcast_to(sink.astype(f32).reshape(1, HK, G, 1, 1), s.shape[:-1] + (1,))
    p = jax.nn.softmax(jnp.concatenate([s_sink, s], axis=-1), axis=-1)[..., 1:]
    o = jnp.einsum('bhglm,bmhd->blhgd', p, v.astype(f32))
    return o.reshape(B, L, HQ * Dh).astype(q.dtype)


def rwkv_prep(z_r, z_k, z_v, z_w, z_a, conv_w, w0, w2, a0, a2, k_k, k_a):
    f32 = jnp.float32
    B, T, C = z_r.shape
    rkv = centred_conv3(jnp.concatenate([z_r, z_k, z_v], axis=-1), conv_w).astype(f32)
    r, k, v = jnp.split(rkv, 3, axis=-1)

    def heads(t):
        return t.reshape(t.shape[:-1] + (B_HEADS, B_HEAD_DIM))

    kk = heads(k * k_k)
    kk = kk / jnp.maximum(jnp.sqrt(jnp.sum(kk * kk, axis=-1, keepdims=True)), 1e-12)
    lw = jnp.tanh(z_w.astype(f32).reshape(B, T, 2, DECAY_LORA))
    la = z_a.astype(f32).reshape(B, T, 2, AAA_LORA)
    w_log = -jax.nn.softplus(-(w0 + jnp.einsum('btdr,drc->btdc', lw, w2))) - 0.5
    decay = jnp.exp(-jnp.exp(w_log))
    a = jax.nn.sigmoid(a0 + jnp.einsum('btdr,drc->btdc', la, a2))
    k_dir = k[:, :, None] * (1.0 + (a - 1.0) * k_a)
    return heads(r), heads(k), heads(v), kk, heads(decay), heads(k_dir), heads(a)


def wkv_scan(state, w, k, v, kk, a, r=None, reverse=False):
    def tm(t):
        return jnp.swapaxes(t, 0, 1)

    def update(S, w_t, k_t, v_t, kk_t, a_t):
        sk = jnp.einsum('bhvk,bhk->bhv', S, kk_t)
        return (S * w_t[:, :, None, :] - sk[..., None] * (kk_t * a_t)[:, :, None, :]
                + v_t[..., None] * k_t[:, :, None, :])

    xs = tuple(tm(t) for t in (w, k, v, kk, a))
    if r is None:
        S, _ = lax.scan(lambda S, inp: (update(S, *inp), None), state, xs, reverse=reverse)
        return S, None

    def step(S, inp):
        S = update(S, *inp[:5])
        return S, jnp.einsum('bhvk,bhk->bhv', S, inp[5])

    S, ys = lax.scan(step, state, xs + (tm(r),), reverse=reverse)
    return S, tm(ys)


def rwkv_readout(y, r, k, v, r_k, gn_w, gn_b):
    y = layer_norm(y, gn_w.reshape(B_HEADS, B_HEAD_DIM), gn_b.reshape(B_HEADS, B_HEAD_DIM), GN_EPS)
    y = y + jnp.sum(r * k * r_k.reshape(B_HEADS, B_HEAD_DIM).astype(jnp.float32), axis=-1, keepdims=True) * v
    return y.reshape(y.shape[:2] + (B_WIDTH,))


def ab_mixer(h_lat, h_ctx, ang_row, ang_col, w_in, w_out, sink, conv_w, w0, w2, a0, a2,
             k_k, k_a, r_k, gn_w, gn_b, ctx_out):
    zl = jnp.split(h_lat @ w_in, AB_SPLITS, axis=-1)
    zc = jnp.split(h_ctx @ w_in, AB_SPLITS, axis=-1)

    def ah(t):
        return t.reshape(t.shape[:2] + (-1, A_HEAD_DIM))

    q_l = axial_rope(ah(zl[0]), ang_row, ang_col)
    k_l = axial_rope(ah(zl[1]), ang_row, ang_col)
    v_l = ah(zl[2])
    k_c, v_c = ah(zc[1]), ah(zc[2])
    o_a = band_context_attention(q_l, k_l, v_l, k_c, v_c, sink) * jax.nn.silu(zl[3])

    r_l, kb_l, vb_l, kk_l, dec_l, kd_l, a_l = rwkv_prep(zl[4], zl[5], zl[6], zl[8], zl[9],
                                                        conv_w, w0, w2, a0, a2, k_k, k_a)
    r_c, kb_c, vb_c, kk_c, dec_c, kd_c, a_c = rwkv_prep(zc[4], zc[5], zc[6], zc[8], zc[9],
                                                        conv_w, w0, w2, a0, a2, k_k, k_a)
    B = h_lat.shape[0]
    S0 = jnp.zeros((B, B_HEADS, B_HEAD_DIM, B_HEAD_DIM), jnp.float32)
    ys_l, ys_c = [], []
    for d, rev in enumerate((False, True)):
        S_c, y_c = wkv_scan(S0, dec_c[:, :, d], kd_c[:, :, d], vb_c, kk_c, a_c[:, :, d],
                            r_c if ctx_out else None, reverse=rev)
        _, y_l = wkv_scan(S_c, dec_l[:, :, d], kd_l[:, :, d], vb_l, kk_l, a_l[:, :, d], r_l, reverse=rev)
        ys_l.append(y_l)
        ys_c.append(y_c)
    o_b = rwkv_readout(ys_l[0] + ys_l[1], r_l, kb_l, vb_l, r_k, gn_w, gn_b).astype(h_lat.dtype)
    o_b = o_b * jax.nn.silu(zl[7])
    out_l = jnp.concatenate([o_a, o_b], axis=-1) @ w_out
    if not ctx_out:
        return out_l, None
    o_ac = context_attention(ah(zc[0]), k_c, v_c, sink) * jax.nn.silu(zc[3])
    o_bc = rwkv_readout(ys_c[0] + ys_c[1], r_c, kb_c, vb_c, r_k, gn_w, gn_b).astype(h_ctx.dtype)
    o_bc = o_bc * jax.nn.silu(zc[7])
    out_c = jnp.concatenate([o_ac, o_bc], axis=-1) @ w_out
    return out_l, out_c


def gmlp_branch(h, w_in, ln_g, ln_b, w_s, b_s, w_out):
    B, T, _ = h.shape
    nc = T // C_CHUNK
    u, v, g = jnp.split(h @ w_in, 3, axis=-1)
    u = jax.nn.gelu(u)
    v = layer_norm(jax.nn.gelu(v), ln_g, ln_b, LN_EPS).astype(h.dtype)
    vb = v.reshape(B, nc, C_CHUNK, C_GROUPS, C_WIDTH // C_GROUPS)
    vm = jnp.einsum('gij,bnjgc->bnigc', w_s, vb) + b_s.T[:, :, None]
    y = u * vm.reshape(B, T, C_WIDTH) * jax.nn.silu(g)
    return y @ w_out


def setup_inputs(seed: int = 0) -> dict:
    key = jax.random.key(seed)
    keys = iter(jax.random.split(key, 40))

    def nrm(shape, std):
        return jax.random.normal(next(keys), shape, jnp.float32) * std

    D = D_MODEL
    conv_base = jnp.array([0.25, 0.5, 0.25], jnp.float32)[None, :, None]
    return {
        'x': nrm((BATCH, SEQ, D), 1.0),
        'c': nrm((BATCH, D), 1.0),
        'ctx': nrm((BATCH, CTX_LEN, D), 1.0),
        'c_ctx': nrm((D,), 1.0),
        'mod_w': nrm((DEPTH, D, 3 * D), 0.5 * D ** -0.5),
        'mod_b': nrm((DEPTH, 3 * D), 0.02),
        'norm_g': 1.0 + nrm((DEPTH, D), 0.02),
        'ab_w_in': nrm((N_EVEN, D, AB_IN), D ** -0.5),
        'ab_w_out': nrm((N_EVEN, AB_MIX, D), AB_MIX ** -0.5),
        'attn_sink': nrm((N_EVEN, A_HEADS), 1.0),
        'rwkv_conv': conv_base + nrm((N_EVEN, 3, 3 * B_WIDTH), 0.1),
        'rwkv_w0': -1.0 + nrm((N_EVEN, 2, B_WIDTH), 1.0),
        'rwkv_w2': nrm((N_EVEN, 2, DECAY_LORA, B_WIDTH), 0.5 * DECAY_LORA ** -0.5),
        'rwkv_a0': nrm((N_EVEN, 2, B_WIDTH), 0.5),
        'rwkv_a2': nrm((N_EVEN, 2, AAA_LORA, B_WIDTH), 0.5 * AAA_LORA ** -0.5),
        'rwkv_k_k': 0.85 + nrm((N_EVEN, B_WIDTH), 0.05),
        'rwkv_k_a': 1.0 + nrm((N_EVEN, B_WIDTH), 0.05),
        'rwkv_r_k': nrm((N_EVEN, B_WIDTH), 0.1),
        'rwkv_gn_w': 1.0 + nrm((N_EVEN, B_WIDTH), 0.02),
        'rwkv_gn_b': nrm((N_EVEN, B_WIDTH), 0.02),
        'gm_w_in': nrm((N_ODD, D, 3 * C_WIDTH), D ** -0.5),
        'gm_ln_g': 1.0 + nrm((N_ODD, C_WIDTH), 0.02),
        'gm_ln_b': nrm((N_ODD, C_WIDTH), 0.02),
        'gm_w_s': nrm((N_ODD, C_GROUPS, C_CHUNK, C_CHUNK), C_CHUNK ** -0.5),
        'gm_b_s': 1.0 + nrm((N_ODD, C_GROUPS, C_CHUNK), 0.02),
        'gm_w_out': nrm((N_ODD, C_WIDTH, D), C_WIDTH ** -0.5),
        'final_g': 1.0 + nrm((D,), 0.02),
    }


def reference(x, c, ctx, c_ctx, mod_w, mod_b, norm_g, ab_w_in, ab_w_out, attn_sink, rwkv_conv,
              rwkv_w0, rwkv_w2, rwkv_a0, rwkv_a2, rwkv_k_k, rwkv_k_a, rwkv_r_k, rwkv_gn_w, rwkv_gn_b,
              gm_w_in, gm_ln_g, gm_ln_b, gm_w_s, gm_b_s, gm_w_out, final_g):
    n_lat = x.shape[1]
    ang_row, ang_col = axial_rope_angles(n_lat)
    cond_lat = jax.nn.silu(c)[:, None, :]
    cond_ctx = jax.nn.silu(c_ctx)
    for l in range(DEPTH):
        i = l // 2
        ctx_read_later = any(j % 2 == 0 for j in range(l + 1, DEPTH))
        shift, scale, gate = jnp.split(cond_lat @ mod_w[l] + mod_b[l], 3, axis=-1)
        h = rms_norm(x, norm_g[l]) * (1.0 + scale) + shift
        if l % 2 == 0 or ctx_read_later:
            shift_c, scale_c, gate_c = jnp.split(cond_ctx @ mod_w[l] + mod_b[l], 3, axis=-1)
            h_c = rms_norm(ctx, norm_g[l]) * (1.0 + scale_c) + shift_c
        if l % 2 == 0:
            y, y_c = ab_mixer(h, h_c, ang_row, ang_col, ab_w_in[i], ab_w_out[i], attn_sink[i],
                              rwkv_conv[i], rwkv_w0[i], rwkv_w2[i], rwkv_a0[i], rwkv_a2[i],
                              rwkv_k_k[i], rwkv_k_a[i], rwkv_r_k[i], rwkv_gn_w[i], rwkv_gn_b[i],
                              ctx_read_later)
        else:
            y = gmlp_branch(h, gm_w_in[i], gm_ln_g[i], gm_ln_b[i], gm_w_s[i], gm_b_s[i], gm_w_out[i])
            y_c = (gmlp_branch(h_c, gm_w_in[i], gm_ln_g[i], gm_ln_b[i], gm_w_s[i], gm_b_s[i], gm_w_out[i])
                   if ctx_read_later else None)
        x = x + gate * y
        if ctx_read_later:
            ctx = ctx + gate_c * y_c
    return rms_norm(x, final_g)
```

```python
import numpy as np
from contextlib import ExitStack
import concourse.bass as bass
import concourse.mybir as mybir
from concourse.bass_utils import run_bass_kernel_spmd

F32 = mybir.dt.float32
BF16 = mybir.dt.bfloat16
AF = mybir.ActivationFunctionType
ALU = mybir.AluOpType
AX = mybir.AxisListType

D = 4096
KC = D // 128
NCORES = 8


class Buf:
    __slots__ = ("w", "r", "name")

    def __init__(self, name=""):
        self.w = {}
        self.r = {}
        self.name = name


class Emit:
    NSLOT = 12

    def __init__(self, nc, es):
        self.nc = nc
        self.engs = {"pe": nc.tensor, "act": nc.scalar, "dve": nc.vector, "pool": nc.gpsimd, "sp": nc.sync}
        self.sem = {}
        self.cnt = {}
        for e in ("pe", "act", "dve", "pool"):
            self.sem[e] = es.enter_context(nc.semaphore("s_" + e))
            self.cnt[e] = 0
        self.qn = {}
        for q in ("sp", "act", "pool"):
            for i in range(self.NSLOT):
                k = "d_%s_%d" % (q, i)
                self.sem[k] = es.enter_context(nc.semaphore(k))
                self.cnt[k] = 0
            self.qn[q] = 0
        self.waited = {e: {} for e in self.engs}

    def _wait(self, eng, deps):
        for k, v in deps.items():
            if v <= 0:
                continue
            if k == "pe" and eng == "pe":
                continue
            if self.waited[eng].get(k, 0) >= v:
                continue
            self.engs[eng].wait_ge(self.sem[k], v)
            self.waited[eng][k] = v

    @staticmethod
    def _merge(dst, src):
        for k, v in src.items():
            if dst.get(k, 0) < v:
                dst[k] = v

    def _deps(self, reads, writes):
        deps = {}
        for b in reads:
            self._merge(deps, b.w)
        for b in writes:
            self._merge(deps, b.w)
            self._merge(deps, b.r)
        return deps

    def _commit(self, key, val, reads, writes):
        for b in reads:
            if b.r.get(key, 0) < val:
                b.r[key] = val
        for b in writes:
            if b.r:
                b.w = {key: val}
                b.r = {}
            else:
                b.w[key] = val

    def op(self, eng, fn, reads=(), writes=()):
        self._wait(eng, self._deps(reads, writes))
        ins = fn(self.engs[eng])
        self.cnt[eng] += 1
        ins.then_inc(self.sem[eng], 1)
        self._commit(eng, self.cnt[eng], reads, writes)

    def dma(self, q, out, in_, reads=(), writes=(), **kw):
        slot = self.qn[q] % self.NSLOT
        self.qn[q] += 1
        key = "d_%s_%d" % (q, slot)
        deps = self._deps(reads, writes)
        if self.cnt[key] > 0:
            deps[key] = max(deps.get(key, 0), self.cnt[key])
        self._wait(q, deps)
        ins = self.engs[q].dma_start(out=out, in_=in_, **kw)
        self.cnt[key] += 16
        ins.then_inc(self.sem[key], 16)
        self._commit(key, self.cnt[key], reads, writes)

    def barrier(self):
        allc = {k: v for k, v in self.cnt.items() if v > 0}
        for e in self.engs:
            self._wait(e, dict(allc))

    def finish(self, bufs):
        deps = {}
        for b in bufs:
            self._merge(deps, b.w)
        self._wait("sp", deps)


def _new_nc():
    return bass.Bass("TRN2", target_bir_lowering=False)


MODC = 3 * D // NCORES


def build_mod():
    nc = _new_nc()
    condT = nc.dram_tensor("condT", [128, KC, 3], F32, kind="ExternalInput").ap()
    mw = nc.dram_tensor("mw", [2, D, MODC], F32, kind="ExternalInput").ap()
    mb = nc.dram_tensor("mb", [2, 3, MODC], F32, kind="ExternalInput").ap()
    out = nc.dram_tensor("out", [2, 3, MODC], F32, kind="ExternalOutput").ap()
    with ExitStack() as es:
        em = Emit(nc, es)
        sc = es.enter_context(nc.sbuf_tensor("sc", [128, KC, 3], F32))
        wt = [es.enter_context(nc.sbuf_tensor("wt%d" % i, [128, KC, 512], F32)) for i in range(2)]
        bt = es.enter_context(nc.sbuf_tensor("bt", [3, 2, MODC], F32))
        ot = es.enter_context(nc.sbuf_tensor("ot", [3, 2, MODC], F32))
        ps = [es.enter_context(nc.psum_tensor("ps%d" % i, [128, 512], F32)) for i in range(2)]
        b_sc, b_bt, b_ot = Buf("sc"), Buf("bt"), Buf("ot")
        b_wt = [Buf("wt0"), Buf("wt1")]
        b_ps = [Buf("ps0"), Buf("ps1")]
        em.dma("sp", sc[:], condT[:, :, :], writes=[b_sc])
        em.dma("sp", bt[:], mb.rearrange("l r n -> r l n"), writes=[b_bt])
        em.op("act", lambda e: e.activation(out=sc[:], in_=sc[:], func=AF.Silu), reads=[b_sc], writes=[b_sc])
        it = 0
        for l in range(2):
            for nt in range(MODC // 512):
                j = it % 2
                it += 1
                em.dma("sp" if j == 0 else "pool", wt[j][:],
                       mw[l, :, nt * 512:(nt + 1) * 512].rearrange("(k p) n -> p k n", p=128), writes=[b_wt[j]])
                for k in range(KC):
                    em.op("pe", lambda e, k=k, j=j: e.matmul(ps[j][0:3, :], lhsT=sc[:, k, :], rhs=wt[j][:, k, :],
                                                              start=(k == 0), stop=(k == KC - 1)),
                          reads=[b_sc, b_wt[j]], writes=[b_ps[j]])
                em.op("dve", lambda e, j=j, l=l, nt=nt: e.tensor_tensor(
                    out=ot[:, l, nt * 512:(nt + 1) * 512], in0=ps[j][0:3, :], in1=bt[:, l, nt * 512:(nt + 1) * 512],
                    op=ALU.add), reads=[b_ps[j], b_bt], writes=[b_ot])
        em.dma("sp", out.rearrange("l r n -> r l n"), ot[:], reads=[b_ot], writes=[b_ot])
        em.finish([b_ot])
    return nc


def run_mod(c, c_ctx, mod_w, mod_b, ncores=NCORES):
    cond = np.concatenate([c, c_ctx[None, :]], axis=0).astype(np.float32)
    condT = np.ascontiguousarray(cond.reshape(3, KC, 128).transpose(2, 1, 0))
    in_maps = []
    for i in range(ncores):
        sl = slice(i * MODC, (i + 1) * MODC)
        in_maps.append({
            "condT": condT,
            "mw": np.ascontiguousarray(mod_w[:, :, sl]),
            "mb": np.ascontiguousarray(np.broadcast_to(mod_b[:, None, sl], (2, 3, MODC))),
        })
    nc = build_mod()
    res = run_bass_kernel_spmd(nc, in_maps, core_ids=list(range(ncores)))
    return np.concatenate([r["out"] for r in res.results], axis=2)


_UID = [0]


def _u(name):
    _UID[0] += 1
    return "%s_u%d" % (name, _UID[0])


class Ring:
    def __init__(self, nc, es, name, shape, dtype, n, psum=False):
        alloc = nc.psum_tensor if psum else nc.sbuf_tensor
        name = _u(name)
        self.t = [es.enter_context(alloc("%s_%d" % (name, i), shape, dtype)) for i in range(n)]
        self.b = [Buf("%s%d" % (name, i)) for i in range(n)]
        self.i = 0

    def next(self):
        j = self.i % len(self.t)
        self.i += 1
        return self.t[j], self.b[j]


def mm(em, out, lhsT, rhs, start, stop, reads, writes):
    em.op("pe", lambda e: e.matmul(out, lhsT=lhsT, rhs=rhs, start=start, stop=stop), reads=reads, writes=writes)


def act(em, out, in_, func, reads, writes, scale=None, bias=None, accum_out=None):
    kw = {}
    if scale is not None:
        kw["scale"] = scale
    if bias is not None:
        kw["bias"] = bias
    if accum_out is not None:
        kw["accum_out"] = accum_out
    em.op("act", lambda e: e.activation(out=out, in_=in_, func=func, **kw), reads=reads, writes=writes)


def tt(em, out, in0, in1, op, reads, writes, eng="dve"):
    em.op(eng, lambda e: e.tensor_tensor(out=out, in0=in0, in1=in1, op=op), reads=reads, writes=writes)


def stt(em, out, in0, scalar, in1, op0, op1, reads, writes):
    em.op("dve", lambda e: e.scalar_tensor_tensor(out=out, in0=in0, scalar=scalar, in1=in1, op0=op0, op1=op1),
          reads=reads, writes=writes)


def ts(em, out, in0, s1, s2, op0, op1, reads, writes, eng="dve"):
    if s2 is None:
        em.op(eng, lambda e: e.tensor_scalar(out=out, in0=in0, scalar1=s1, scalar2=None, op0=op0),
              reads=reads, writes=writes)
    else:
        em.op(eng, lambda e: e.tensor_scalar(out=out, in0=in0, scalar1=s1, scalar2=s2, op0=op0, op1=op1),
              reads=reads, writes=writes)


def cast(em, out, in_, reads, writes, eng="pool"):
    em.op(eng, lambda e: e.tensor_copy(out=out, in_=in_), reads=reads, writes=writes)


GRP_W = 768
NORM_EPS = 1e-6
LN_EPS = 1e-5


def build_l3(T, NG, TP):
    CW = NG * GRP_W
    NCC = CW // 128
    NCT = CW // 512
    NTB = TP // 128
    NTS = TP // 512
    NPASS = T // TP
    TS = 256
    NS = TP // TS
    nc = _new_nc()
    xT = nc.dram_tensor("xT", [128, KC, T], F32, kind="ExternalInput").ap()
    oT = nc.dram_tensor("oT", [128, KC, T], BF16, kind="ExternalInput").ap()
    w_out = nc.dram_tensor("w_out", [D, D], F32, kind="ExternalInput").ap()
    gw_in = nc.dram_tensor("gw_in", [D, 3 * CW], F32, kind="ExternalInput").ap()
    gw_out = nc.dram_tensor("gw_out", [CW, D], F32, kind="ExternalInput").ap()
    vec32 = nc.dram_tensor("vec32", [128, 6, KC], F32, kind="ExternalInput").ap()
    lnv = nc.dram_tensor("lnv", [128, 2, NCC], F32, kind="ExternalInput").ap()
    wsT = nc.dram_tensor("wsT", [128, NG, 128], F32, kind="ExternalInput").ap()
    bsb = nc.dram_tensor("bsb", [128, NG, 128], F32, kind="ExternalInput").ap()
    outT = nc.dram_tensor("outT", [128, KC, T], F32, kind="ExternalOutput").ap()
    xnewD = nc.dram_tensor("xnewD", [128, KC, T], F32).ap()
    xfinD = nc.dram_tensor("xfinD", [128, KC, TP], F32).ap()
    gvD = nc.dram_tensor("gvD", [TP, CW], BF16).ap()
    yD = nc.dram_tensor("yD", [NCC, 128, TP], BF16).ap()
    with ExitStack() as es:
        em = Emit(nc, es)
        sb = lambda name, shape, dt: es.enter_context(nc.sbuf_tensor(name, shape, dt))
        PS = Ring(nc, es, "ps", [128, 512], F32, 8, psum=True)
        v32 = sb("v32", [128, 6, KC], F32)
        lnt = sb("lnt", [128, 2, NCC], F32)
        gs1 = sb("gs1", [128, KC], F32)
        ones = sb("ones", [128, 128], F32)
        wsf = sb("wsf", [128, NG, 128], F32)
        wsb = sb("wsb", [128, NG, 128], BF16)
        rsb = sb("rsb", [128, NG, 128], F32)
        bsbt = sb("bsbt", [128, NG, 128], F32)
        stat = sb("stat", [128, 4, NTB], F32)
        h1T = sb("h1T", [128, KC, TP], BF16)
        b_c, b_out, b_st = Buf("consts"), Buf("out"), Buf("stats")
        em.dma("sp", v32[:], vec32[:, :, :], writes=[b_c])
        em.dma("sp", lnt[:], lnv[:, :, :], writes=[b_c])
        em.dma("sp", wsf[:], wsT[:, :, :], writes=[b_c])
        em.dma("sp", bsbt[:], bsb[:, :, :], writes=[b_c])
        em.op("dve", lambda e: e.memset(ones[:], 1.0), writes=[b_c])
        ts(em, gs1[:], v32[:, 2, :], 1.0, None, ALU.add, None, [b_c], [b_c])
        tt(em, gs1[:], gs1[:], v32[:, 1, :], ALU.mult, [b_c], [b_c])
        cast(em, wsb[:], wsf[:], [b_c], [b_c], eng="dve")
        wsf2 = wsf[:].rearrange("p g i -> p (g i)")
        rsb2 = rsb[:].rearrange("p g i -> p (g i)")
        for q in range((NG * 128 + 511) // 512):
            c0, c1 = q * 512, min(NG * 128, (q + 1) * 512)
            pt, pb = PS.next()
            mm(em, pt[:, 0:c1 - c0], ones[:, :], wsf2[:, c0:c1], True, True, [b_c], [pb])
            cast(em, rsb2[:, c0:c1], pt[:, 0:c1 - c0], [pb], [b_c], eng="dve")
        gate0, shift1, gate1, fing = v32[:, 0, :], v32[:, 3, :], v32[:, 4, :], v32[:, 5, :]
        h1b = [[Buf("h1_%d_%d" % (s, n)) for n in range(KC)] for s in range(NS)]
        xnew_b = [Buf("xnewD%d" % p) for p in range(NPASS)]
        gvb = [Buf("gvD%d" % tb) for tb in range(NTB)]
        yb = [Buf("yD%d" % cc) for cc in range(NCC)]
        xfb = [Buf("xfin%d" % n) for n in range(KC)]

        def rmsnorm(sqr, tmpr, rs_t, rs_b, xt, xb, emit_chunk):
            pt, pb = PS.next()
            for n in range(KC):
                sq, sqb = sqr.next()
                act(em, sq[:], xt[:, n, :], AF.Square, [xb[n]], [sqb])
                mm(em, pt[:, 0:TS], ones[:, :], sq[:], n == 0, n == KC - 1, [sqb, b_c], [pb])
            act(em, rs_t[:], pt[:, 0:TS], AF.Sqrt, [pb], [rs_b], scale=1.0 / D, bias=NORM_EPS)
            em.op("dve", lambda e: e.reciprocal(out=rs_t[:], in_=rs_t[:]), reads=[rs_b], writes=[rs_b])
            for n in range(KC):
                tmp, tmpb = tmpr.next()
                tt(em, tmp[:], xt[:, n, :], rs_t[:], ALU.mult, [xb[n], rs_b], [tmpb])
                emit_chunk(n, tmp, tmpb)

        for ps_i in range(NPASS):
            P0 = ps_i * TP
            em.barrier()
            with ExitStack() as ph:
                psb = lambda name, shape, dt: ph.enter_context(nc.sbuf_tensor(_u(name), shape, dt))
                xt = psb("p1_xt", [128, KC, TS], F32)
                ot = psb("p1_ot", [128, KC, TS], BF16)
                wfr = Ring(nc, ph, "p1_wf", [128, KC, 128], F32, 2)
                wbr = Ring(nc, ph, "p1_wb", [128, KC, 128], BF16, 2)
                sqr = Ring(nc, ph, "p1_sq", [128, TS], F32, 2)
                tmpr = Ring(nc, ph, "p1_tmp", [128, TS], F32, 2)
                rs_t = psb("p1_rs", [128, TS], F32)
                rs_b = Buf("rs")
                xb = [Buf("xt%d" % n) for n in range(KC)]
                ob = Buf("ot")
                for s in range(NS):
                    t0 = P0 + s * TS
                    em.dma("sp", xt[:], xT[:, :, t0:t0 + TS], writes=xb)
                    em.dma("sp", ot[:], oT[:, :, t0:t0 + TS], writes=[ob])
                    for n in range(KC):
                        wf, wfb = wfr.next()
                        em.dma("sp", wf[:],
                               w_out[:, n * 128:(n + 1) * 128].rearrange("(k p) n -> p k n", p=128), writes=[wfb])
                        wb, wbb = wbr.next()
                        cast(em, wb[:], wf[:], [wfb], [wbb], eng="dve")
                        pt, pb = PS.next()
                        for k in range(KC):
                            mm(em, pt[:, 0:TS], wb[:, k, :], ot[:, k, :], k == 0, k == KC - 1, [wbb, ob], [pb])
                        stt(em, xt[:, n, :], pt[:, 0:TS], gate0[:, n:n + 1], xt[:, n, :], ALU.mult, ALU.add,
                            [pb, b_c, xb[n]], [xb[n]])
                    em.dma("pool", xnewD[:, :, t0:t0 + TS], xt[:], reads=xb, writes=[xnew_b[ps_i]])

                    def emit_h1(n, tmp, tmpb, s=s):
                        act(em, h1T[:, n, s * TS:(s + 1) * TS], tmp[:], AF.Identity, [tmpb, b_c], [h1b[s][n]],
                            scale=gs1[:, n:n + 1], bias=shift1[:, n:n + 1])
                    rmsnorm(sqr, tmpr, rs_t, rs_b, xt, xb, emit_h1)
                em.barrier()
            with ExitStack() as ph:
                psb = lambda name, shape, dt: ph.enter_context(nc.sbuf_tensor(_u(name), shape, dt))
                wfr = Ring(nc, ph, "v_wf", [128, 8, 512], F32, 2)
                wbr = Ring(nc, ph, "v_wb", [128, KC, 512], BF16, 2)
                gvr = Ring(nc, ph, "v_gv", [128, 512], BF16, 3)
                jkr = Ring(nc, ph, "v_jk", [128, 512], BF16, 2)
                ssum = psb("v_ssum", [128, NTB, NCT], F32)
                ssq = psb("v_ssq", [128, NTB, NCT], F32)
                st2 = psb("v_st2", [128, 2, NTB], F32)
                for ct in range(NCT):
                    col0 = CW + ct * 512
                    wb, wbb = wbr.next()
                    for q in range(4):
                        wf, wfb = wfr.next()
                        em.dma("sp", wf[:],
                               gw_in[q * 1024:(q + 1) * 1024, col0:col0 + 512].rearrange("(k p) n -> p k n", p=128),
                               writes=[wfb])
                        cast(em, wb[:, q * 8:(q + 1) * 8, :], wf[:], [wfb], [wbb], eng="dve")
                    for tb in range(NTB):
                        pt, pb = PS.next()
                        s0 = (tb * 128) // TS
                        for k in range(KC):
                            mm(em, pt[:, :], h1T[:, k, tb * 128:(tb + 1) * 128], wb[:, k, :], k == 0, k == KC - 1,
                               [wbb, h1b[s0][k]], [pb])
                        gv, gb = gvr.next()
                        act(em, gv[:], pt[:, :], AF.Gelu_apprx_tanh, [pb], [gb, b_st],
                            accum_out=ssum[:, tb, ct:ct + 1])
                        jk, jb = jkr.next()
                        act(em, jk[:], gv[:], AF.Square, [gb], [jb, b_st], accum_out=ssq[:, tb, ct:ct + 1])
                        em.dma("pool", gvD[tb * 128:(tb + 1) * 128, ct * 512:(ct + 1) * 512], gv[:],
                               reads=[gb], writes=[gvb[tb]])
                em.op("dve", lambda e: e.tensor_reduce(out=stat[:, 0, :], in_=ssum[:], axis=AX.X, op=ALU.add),
                      reads=[b_st], writes=[b_st])
                em.op("dve", lambda e: e.tensor_reduce(out=stat[:, 1, :], in_=ssq[:], axis=AX.X, op=ALU.add),
                      reads=[b_st], writes=[b_st])
                ts(em, stat[:, 0, :], stat[:, 0, :], 1.0 / CW, None, ALU.mult, None, [b_st], [b_st])
                ts(em, stat[:, 1, :], stat[:, 1, :], 1.0 / CW, None, ALU.mult, None, [b_st], [b_st])
                tt(em, st2[:, 0, :], stat[:, 0, :], stat[:, 0, :], ALU.mult, [b_st], [b_st])
                tt(em, st2[:, 1, :], stat[:, 1, :], st2[:, 0, :], ALU.subtract, [b_st], [b_st])
                act(em, stat[:, 2, :], st2[:, 1, :], AF.Sqrt, [b_st], [b_st], scale=1.0, bias=LN_EPS)
                em.op("dve", lambda e: e.reciprocal(out=stat[:, 2, :], in_=stat[:, 2, :]), reads=[b_st], writes=[b_st])
                tt(em, st2[:, 0, :], stat[:, 0, :], stat[:, 2, :], ALU.mult, [b_st], [b_st])
                ts(em, stat[:, 3, :], st2[:, 0, :], -1.0, None, ALU.mult, None, [b_st], [b_st])
                em.barrier()
            with ExitStack() as ph:
                wfr2 = Ring(nc, ph, "u_wf", [128, KC, 128], F32, 2)
                wbr2 = Ring(nc, ph, "u_wb", [128, KC, 128], BF16, 3)
                utr = Ring(nc, ph, "u_ut", [128, TP], F32, 2)
                gtr = Ring(nc, ph, "u_gt", [128, TP], F32, 2)
                ytr = Ring(nc, ph, "u_yt", [128, TP], BF16, 2)
                gvtr = Ring(nc, ph, "u_gvt", [128, 128], BF16, 4)
                ntr = Ring(nc, ph, "u_nt", [128, 128], BF16, 4)
                addr = Ring(nc, ph, "u_add", [128, 128], F32, 2)
                vmr = Ring(nc, ph, "u_vm", [128, 512], F32, 2)
                for cc in range(NCC):
                    g = cc // (GRP_W // 128)
                    wbs = []
                    for which, cbase in ((0, 0), (1, 2 * CW)):
                        wf, wfb = wfr2.next()
                        em.dma("sp", wf[:],
                               gw_in[:, cbase + cc * 128:cbase + (cc + 1) * 128].rearrange("(k p) n -> p k n", p=128),
                               writes=[wfb])
                        wb, wbb = wbr2.next()
                        cast(em, wb[:], wf[:], [wfb], [wbb], eng="dve")
                        wbs.append((wb, wbb))
                    ut, ub = utr.next()
                    gt, gtb = gtr.next()
                    for tsi in range(NTS):
                        for (wb, wbb), dst, dstb, fn in ((wbs[0], ut, ub, AF.Gelu_apprx_tanh), (wbs[1], gt, gtb, AF.Silu)):
                            pt, pb = PS.next()
                            for k in range(KC):
                                rd = [wbb] + [h1b[s_][k] for s_ in range(tsi * 512 // TS, (tsi + 1) * 512 // TS)]
                                mm(em, pt[:, :], wb[:, k, :], h1T[:, k, tsi * 512:(tsi + 1) * 512], k == 0, k == KC - 1,
                                   rd, [pb])
                            act(em, dst[:, tsi * 512:(tsi + 1) * 512], pt[:, :], fn, [pb], [dstb])
                    addt, addb = addr.next()
                    stt(em, addt[:], rsb[:, g, :], lnt[:, 1, cc:cc + 1], bsbt[:, g, :], ALU.mult, ALU.add, [b_c], [addb])
                    yt, ytb = ytr.next()
                    for hf in range(NTS):
                        pt, pb = PS.next()
                        for j4 in range(4):
                            tb = hf * 4 + j4
                            gvt, gvtb = gvtr.next()
                            em.dma("sp", gvt[:],
                                   gvD[tb * 128:(tb + 1) * 128, cc * 128:(cc + 1) * 128], reads=[gvb[tb]], writes=[gvtb])
                            nt, ntb = ntr.next()
                            act(em, nt[:], gvt[:], AF.Identity, [gvtb, b_st], [ntb],
                                scale=stat[:, 2, tb:tb + 1], bias=stat[:, 3, tb:tb + 1])
                            mm(em, pt[:, j4 * 128:(j4 + 1) * 128], nt[:, :], wsb[:, g, :], True, True, [ntb, b_c], [pb])
                        vm, vmb = vmr.next()
                        stt(em, vm[:].rearrange("p (a i) -> p a i", a=4), pt[:, :].rearrange("p (a i) -> p a i", a=4),
                            lnt[:, 0, cc:cc + 1], addt[:, None, :].broadcast_to([128, 4, 128]), ALU.mult, ALU.add,
                            [pb, b_c, addb], [vmb])
                        tt(em, vm[:], vm[:], ut[:, hf * 512:(hf + 1) * 512], ALU.mult, [vmb, ub], [vmb])
                        tt(em, yt[:, hf * 512:(hf + 1) * 512], vm[:], gt[:, hf * 512:(hf + 1) * 512], ALU.mult,
                           [vmb, gtb], [ytb])
                    em.dma("pool", yD[cc], yt[:], reads=[ytb], writes=[yb[cc]])
                em.barrier()
            with ExitStack() as ph:
                w2fr = Ring(nc, ph, "o_wf", [128, 256], F32, 3)
                w2br = Ring(nc, ph, "o_wb", [128, 256], BF16, 3)
                ytr = Ring(nc, ph, "o_yt", [128, TP], BF16, 3)
                xnr = Ring(nc, ph, "o_xn", [128, TP], F32, 2)
                for ng in range(KC // 2):
                    pss = [[PS.next() for _ in range(NTS)] for _ in range(2)]
                    for cc in range(NCC):
                        wf, wfb = w2fr.next()
                        em.dma("sp", wf[:], gw_out[cc * 128:(cc + 1) * 128, ng * 256:(ng + 1) * 256], writes=[wfb])
                        wb, wbb = w2br.next()
                        cast(em, wb[:], wf[:], [wfb], [wbb], eng="dve")
                        yt, ytb = ytr.next()
                        em.dma("sp", yt[:], yD[cc], reads=[yb[cc]], writes=[ytb])
                        for nn in range(2):
                            for tsi in range(NTS):
                                pt, pb = pss[nn][tsi]
                                mm(em, pt[:, :], wb[:, nn * 128:(nn + 1) * 128], yt[:, tsi * 512:(tsi + 1) * 512],
                                   cc == 0, cc == NCC - 1, [wbb, ytb], [pb])
                    for nn in range(2):
                        n = ng * 2 + nn
                        xn, xnb = xnr.next()
                        em.dma("sp", xn[:], xnewD[:, n, P0:P0 + TP], reads=[xnew_b[ps_i]], writes=[xnb])
                        for tsi in range(NTS):
                            pt, pb = pss[nn][tsi]
                            stt(em, xn[:, tsi * 512:(tsi + 1) * 512], pt[:, :], gate1[:, n:n + 1],
                                xn[:, tsi * 512:(tsi + 1) * 512], ALU.mult, ALU.add, [pb, b_c, xnb], [xnb])
                        em.dma("pool", xfinD[:, n, :], xn[:], reads=[xnb], writes=[xfb[n]])
                em.barrier()
            with ExitStack() as ph:
                psb = lambda name, shape, dt: ph.enter_context(nc.sbuf_tensor(_u(name), shape, dt))
                xt = psb("f_xt", [128, KC, TS], F32)
                ores = psb("f_o", [128, KC, TS], F32)
                sqr = Ring(nc, ph, "f_sq", [128, TS], F32, 2)
                tmpr = Ring(nc, ph, "f_tmp", [128, TS], F32, 2)
                rs_t = psb("f_rs", [128, TS], F32)
                rs_b = Buf("frs")
                xb = [Buf("fx%d" % n) for n in range(KC)]
                orb = [Buf("fo%d" % n) for n in range(KC)]
                for s in range(NS):
                    em.dma("sp", xt[:], xfinD[:, :, s * TS:(s + 1) * TS], reads=xfb, writes=xb)

                    def emit_out(n, tmp, tmpb):
                        ts(em, ores[:, n, :], tmp[:], fing[:, n:n + 1], None, ALU.mult, None, [tmpb, b_c], [orb[n]], eng="pool")
                    rmsnorm(sqr, tmpr, rs_t, rs_b, xt, xb, emit_out)
                    em.dma("sp", outT[:, :, P0 + s * TS:P0 + (s + 1) * TS], ores[:], reads=orb, writes=[b_out])
                em.barrier()
        em.finish([b_out])
    return nc


def fm(v, nchunk):
    return np.ascontiguousarray(np.asarray(v, np.float32).reshape(nchunk, 128).T)


def fmT(a):
    t, f = a.shape
    return np.ascontiguousarray(a.reshape(t, f // 128, 128).transpose(2, 1, 0))


def l3_core_inputs(x_tok, oT_tok, w_out, gw_in, gw_out, gate0, norm_g1, scale1, shift1, gate1, final_g,
                   ln_g, ln_b, w_s, b_s):
    NG = w_s.shape[0]
    NCC = NG * GRP_W // 128
    vec32 = np.stack([fm(v, KC) for v in (gate0, norm_g1, scale1, shift1, gate1, final_g)], axis=1)
    lnv = np.stack([fm(ln_g, NCC), fm(ln_b, NCC)], axis=1)
    return {
        "xT": fmT(x_tok), "oT": oT_tok, "w_out": np.ascontiguousarray(w_out), "gw_in": np.ascontiguousarray(gw_in),
        "gw_out": np.ascontiguousarray(gw_out), "vec32": np.ascontiguousarray(vec32), "lnv": np.ascontiguousarray(lnv),
        "wsT": np.ascontiguousarray(np.transpose(w_s, (2, 0, 1))),
        "bsb": np.ascontiguousarray(np.broadcast_to(b_s[None, :, :], (128,) + b_s.shape)),
    }


CHW = [128] * 31 + [96] * 4
CHO = [sum(CHW[:i]) for i in range(len(CHW) + 1)]
NWC = CHO[-1]
A_SCALE = 128 ** -0.5
NEG = -30000.0
CH = 64


def rmsnorm_fm(em, PS, ones, b_c, sqr, tmpr, rs_t, rs_b, xt, xb, N, emit_chunk):
    pt, pb = PS.next()
    for n in range(KC):
        sq, sqb = sqr.next()
        act(em, sq[:, 0:N], xt[:, n, 0:N], AF.Square, [xb[n]], [sqb])
        mm(em, pt[:, 0:N], ones[:, :], sq[:, 0:N], n == 0, n == KC - 1, [sqb, b_c], [pb])
    act(em, rs_t[:, 0:N], pt[:, 0:N], AF.Sqrt, [pb], [rs_b], scale=1.0 / D, bias=NORM_EPS)
    em.op("dve", lambda e: e.reciprocal(out=rs_t[:, 0:N], in_=rs_t[:, 0:N]), reads=[rs_b], writes=[rs_b])
    for n in range(KC):
        tmp, tmpb = tmpr.next()
        tt(em, tmp[:, 0:N], xt[:, n, 0:N], rs_t[:, 0:N], ALU.mult, [xb[n], rs_b], [tmpb])
        emit_chunk(n, tmp, tmpb)


def build_l2(TL, TC, do_b=True, dbg=False):
    TT = TL + TC
    NQB = TL // 128
    nc = _new_nc()
    inp = lambda name, shape, dt=F32: nc.dram_tensor(name, shape, dt, kind="ExternalInput").ap()
    xT = inp("xT", [128, KC, TL])
    cT = inp("cT", [128, KC, TC])
    wcols = inp("wcols", [D, NWC])
    vec = inp("vec", [128, 6, KC])
    sinkb = inp("sinkb", [128, 4])
    cosT = inp("cosT", [128, TL])
    sinT = inp("sinT", [128, TL])
    ident = inp("ident", [128, 128])
    amask = inp("amask", [128, 3, 384])
    oT = nc.dram_tensor("oT", [128, 8, TL], BF16, kind="ExternalOutput").ap()
    hD = nc.dram_tensor("hD", [128, KC, TT], BF16).ap()
    wD = nc.dram_tensor("wD", [128, KC, NWC], BF16).ap()
    qD = nc.dram_tensor("qD", [4, 128, TL], BF16).ap()
    kD = nc.dram_tensor("kD", [128, TT], BF16).ap()
    vD = nc.dram_tensor("vD", [TT, 128], BF16).ap()
    gaD = nc.dram_tensor("gaD", [4, 128, TL], BF16).ap()
    gbD = nc.dram_tensor("gbD", [4, 128, TL], BF16).ap()
    zD = nc.dram_tensor("zD", [12, 128, TT], F32).ap()
    lD = nc.dram_tensor("lD", [4, 96, TT], F32).ap()
    with ExitStack() as es:
        em = Emit(nc, es)
        sb = lambda name, shape, dt: es.enter_context(nc.sbuf_tensor(name, shape, dt))
        PS = Ring(nc, es, "ps", [128, 512], F32, 6, psum=True)
        PB = Ring(nc, es, "pb", [128, 1024], BF16, 2, psum=True)
        b_c, b_out = Buf("consts"), Buf("out")
        vt = sb("vt", [128, 6, KC], F32)
        gsl = sb("gsl", [128, KC], F32)
        gsc = sb("gsc", [128, KC], F32)
        ones = sb("ones", [128, 128], F32)
        idf = sb("idf", [128, 128], F32)
        idb = sb("idb", [128, 128], BF16)
        skb = sb("skb", [128, 4], F32)
        msk = sb("msk", [128, 3, 384], F32)
        em.dma("sp", vt[:], vec[:, :, :], writes=[b_c])
        em.dma("sp", idf[:], ident[:, :], writes=[b_c])
        em.dma("sp", skb[:], sinkb[:, :], writes=[b_c])
        em.dma("sp", msk[:], amask[:, :, :], writes=[b_c])
        em.op("dve", lambda e: e.memset(ones[:], 1.0), writes=[b_c])
        cast(em, idb[:], idf[:], [b_c], [b_c], eng="dve")
        for dst, sidx in ((gsl, 1), (gsc, 3)):
            ts(em, dst[:], vt[:, sidx, :], 1.0, None, ALU.add, None, [b_c], [b_c])
            tt(em, dst[:], dst[:], vt[:, 0, :], ALU.mult, [b_c], [b_c])
        hb = Buf("hD")
        wDb = Buf("wD")
        with ExitStack() as ph:
            wfr = Ring(nc, ph, "w_wf", [128, 8, 512], F32, 3)
            wbr = Ring(nc, ph, "w_wb", [128, 8, 512], BF16, 3)
            ncol = (NWC + 511) // 512
            for ct in range(ncol):
                c0, c1 = ct * 512, min(NWC, (ct + 1) * 512)
                for q in range(4):
                    wf, wfb = wfr.next()
                    em.dma("sp" if q % 2 == 0 else "act", wf[:, :, 0:c1 - c0],
                           wcols[q * 1024:(q + 1) * 1024, c0:c1].rearrange("(k p) n -> p k n", p=128), writes=[wfb])
                    wb, wbb = wbr.next()
                    cast(em, wb[:, :, 0:c1 - c0], wf[:, :, 0:c1 - c0], [wfb], [wbb], eng="dve")
                    em.dma("pool", wD[:, q * 8:(q + 1) * 8, c0:c1], wb[:, :, 0:c1 - c0], reads=[wbb], writes=[wDb])
            em.barrier()
        with ExitStack() as ph:
            psb = lambda name, shape, dt: ph.enter_context(nc.sbuf_tensor(_u(name), shape, dt))
            TS = 256
            xt = psb("h_xt", [128, KC, TS], F32)
            ht = psb("h_ht", [128, KC, TS], BF16)
            sqr = Ring(nc, ph, "h_sq", [128, TS], F32, 2)
            tmpr = Ring(nc, ph, "h_tmp", [128, TS], F32, 2)
            rs_t = psb("h_rs", [128, TS], F32)
            rs_b = Buf("rs")
            xb = [Buf("hx%d" % n) for n in range(KC)]
            hcb = [Buf("hh%d" % n) for n in range(KC)]
            for s in range(TT // TS):
                t0 = s * TS
                isc = t0 >= TL
                src = cT[:, :, t0 - TL:t0 - TL + TS] if isc else xT[:, :, t0:t0 + TS]
                em.dma("sp", xt[:], src, writes=xb)
                gs_, sh_ = (gsc, vt[:, 4, :]) if isc else (gsl, vt[:, 2, :])

                def emit_h(n, tmp, tmpb, gs_=gs_, sh_=sh_):
                    act(em, ht[:, n, :], tmp[:], AF.Identity, [tmpb, b_c], [hcb[n]],
                        scale=gs_[:, n:n + 1], bias=sh_[:, n:n + 1])
                rmsnorm_fm(em, PS, ones, b_c, sqr, tmpr, rs_t, rs_b, xt, xb, TS, emit_h)
                em.dma("act", hD[:, :, t0:t0 + TS], ht[:], reads=hcb, writes=[hb])
            em.barrier()
        qDb, kDb, vDb, gaDb, gbDb, zDb, lDb = (Buf(n) for n in ("qD", "kD", "vD", "gaD", "gbD", "zD", "lD"))
        with ExitStack() as ph:
            psb = lambda name, shape, dt: ph.enter_context(nc.sbuf_tensor(_u(name), shape, dt))
            htr = Ring(nc, ph, "p_h", [128, KC, 512], BF16, 2)
            wr = Ring(nc, ph, "p_w", [128, KC, 128], BF16, 4)
            csr = Ring(nc, ph, "p_cs", [128, 2, 512], F32, 2)
            e32 = Ring(nc, ph, "p_e32", [128, 512], F32, 4)
            e16 = Ring(nc, ph, "p_e16", [128, 512], BF16, 4)
            tiles = [(t0, 512, False) for t0 in range(0, TL, 512)] + [(TL + t0, min(512, TC - t0), True) for t0 in range(0, TC, 512)]
            for (t0, N, isc) in tiles:
                ht, htb = htr.next()
                em.dma("sp", ht[:, :, 0:N], hD[:, :, t0:t0 + N], reads=[hb], writes=[htb])
                if not isc:
                    cs, csb = csr.next()
                    em.dma("sp", cs[:, 0, 0:N], cosT[:, t0:t0 + N], writes=[csb])
                    em.dma("sp", cs[:, 1, 0:N], sinT[:, t0:t0 + N], writes=[csb])

                def proj(ci, swap_roles=False):
                    w, wb_ = wr.next()
                    wd = CHW[ci]
                    em.dma("sp", w[:, :, 0:wd], wD[:, :, CHO[ci]:CHO[ci] + wd], reads=[wDb], writes=[wb_])
                    pt, pb = PS.next()
                    if not swap_roles:
                        for k in range(KC):
                            mm(em, pt[0:wd, 0:N], w[:, k, 0:wd], ht[:, k, 0:N], k == 0, k == KC - 1, [wb_, htb], [pb])
                    else:
                        for blk in range(N // 128):
                            for k in range(KC):
                                mm(em, pt[:, blk * 128:(blk + 1) * 128], ht[:, k, blk * 128:(blk + 1) * 128], w[:, k, :],
                                   k == 0, k == KC - 1, [wb_, htb], [pb])
                    return pt, pb

                def rope_pair(ci, scale, dst_ap, dstbuf):
                    p1, b1 = proj(ci)
                    p2, b2 = proj(ci + (4 if ci < 4 else 1))
                    t1, t1b = e32.next()
                    t2, t2b = e32.next()
                    stt(em, t1[:, 0:N], p1[:, 0:N], scale, cs[:, 0, 0:N], ALU.mult, ALU.mult, [b1, csb], [t1b])
                    stt(em, t2[:, 0:N], p2[:, 0:N], scale, cs[:, 1, 0:N], ALU.mult, ALU.mult, [b2, csb], [t2b])
                    o16, o16b = e16.next()
                    tt(em, o16[:, 0:N], t1[:, 0:N], t2[:, 0:N], ALU.add, [t1b, t2b], [o16b], eng="pool")
                    em.dma("pool", dst_ap, o16[:, 0:N], reads=[o16b], writes=[dstbuf])

                if not isc:
                    for h in range(4):
                        rope_pair(h, A_SCALE, qD[h, :, t0:t0 + N], qDb)
                    rope_pair(8, 1.0, kD[:, t0:t0 + N], kDb)
                else:
                    p1, b1 = proj(8)
                    o16, o16b = e16.next()
                    act(em, o16[:, 0:N], p1[:, 0:N], AF.Copy, [b1], [o16b])
                    em.dma("pool", kD[:, t0:t0 + N], o16[:, 0:N], reads=[o16b], writes=[kDb])
                p1, b1 = proj(10, swap_roles=True)
                o16, o16b = e16.next()
                act(em, o16[:, 0:N], p1[:, 0:N], AF.Copy, [b1], [o16b])
                em.dma("pool", vD[t0:t0 + N, :].rearrange("(b p) d -> p b d", p=128),
                       o16[:, 0:N].rearrange("p (b d) -> p b d", d=128), reads=[o16b], writes=[vDb])
                if not isc:
                    for h in range(4):
                        for ci, dstD, dstb in ((11 + h, gaD, gaDb), (27 + h, gbD, gbDb)):
                            if ci >= 27 and not do_b:
                                continue
                            p1, b1 = proj(ci)
                            o16, o16b = e16.next()
                            act(em, o16[:, 0:N], p1[:, 0:N], AF.Silu, [b1], [o16b])
                            em.dma("pool", dstD[h, :, t0:t0 + N], o16[:, 0:N], reads=[o16b], writes=[dstb])
                if do_b:
                    for j in range(12):
                        p1, b1 = proj(15 + j)
                        o32, o32b = e32.next()
                        act(em, o32[:, 0:N], p1[:, 0:N], AF.Copy, [b1], [o32b])
                        em.dma("pool", zD[j, :, t0:t0 + N], o32[:, 0:N], reads=[o32b], writes=[zDb])
                    for j in range(4):
                        p1, b1 = proj(31 + j)
                        o32, o32b = e32.next()
                        act(em, o32[0:96, 0:N], p1[0:96, 0:N], AF.Tanh if j < 2 else AF.Copy, [b1], [o32b])
                        em.dma("pool", lD[j, :, t0:t0 + N], o32[0:96, 0:N], reads=[o32b], writes=[lDb])
            em.barrier()
        with ExitStack() as ph:
            psb = lambda name, shape, dt: ph.enter_context(nc.sbuf_tensor(_u(name), shape, dt))
            kc = psb("a_kc", [128, TC], BF16)
            vc = psb("a_vc", [128, TC // 128, 128], BF16)
            em.dma("sp", kc[:], kD[:, TL:TL + TC], reads=[kDb], writes=[b_c])
            em.dma("sp", vc[:], vD[TL:TL + TC, :].rearrange("(b p) d -> p b d", p=128), reads=[vDb], writes=[b_c])
            NCB = TC // 128
            NK = 384 + TC
            kbr = Ring(nc, ph, "a_kb", [128, 384], BF16, 2)
            vbr = Ring(nc, ph, "a_vb", [128, 3, 128], BF16, 2)
            qtr = Ring(nc, ph, "a_q", [128, 4, 128], BF16, 2)
            gar = Ring(nc, ph, "a_ga", [128, 4, 128], BF16, 2)
            sbr = Ring(nc, ph, "a_s", [128, 384], F32, 2)
            pr = Ring(nc, ph, "a_p", [128, NK], BF16, 2)
            ptr_ = Ring(nc, ph, "a_pT", [128, NK], BF16, 2)
            smr = Ring(nc, ph, "a_sm", [128, 8], F32, 4)
            otr = Ring(nc, ph, "a_o", [128, 4, 128], BF16, 2)
            for n in range(NQB):
                kb, kbb = kbr.next()
                vb, vbb = vbr.next()
                blks = [min(max(n - 1 + j, 0), NQB - 1) for j in range(3)]
                for j, bk in enumerate(blks):
                    em.dma("sp", kb[:, j * 128:(j + 1) * 128], kD[:, bk * 128:(bk + 1) * 128], reads=[kDb], writes=[kbb])
                    em.dma("act", vb[:, j, :], vD[bk * 128:(bk + 1) * 128, :], reads=[vDb], writes=[vbb])
                mi = 0 if n == 0 else (2 if n == NQB - 1 else 1)
                qt, qtb = qtr.next()
                ga, gab = gar.next()
                em.dma("sp", qt[:], qD[:, :, n * 128:(n + 1) * 128].rearrange("h d q -> d h q"), reads=[qDb], writes=[qtb])
                em.dma("act", ga[:], gaD[:, :, n * 128:(n + 1) * 128].rearrange("h d q -> d h q"), reads=[gaDb], writes=[gab])
                ot, otb = otr.next()
                for h in range(4):
                    p1, b1 = PS.next()
                    p2, b2 = PS.next()
                    mm(em, p1[:, 0:384], qt[:, h, :], kb[:, :], True, True, [qtb, kbb], [b1])
                    mm(em, p2[:, 0:TC], qt[:, h, :], kc[:, :], True, True, [qtb, b_c], [b2])
                    s_, s_b = sbr.next()
                    tt(em, s_[:], p1[:, 0:384], msk[:, mi, :], ALU.add, [b1, b_c], [s_b])
                    sm, smb = smr.next()
                    em.op("dve", lambda e: e.tensor_reduce(out=sm[:, 0:1], in_=s_[:], axis=AX.X, op=ALU.max), reads=[s_b], writes=[smb])
                    em.op("dve", lambda e: e.tensor_reduce(out=sm[:, 1:2], in_=p2[:, 0:TC], axis=AX.X, op=ALU.max), reads=[b2], writes=[smb])
                    stt(em, sm[:, 2:3], sm[:, 0:1], skb[:, h:h + 1], sm[:, 1:2], ALU.max, ALU.max, [smb, b_c], [smb])
                    ts(em, sm[:, 3:4], sm[:, 2:3], -1.0, None, ALU.mult, None, [smb], [smb])
                    p_, p_b = pr.next()
                    act(em, p_[:, 0:384], s_[:], AF.Exp, [s_b, smb], [p_b, smb], bias=sm[:, 3:4], scale=1.0, accum_out=sm[:, 4:5])
                    act(em, p_[:, 384:NK], p2[:, 0:TC], AF.Exp, [b2, smb], [p_b, smb], bias=sm[:, 3:4], scale=1.0, accum_out=sm[:, 5:6])
                    act(em, sm[:, 6:7], sm[:, 3:4], AF.Exp, [smb, b_c], [smb], bias=skb[:, h:h + 1], scale=1.0)
                    tt(em, sm[:, 7:8], sm[:, 4:5], sm[:, 5:6], ALU.add, [smb], [smb])
                    tt(em, sm[:, 7:8], sm[:, 7:8], sm[:, 6:7], ALU.add, [smb], [smb])
                    em.op("dve", lambda e: e.reciprocal(out=sm[:, 7:8], in_=sm[:, 7:8]), reads=[smb], writes=[smb])
                    ts(em, p_[:], p_[:], sm[:, 7:8], None, ALU.mult, None, [p_b, smb], [p_b])
                    pT_ps, pTb = PB.next()
                    nb = 3 + NCB
                    for j in range(nb):
                        em.op("pe", lambda e, j=j: e.transpose(out=pT_ps[:, j * 128:(j + 1) * 128], in_=p_[:, j * 128:(j + 1) * 128],
                                                             identity=idb[:, :]), reads=[p_b, b_c], writes=[pTb])
                    pT, pTsb = ptr_.next()
                    act(em, pT[:], pT_ps[:, 0:NK], AF.Copy, [pTb], [pTsb])
                    po, pob = PS.next()
                    for j in range(nb):
                        vv = vb[:, j, :] if j < 3 else vc[:, j - 3, :]
                        mm(em, po[:, 0:128], vv, pT[:, j * 128:(j + 1) * 128], j == 0, j == nb - 1, [vbb, b_c, pTsb], [pob])
                    tt(em, ot[:, h, :], po[:, 0:128], ga[:, h, :], ALU.mult, [pob, gab], [otb])
                em.dma("pool", oT[:, 0:4, n * 128:(n + 1) * 128], ot[:], reads=[otb], writes=[b_out])
            em.barrier()
        if do_b:
            _l2_rwkv(nc, es, em, PS, PB, TL, TC, zD, lD, gbD, oT, (zDb, lDb, gbDb), b_c, b_out, ones, idf, idb, inp, dbg)
        em.finish([b_out])
    return nc


GN_EPS = 64e-5
DECAY_K = -0.6065306597126334


def _l2_rwkv(nc, es, em, PS, PB, TL, TC, zD, lD, gbD, oT, dbufs, b_c, b_out, ones, idf, idb, inp, dbg=False):
    zDb, lDb, gbDb = dbufs
    TT = TL + TC
    NCH = TT // CH
    cv = inp("cv", [128, 3, 3, 4])
    pv = inp("pv", [128, 9, 4])
    w2c = inp("w2c", [96, 2, 512])
    a2c = inp("a2c", [96, 2, 512])
    gnb = inp("gnb", [128, 2, 512])
    rmask = inp("rmask", [128, 4, 128])
    blk1 = inp("blk1", [128, 128])
    kd = {"kind": "ExternalOutput"} if dbg else {}
    opD = nc.dram_tensor("opD", [2, 4, 4, 128, TT], BF16, **kd).ap()
    vcD = nc.dram_tensor("vcD", [4, 128, TT], BF16, **kd).ap()
    bvD = nc.dram_tensor("bvD", [4, 128, TL], F32, **kd).ap()
    gcD = nc.dram_tensor("gcD", [2, 4, 128, NCH], F32, **kd).ap()
    yD = nc.dram_tensor("yD", [2, TL, 8, 64], F32, **kd).ap()
    opDb, vcDb, bvDb, gcDb = Buf("opD"), Buf("vcD"), Buf("bvD"), Buf("gcD")
    yDb = [Buf("yD0"), Buf("yD1")]
    sb = lambda name, shape, dt: es.enter_context(nc.sbuf_tensor(name, shape, dt))
    cvt = sb("r_cv", [128, 3, 3, 4], F32)
    pvt = sb("r_pv", [128, 9, 4], F32)
    w2t = sb("r_w2", [96, 2, 512], F32)
    a2t = sb("r_a2", [96, 2, 512], F32)
    gnt = sb("r_gn", [128, 2, 512], F32)
    mkt = sb("r_mk", [128, 4, 128], F32)
    b1t = sb("r_b1", [128, 128], F32)
    for dst, src_ in ((cvt, cv), (pvt, pv), (w2t, w2c), (a2t, a2c), (gnt, gnb), (mkt, rmask), (b1t, blk1)):
        em.dma("sp", dst[:], src_, writes=[b_c])
    ts(em, pvt[:, 7, :], pvt[:, 5, :], -1.0, 1.0, ALU.mult, ALU.add, [b_c], [b_c])
    with ExitStack() as ph:
        def R_(name, shape, dt, n):
            return Ring(nc, ph, name, shape, dt, n)
        ztr = R_("b1_z", [128, 514], F32, 3)
        rkv = R_("b1_rkv", [128, 3, 512], F32, 2)
        t32 = R_("b1_t", [128, 512], F32, 12)
        kkr = R_("b1_kk", [128, 512], F32, 2)
        lwr = R_("b1_lw", [96, 2, 512], F32, 2)
        gct = R_("b1_gc", [128, 8], F32, 4)
        o16r = R_("b1_o16", [128, 512], BF16, 6)
        tiles = [(t0, 512, 0, TL) for t0 in range(0, TL, 512)] + [(TL + t0, min(512, TC - t0), TL, TT) for t0 in range(0, TC, 512)]
        for c in range(4):
            for (t0, N, s0, s1) in tiles:
                J = N // CH
                islat = s0 == 0
                x3, x3b = rkv.next()
                for kind in range(3):
                    zt, ztb = ztr.next()
                    j = kind * 4 + c
                    em.dma("sp", zt[:, 1:N + 1], zD[j, :, t0:t0 + N], reads=[zDb], writes=[ztb])
                    if t0 > s0:
                        em.dma("act", zt[:, 0:1], zD[j, :, t0 - 1:t0], reads=[zDb], writes=[ztb], allow_slow_non_contiguous=True)
                    else:
                        em.op("pool", lambda e: e.memset(zt[:, 0:1], 0.0), writes=[ztb])
                    if t0 + N < s1:
                        em.dma("act", zt[:, N + 1:N + 2], zD[j, :, t0 + N:t0 + N + 1], reads=[zDb], writes=[ztb], allow_slow_non_contiguous=True)
                    else:
                        em.op("pool", lambda e: e.memset(zt[:, N + 1:N + 2], 0.0), writes=[ztb])
                    o = x3[:, kind, 0:N]
                    ts(em, o, zt[:, 0:N], cvt[:, kind, 0, c:c + 1], None, ALU.mult, None, [ztb, b_c], [x3b])
                    stt(em, o, zt[:, 1:N + 1], cvt[:, kind, 1, c:c + 1], o, ALU.mult, ALU.add, [ztb, b_c, x3b], [x3b])
                    stt(em, o, zt[:, 2:N + 2], cvt[:, kind, 2, c:c + 1], o, ALU.mult, ALU.add, [ztb, b_c, x3b], [x3b])
                r_t, k_t, v_t = x3[:, 0, 0:N], x3[:, 1, 0:N], x3[:, 2, 0:N]
                v16, v16b = o16r.next()
                cast(em, v16[:, 0:N], v_t, [x3b], [v16b], eng="pool")
                em.dma("pool", vcD[c, :, t0:t0 + N], v16[:, 0:N], reads=[v16b], writes=[vcDb])
                sq, sqb = t32.next()
                act(em, sq[:, 0:N], k_t, AF.Square, [x3b, b_c], [sqb], scale=pvt[:, 4, c:c + 1])
                pt, pb = PS.next()
                mm(em, pt[:, 0:N], b1t[:, :], sq[:, 0:N], True, True, [sqb, b_c], [pb])
                nr, nrb = t32.next()
                act(em, nr[:, 0:N], pt[:, 0:N], AF.Sqrt, [pb], [nrb])
                ts(em, nr[:, 0:N], nr[:, 0:N], 1e-12, None, ALU.max, None, [nrb], [nrb])
                em.op("dve", lambda e: e.reciprocal(out=nr[:, 0:N], in_=nr[:, 0:N]), reads=[nrb], writes=[nrb])
                kk, kkb = kkr.next()
                stt(em, kk[:, 0:N], k_t, pvt[:, 4, c:c + 1], nr[:, 0:N], ALU.mult, ALU.mult, [x3b, b_c, nrb], [kkb])
                if islat:
                    rk, rkb = t32.next()
                    stt(em, rk[:, 0:N], r_t, pvt[:, 6, c:c + 1], k_t, ALU.mult, ALU.mult, [x3b, b_c], [rkb])
                    pt, pb = PS.next()
                    mm(em, pt[:, 0:N], b1t[:, :], rk[:, 0:N], True, True, [rkb, b_c], [pb])
                    bv, bvb = t32.next()
                    tt(em, bv[:, 0:N], pt[:, 0:N], v_t, ALU.mult, [pb, x3b], [bvb])
                    em.dma("pool", bvD[c, :, t0:t0 + N], bv[:, 0:N], reads=[bvb], writes=[bvDb])
                lw, lwb = lwr.next()
                la, lab = lwr.next()
                for d in range(2):
                    em.dma("sp", lw[:, d, 0:N], lD[d, :, t0:t0 + N], reads=[lDb], writes=[lwb])
                    em.dma("act", la[:, d, 0:N], lD[2 + d, :, t0:t0 + N], reads=[lDb], writes=[lab])
                for d in range(2):
                    pt, pb = PS.next()
                    mm(em, pt[:, 0:N], w2t[:, d, c * 128:(c + 1) * 128], lw[:, d, 0:N], True, True, [lwb, b_c], [pb])
                    lg, lgb = t32.next()
                    act(em, lg[:, 0:N], pt[:, 0:N], AF.Sigmoid, [pb, b_c], [lgb], bias=pvt[:, d, c:c + 1], scale=1.0)
                    ts(em, lg[:, 0:N], lg[:, 0:N], DECAY_K, None, ALU.mult, None, [lgb], [lgb], eng="pool")
                    pt, pb = PS.next()
                    mm(em, pt[:, 0:N], a2t[:, d, c * 128:(c + 1) * 128], la[:, d, 0:N], True, True, [lab, b_c], [pb])
                    a_, a_b = t32.next()
                    act(em, a_[:, 0:N], pt[:, 0:N], AF.Sigmoid, [pb, b_c], [a_b], bias=pvt[:, 2 + d, c:c + 1], scale=1.0)
                    P_, P_b = t32.next()
                    for jj in range(J):
                        em.op("dve", lambda e, jj=jj: e.tensor_tensor_scan(
                            out=P_[:, jj * CH:(jj + 1) * CH], data0=ones[:, 0:CH], data1=lg[:, jj * CH:(jj + 1) * CH],
                            initial=0.0, op0=ALU.mult, op1=ALU.add), reads=[lgb, b_c], writes=[P_b])
                    P3 = P_[:, 0:N].rearrange("p (j i) -> p j i", i=CH)
                    tot = P3[:, :, CH - 1:CH]
                    cin, cinb = t32.next()
                    cex, cexb = t32.next()
                    if d == 0:
                        tt(em, cex[:, 0:N], P_[:, 0:N], lg[:, 0:N], ALU.subtract, [P_b, lgb], [cexb], eng="pool")
                        cin_ap, cin_rd = P_[:, 0:N], P_b
                    else:
                        tt(em, cex[:, 0:N].rearrange("p (j i) -> p j i", i=CH), tot.broadcast_to([128, J, CH]), P3, ALU.subtract,
                           [P_b], [cexb])
                        tt(em, cin[:, 0:N], cex[:, 0:N], lg[:, 0:N], ALU.add, [cexb, lgb], [cinb], eng="pool")
                        cin_ap, cin_rd = cin[:, 0:N], cinb
                    g8, g8b = gct.next()
                    act(em, g8[:, 0:J], P3[:, :, CH - 1], AF.Exp, [P_b], [g8b])
                    em.dma("pool", gcD[d, c, :, t0 // CH:t0 // CH + J], g8[:, 0:J], reads=[g8b], writes=[gcDb])
                    ein, einb = t32.next()
                    eni, enib = t32.next()
                    act(em, ein[:, 0:N], cin_ap, AF.Exp, [cin_rd], [einb])
                    act(em, eni[:, 0:N], cin_ap, AF.Exp, [cin_rd], [enib], scale=-1.0)
                    act(em, cex[:, 0:N], cex[:, 0:N], AF.Exp, [cexb], [cexb])
                    o1, o1b = o16r.next()
                    tt(em, o1[:, 0:N], kk[:, 0:N], cex[:, 0:N], ALU.mult, [kkb, cexb], [o1b], eng="pool")
                    em.dma("sp", opD[d, 0, c, :, t0:t0 + N], o1[:, 0:N], reads=[o1b], writes=[opDb])
                    o3, o3b = o16r.next()
                    tt(em, o3[:, 0:N], r_t, ein[:, 0:N], ALU.mult, [x3b, einb], [o3b])
                    em.dma("act", opD[d, 1, c, :, t0:t0 + N], o3[:, 0:N], reads=[o3b], writes=[opDb])
                    o2, o2b = t32.next()
                    tt(em, o2[:, 0:N], kk[:, 0:N], a_[:, 0:N], ALU.mult, [kkb, a_b], [o2b], eng="pool")
                    o4, o4b = o16r.next()
                    stt(em, o4[:, 0:N], o2[:, 0:N], -1.0, eni[:, 0:N], ALU.mult, ALU.mult, [o2b, enib], [o4b])
                    em.dma("sp", opD[d, 2, c, :, t0:t0 + N], o4[:, 0:N], reads=[o4b], writes=[opDb])
                    ts(em, a_[:, 0:N], a_[:, 0:N], pvt[:, 5, c:c + 1], pvt[:, 7, c:c + 1], ALU.mult, ALU.add, [a_b, b_c], [a_b])
                    tt(em, a_[:, 0:N], a_[:, 0:N], k_t, ALU.mult, [a_b, x3b], [a_b], eng="pool")
                    o5, o5b = o16r.next()
                    tt(em, o5[:, 0:N], a_[:, 0:N], eni[:, 0:N], ALU.mult, [a_b, enib], [o5b])
                    em.dma("act", opD[d, 3, c, :, t0:t0 + N], o5[:, 0:N], reads=[o5b], writes=[opDb])
        em.barrier()
    with ExitStack() as ph:
        def R_(name, n, shape=(128, 128), dt=BF16):
            return Ring(nc, ph, name, list(shape), dt, n)
        gall = ph.enter_context(nc.sbuf_tensor("s_gc", [128, 2, 4, NCH], F32))
        em.dma("sp", gall[:], gcD.rearrange("d c p n -> p d c n"), reads=[gcDb], writes=[b_c])
        PS_full, PB_full = PS, PB

        class BS:
            pass
        sets = {}
        for d in range(2):
            for par in range(2):
                s_ = BS()
                nm = "_%d_%d" % (d, par)
                s_.ops = [R_("s_op%d%s" % (k, nm), 1, (128, 4, 2, 64)) for k in range(5)]
                for r_ in s_.ops:
                    em.op("pool", lambda e, t_=r_.t[0]: e.memset(t_[:], 0.0), writes=[r_.b[0]])
                s_.trs = [R_("s_tr%d%s" % (k, nm), 1, (128, 4, 128)) for k in range(3)]
                s_.mks = [R_("s_mk%d%s" % (k, nm), 1, (128, 4, 128)) for k in range(5)]
                s_.pw = R_("s_pw" + nm, 4, (128, 4, 128))
                s_.wt = R_("s_wt" + nm, 2, (128, 4, 128))
                s_.x1 = R_("s_x1" + nm, 1, (128, 4, 128))
                s_.u = R_("s_u" + nm, 1, (128, 4, 128))
                s_.y = R_("s_y" + nm, 1, (128, 4, 128), dt=F32)
                s_.hn = R_("s_hn" + nm, 1, (128, 4, 128), dt=F32)
                sets[(d, par)] = s_
        H4 = [ph.enter_context(nc.sbuf_tensor("s_H4_%d" % d, [128, 4, 128], F32)) for d in range(2)]
        Hh4 = [ph.enter_context(nc.sbuf_tensor("s_Hh4_%d" % d, [128, 4, 128], BF16)) for d in range(2)]
        H4b = [Buf("H4_0"), Buf("H4_1")]
        for d in range(2):
            em.op("pool", lambda e, d=d: e.memset(H4[d][:], 0.0), writes=[H4b[d]])
            em.op("pool", lambda e, d=d: e.memset(Hh4[d][:], 0.0), writes=[H4b[d]])
        seqs = []
        for d in range(2):
            cc_ = [(TL + j * CH, False) for j in range(TC // CH)]
            ll_ = [(j * CH, True) for j in range(TL // CH)]
            seqs.append(cc_ + ll_ if d == 0 else cc_[::-1] + ll_[::-1])

        def f2(t4, c):
            return t4[:, c].rearrange("p a i -> p (a i)")

        def fl(t3):
            return t3[:].rearrange("p c i -> p (c i)")

        def v3(ap2):
            return ap2.rearrange("p (c i) -> p c i", c=4)

        def bc(ap2):
            return ap2[:, None, :].broadcast_to([128, 4, 128])

        def indep(u):
            d, o, islat, S = u
            st = {}
            ops = []
            for k in range(5):
                t4, tb = S.ops[k].next()
                for hh in range(2):
                    if k == 4:
                        src_ = vcD[:, hh * 64:(hh + 1) * 64, o:o + CH]
                    else:
                        src_ = opD[d, k, :, hh * 64:(hh + 1) * 64, o:o + CH]
                    em.dma("sp", t4[hh * 64:(hh + 1) * 64, :, hh, :], src_.rearrange("c p t -> p c t"),
                           reads=[vcDb if k == 4 else opDb], writes=[tb])
                ops.append((t4, tb))
            st["ops"] = ops
            (At, Atb), (Rt, Rtb), (Bt, Btb), (Kt, Ktb), (Vt, Vtb) = ops
            yield
            trs = []
            for k, (src_, sbuf_) in enumerate(((Bt, Btb), (Kt, Ktb), (Vt, Vtb))):
                pt, pb = PB_full.next()
                for c in range(4):
                    em.op("pe", lambda e, pt=pt, src_=src_, c=c: e.transpose(out=pt[:, c * 128:(c + 1) * 128], in_=f2(src_, c),
                                                                            identity=idb[:, :]), reads=[sbuf_, b_c], writes=[pb])
                t_, tb_ = S.trs[k].next()
                em.op("dve", lambda e, t_=t_, pt=pt: e.tensor_copy(out=fl(t_), in_=pt[:, 0:512]), reads=[pb], writes=[tb_])
                trs.append((t_, tb_))
                yield
            st["trs"] = trs
            ms, mi_, mL = (0, 1, 2) if d == 0 else (2, 3, 0)
            specs = ((Kt, Ktb, At, Atb, ms), (Bt, Btb, At, Atb, ms), (At, Atb, Bt, Btb, mL), (Bt, Btb, Rt, Rtb, mi_), (Kt, Ktb, Rt, Rtb, mi_))
            mks = []
            for k, (l_, lb_, r_, rb_, mi) in enumerate(specs):
                pt, pb = PS_full.next()
                for c in range(4):
                    mm(em, pt[:, c * 128:(c + 1) * 128], f2(l_, c), f2(r_, c), True, True, [lb_, rb_], [pb])
                t_, tb_ = S.mks[k].next()
                tt(em, t_[:], v3(pt[:, :]), bc(mkt[:, mi, :]), ALU.mult, [pb, b_c], [tb_])
                mks.append((t_, tb_))
                yield
            st["mks"] = mks
            (Np, Npb), (Lp, Lpb) = mks[1], mks[2]
            WT, WTb = S.wt.next()
            tt(em, WT[:], Np[:], bc(idf[:, :]), ALU.add, [Npb, b_c], [WTb], eng="pool")
            for lvl in range(5):
                pt, pb = PS_full.next()
                for c in range(4):
                    mm(em, pt[:, c * 128:(c + 1) * 128], Np[:, c, :], Lp[:, c, :], True, True, [Npb, Lpb], [pb])
                L2, L2b = S.pw.next()
                act(em, fl(L2), pt[:, :], AF.Copy, [pb], [L2b])
                if lvl < 4:
                    pt, pb = PS_full.next()
                    for c in range(4):
                        mm(em, pt[:, c * 128:(c + 1) * 128], Lp[:, c, :], Np[:, c, :], True, True, [Npb, Lpb], [pb])
                    N2, N2b = S.pw.next()
                    em.op("dve", lambda e, N2=N2, pt=pt: e.tensor_copy(out=fl(N2), in_=pt[:, :]), reads=[pb], writes=[N2b])
                yield
                pt, pb = PS_full.next()
                for c in range(4):
                    mm(em, pt[:, c * 128:(c + 1) * 128], L2[:, c, :], WT[:, c, :], True, True, [L2b, WTb], [pb])
                WTn, WTnb = S.wt.next()
                tt(em, fl(WTn), pt[:, :], fl(WT), ALU.add, [pb, WTb], [WTnb])
                WT, WTb = WTn, WTnb
                if lvl < 4:
                    Np, Npb, Lp, Lpb = N2, N2b, L2, L2b
                yield
            st["WT"] = (WT, WTb)
            u_state[u[:3]] = st

        def dep(u):
            d, o, islat, S = u
            st = u_state.pop(u[:3])
            (At, Atb), (Rt, Rtb), (Bt, Btb), (Kt, Ktb), (Vt, Vtb) = st["ops"]
            (Bm, Bmb), (Km, Kmb), (Vm, Vmb) = st["trs"]
            (Lak, Lakb), _n, _l, (Trb, Trbb), (Trk, Trkb) = st["mks"]
            WT, WTb = st["WT"]
            H, Hh, Hbuf = H4[d], Hh4[d], H4b[d]
            pt, pb = PS_full.next()
            for c in range(4):
                mm(em, pt[:, c * 128:(c + 1) * 128], f2(At, c), Hh[:, c, :], True, False, [Atb, Hbuf], [pb])
                mm(em, pt[:, c * 128:(c + 1) * 128], Lak[:, c, :], Vm[:, c, :], False, True, [Lakb, Vmb], [pb])
            X1, X1b = S.x1.next()
            act(em, fl(X1), pt[:, :], AF.Copy, [pb], [X1b])
            yield
            pt, pb = PS_full.next()
            for c in range(4):
                mm(em, pt[:, c * 128:(c + 1) * 128], WT[:, c, :], X1[:, c, :], True, True, [WTb, X1b], [pb])
            U, Ub = S.u.next()
            em.op("dve", lambda e: e.tensor_copy(out=fl(U), in_=pt[:, :]), reads=[pb], writes=[Ub])
            yield
            if islat:
                pt, pb = PS_full.next()
                for c in range(4):
                    mm(em, pt[:, c * 128:(c + 1) * 128], f2(Rt, c), Hh[:, c, :], True, False, [Rtb, Hbuf], [pb])
                    mm(em, pt[:, c * 128:(c + 1) * 128], Trb[:, c, :], U[:, c, :], False, False, [Trbb, Ub], [pb])
                    mm(em, pt[:, c * 128:(c + 1) * 128], Trk[:, c, :], Vm[:, c, :], False, True, [Trkb, Vmb], [pb])
                Y, Yb = S.y.next()
                act(em, fl(Y), pt[:, :], AF.Copy, [pb], [Yb])
                for hh in range(2):
                    em.dma("sp", yD[d, o:o + CH, hh:8:2, :], Y[hh * 64:(hh + 1) * 64, :, hh * 64:(hh + 1) * 64],
                           reads=[Yb], writes=[yDb[d]])
            pt, pb = PS_full.next()
            for c in range(4):
                mm(em, pt[:, c * 128:(c + 1) * 128], Bm[:, c, :], U[:, c, :], True, False, [Bmb, Ub], [pb])
                mm(em, pt[:, c * 128:(c + 1) * 128], Km[:, c, :], Vm[:, c, :], False, True, [Kmb, Vmb], [pb])
            Hn, Hnb = S.hn.next()
            tt(em, fl(Hn), pt[:, :], fl(H), ALU.add, [pb, Hbuf], [Hnb])
            ci = o // CH
            gam = gall[:, d, :, ci:ci + 1].broadcast_to([128, 4, 128])
            tt(em, H[:], Hn[:], gam, ALU.mult, [Hnb, b_c], [Hbuf])
            tt(em, Hh[:], Hn[:], gam, ALU.mult, [Hnb, b_c], [Hbuf], eng="pool")
            yield

        def rr(gens):
            gens = list(gens)
            while gens:
                nxt = []
                for g_ in gens:
                    if next(g_, "done") != "done":
                        nxt.append(g_)
                gens = nxt

        steps = [[(d,) + seqs[d][i] + (sets[(d, i & 1)],) for d in range(2)] for i in range(NCH)]
        u_state = {}
        rr([indep(u) for u in steps[0]])
        for si in range(NCH):
            gens = [dep(u) for u in steps[si]]
            if si + 1 < NCH:
                gens += [indep(u) for u in steps[si + 1]]
            rr(gens)
        em.barrier()
    with ExitStack() as ph:
        def R_(name, shape, dt, n):
            return Ring(nc, ph, name, shape, dt, n)
        yfr = R_("o_yf", [128, 8, 64], F32, 2)
        yrr = R_("o_yr", [128, 8, 64], F32, 2)
        ycr = R_("o_yc", [128, 8, 64], F32, 2)
        sqr2 = R_("o_sq", [128, 8, 64], F32, 2)
        str_ = R_("o_st", [128, 2, 8], F32, 2)
        bvr = R_("o_bv", [128, 4, 128], F32, 2)
        gbr = R_("o_gb", [128, 4, 128], BF16, 2)
        obr = R_("o_ob", [128, 4, 128], BF16, 2)
        for n in range(TL // 128):
            t0 = n * 128
            yf, yfb = yfr.next()
            yr2, yrb = yrr.next()
            em.dma("sp", yf[:], yD[0, t0:t0 + 128, :, :], reads=[yDb[0]], writes=[yfb])
            em.dma("act", yr2[:], yD[1, t0:t0 + 128, :, :], reads=[yDb[1]], writes=[yrb])
            tt(em, yf[:], yf[:], yr2[:], ALU.add, [yfb, yrb], [yfb], eng="pool")
            st_, stb = str_.next()
            em.op("dve", lambda e: e.tensor_reduce(out=st_[:, 0, :], in_=yf[:], axis=AX.X, op=ALU.add), reads=[yfb], writes=[stb])
            ts(em, st_[:, 0, :], st_[:, 0, :], 1.0 / 64, None, ALU.mult, None, [stb], [stb])
            yc, ycb = ycr.next()
            tt(em, yc[:], yf[:], st_[:, 0, :][:, :, None].broadcast_to([128, 8, 64]), ALU.subtract, [yfb, stb], [ycb])
            sq, sqb = sqr2.next()
            act(em, sq[:], yc[:], AF.Square, [ycb], [sqb])
            em.op("dve", lambda e: e.tensor_reduce(out=st_[:, 1, :], in_=sq[:], axis=AX.X, op=ALU.add), reads=[sqb], writes=[stb])
            act(em, st_[:, 1, :], st_[:, 1, :], AF.Sqrt, [stb], [stb], scale=1.0 / 64, bias=GN_EPS)
            em.op("dve", lambda e: e.reciprocal(out=st_[:, 1, :], in_=st_[:, 1, :]), reads=[stb], writes=[stb])
            tt(em, yc[:], yc[:], st_[:, 1, :][:, :, None].broadcast_to([128, 8, 64]), ALU.mult, [ycb, stb], [ycb])
            yc2 = yc[:].rearrange("p h v -> p (h v)")
            tt(em, yc2, yc2, gnt[:, 0, :], ALU.mult, [ycb, b_c], [ycb], eng="pool")
            tt(em, yc2, yc2, gnt[:, 1, :], ALU.add, [ycb, b_c], [ycb], eng="pool")
            pt, pb = PS_full.next()
            for c in range(4):
                em.op("pe", lambda e, c=c: e.transpose(out=pt[:, c * 128:(c + 1) * 128], in_=yc2[:, c * 128:(c + 1) * 128],
                                                       identity=idf[:, :]), reads=[ycb, b_c], writes=[pb])
            bv, bvb = bvr.next()
            gb_, gbb = gbr.next()
            em.dma("sp", bv[:], bvD[:, :, t0:t0 + 128].rearrange("c p t -> p c t"), reads=[bvDb], writes=[bvb])
            em.dma("act", gb_[:], gbD[:, :, t0:t0 + 128].rearrange("c p t -> p c t"), reads=[gbDb], writes=[gbb])
            tt(em, bv[:].rearrange("p c t -> p (c t)"), pt[:, :], bv[:].rearrange("p c t -> p (c t)"), ALU.add, [pb, bvb], [bvb])
            ob_, obb = obr.next()
            tt(em, ob_[:], bv[:], gb_[:], ALU.mult, [bvb, gbb], [obb], eng="pool")
            em.dma("pool", oT[:, 4:8, t0:t0 + 128], ob_[:], reads=[obb], writes=[b_out])
        em.barrier()


A_Q, A_KV, B_W = 2048, 512, 2048
OFF = {"q": 0, "k": 2048, "v": 2560, "ga": 3072, "r": 5120, "kb": 7168, "vb": 9216, "gb": 11264, "lw": 13312, "la": 13504}
SWAP = np.concatenate([np.arange(32, 64), np.arange(0, 32), np.arange(96, 128), np.arange(64, 96)])


def l2_wcols(w_in, g):
    cols = []
    for h in range(4):
        cols.append(np.arange(128) + OFF["q"] + (4 * g + h) * 128)
    for h in range(4):
        cols.append(SWAP + OFF["q"] + (4 * g + h) * 128)
    cols.append(np.arange(128) + OFF["k"] + g * 128)
    cols.append(SWAP + OFF["k"] + g * 128)
    cols.append(np.arange(128) + OFF["v"] + g * 128)
    for h in range(4):
        cols.append(np.arange(128) + OFF["ga"] + (4 * g + h) * 128)
    for nm in ("r", "kb", "vb", "gb"):
        for c in range(4):
            cols.append(np.arange(128) + OFF[nm] + g * 512 + c * 128)
    for nm in ("lw", "la"):
        for d in range(2):
            cols.append(np.arange(96) + OFF[nm] + d * 96)
    return np.ascontiguousarray(w_in[:, np.concatenate(cols)])


def rope_tables(TL):
    t = np.arange(TL)
    row = (t // 64).astype(np.float32)
    col = (t % 64).astype(np.float32)
    inv = (np.float32(10000.0) ** (-np.arange(32, dtype=np.float32) / np.float32(32))).astype(np.float32)
    cosT = np.zeros((128, TL), np.float32)
    sinT = np.zeros((128, TL), np.float32)
    for p in range(128):
        idx = p % 64
        ang = (row if p < 64 else col) * inv[idx % 32]
        cosT[p] = np.cos(ang)
        sinT[p] = np.sin(ang) * (-1.0 if idx < 32 else 1.0)
    return cosT, sinT


def attn_masks():
    qi = np.arange(128)[:, None]
    j = np.arange(128)[None, :]
    lo = np.where(j >= qi, 0.0, NEG).astype(np.float32)
    hi = np.where(j <= qi, 0.0, NEG).astype(np.float32)
    z = np.zeros((128, 128), np.float32)
    ng = np.full((128, 128), NEG, np.float32)
    m = np.stack([np.concatenate([ng, z, hi], 1), np.concatenate([lo, z, hi], 1), np.concatenate([lo, z, ng], 1)], axis=1)
    return np.ascontiguousarray(m)


def rwkv_masks():
    i = np.arange(64)[:, None]
    j = np.arange(64)[None, :]
    pats = [(i < j), (i <= j), (i > j), (i >= j)]
    out = np.zeros((128, 4, 128), np.float32)
    for k, p in enumerate(pats):
        out[:, k, :] = np.tile(p.astype(np.float32), (2, 2))
    return out


def l2_rwkv_inputs(g, conv, w0, w2, a0, a2, k_k, k_a, r_k, gn_w, gn_b):
    sl = slice(g * 512, (g + 1) * 512)
    pc = lambda v: np.ascontiguousarray(np.asarray(v[sl], np.float32).reshape(4, 128).T)
    cv = np.zeros((128, 3, 3, 4), np.float32)
    for kind in range(3):
        for tap in range(3):
            cv[:, kind, tap, :] = pc(conv[tap, kind * 2048:(kind + 1) * 2048])
    pv = np.zeros((128, 9, 4), np.float32)
    for i, v in enumerate((w0[0], w0[1], a0[0], a0[1], k_k, k_a, r_k)):
        pv[:, i, :] = pc(v)
    blk1 = np.zeros((128, 128), np.float32)
    blk1[:64, :64] = 1
    blk1[64:, 64:] = 1
    return {
        "cv": cv, "pv": pv,
        "w2c": np.ascontiguousarray(np.transpose(w2[:, :, sl], (1, 0, 2))),
        "a2c": np.ascontiguousarray(np.transpose(a2[:, :, sl], (1, 0, 2))),
        "gnb": np.ascontiguousarray(np.broadcast_to(np.stack([gn_w[sl], gn_b[sl]])[None], (128, 2, 512))).astype(np.float32),
        "rmask": rwkv_masks(), "blk1": blk1,
    }


BATCH, SEQ, CTX = 2, 8192, 256
_BF16_NP = mybir.dt.np(BF16)


def kernel(x, c, ctx, c_ctx, mod_w, mod_b, norm_g, ab_w_in, ab_w_out, attn_sink, rwkv_conv, rwkv_w0, rwkv_w2,
           rwkv_a0, rwkv_a2, rwkv_k_k, rwkv_k_a, rwkv_r_k, rwkv_gn_w, rwkv_gn_b, gm_w_in, gm_ln_g, gm_ln_b,
           gm_w_s, gm_b_s, gm_w_out, final_g):
    f32 = lambda a: np.asarray(a, dtype=np.float32)
    x, c, ctx, c_ctx = f32(x), f32(c), f32(ctx), f32(c_ctx)
    mod = run_mod(c, c_ctx, f32(mod_w), f32(mod_b))
    shift = mod[:, :, 0:D]
    scale = mod[:, :, D:2 * D]
    gate = mod[:, :, 2 * D:3 * D]
    w_in = f32(ab_w_in[0])
    cosT, sinT = rope_tables(SEQ)
    ident = np.eye(128, dtype=np.float32)
    am = attn_masks()
    xTb = [fmT(x[b]) for b in range(BATCH)]
    cTb = [fmT(ctx[b]) for b in range(BATCH)]
    wcs = [l2_wcols(w_in, g) for g in range(4)]
    rw = [l2_rwkv_inputs(g, f32(rwkv_conv[0]), f32(rwkv_w0[0]), f32(rwkv_w2[0]), f32(rwkv_a0[0]), f32(rwkv_a2[0]),
                         f32(rwkv_k_k[0]), f32(rwkv_k_a[0]), f32(rwkv_r_k[0]), f32(rwkv_gn_w[0]), f32(rwkv_gn_b[0]))
          for g in range(4)]
    in_maps = []
    for i in range(NCORES):
        b, g = i // 4, i % 4
        vec = np.stack([fm(v, KC) for v in (norm_g[0], scale[0, b], shift[0, b], scale[0, 2], shift[0, 2], shift[0, 2])], axis=1)
        m = {"xT": xTb[b], "cT": cTb[b], "wcols": wcs[g], "vec": np.ascontiguousarray(vec),
             "sinkb": np.ascontiguousarray(np.broadcast_to(f32(attn_sink[0])[None, 4 * g:4 * g + 4], (128, 4))),
             "cosT": cosT, "sinT": sinT, "ident": ident, "amask": am}
        m.update(rw[g])
        in_maps.append(m)
    nc2 = build_l2(SEQ, CTX, do_b=True)
    res2 = run_bass_kernel_spmd(nc2, in_maps, core_ids=list(range(NCORES)))
    del in_maps, xTb, cTb, wcs
    oTf = np.zeros((BATCH, 128, KC, SEQ), dtype=_BF16_NP)
    for i in range(NCORES):
        b, g = i // 4, i % 4
        o = res2.results[i]["oT"]
        oTf[b, :, g * 4:(g + 1) * 4, :] = o[:, 0:4, :]
        oTf[b, :, 16 + g * 4:16 + (g + 1) * 4, :] = o[:, 4:8, :]
    del res2
    TCORE = BATCH * SEQ // NCORES
    w_out, gw_in, gw_out = f32(ab_w_out[0]), f32(gm_w_in[0]), f32(gm_w_out[0])
    in_maps = []
    for i in range(NCORES):
        b, s = i // 4, (i % 4) * TCORE
        in_maps.append(l3_core_inputs(x[b, s:s + TCORE], np.ascontiguousarray(oTf[b, :, :, s:s + TCORE]), w_out, gw_in, gw_out,
                                      gate[0, b], norm_g[1], scale[1, b], shift[1, b], gate[1, b], final_g,
                                      f32(gm_ln_g[0]), f32(gm_ln_b[0]), f32(gm_w_s[0]), f32(gm_b_s[0])))
    nc3 = build_l3(TCORE, 16, 1024)
    res3 = run_bass_kernel_spmd(nc3, in_maps, core_ids=list(range(NCORES)))
    del in_maps
    out = np.empty((BATCH, SEQ, D), np.float32)
    for i in range(NCORES):
        b, s = i // 4, (i % 4) * TCORE
        out[b, s:s + TCORE] = res3.results[i]["outT"].transpose(2, 1, 0).reshape(TCORE, D)
    return out
```

```python
import numpy as np
from contextlib import ExitStack
import concourse.bass as bass
import concourse.mybir as mybir
from concourse.bass_utils import run_bass_kernel_spmd

F32 = mybir.dt.float32
BF16 = mybir.dt.bfloat16
AF = mybir.ActivationFunctionType
ALU = mybir.AluOpType
AX = mybir.AxisListType

D = 4096
KC = D // 128
NCORES = 8


class Buf:
    __slots__ = ("w", "r", "name")

    def __init__(self, name=""):
        self.w = {}
        self.r = {}
        self.name = name


class Emit:
    NSLOT = 12

    def __init__(self, nc, es):
        self.nc = nc
        self.engs = {"pe": nc.tensor, "act": nc.scalar, "dve": nc.vector, "pool": nc.gpsimd, "sp": nc.sync}
        self.sem = {}
        self.cnt = {}
        for e in ("pe", "act", "dve", "pool"):
            self.sem[e] = es.enter_context(nc.semaphore("s_" + e))
            self.cnt[e] = 0
        self.qn = {}
        for q in ("sp", "act", "pool"):
            for i in range(self.NSLOT):
                k = "d_%s_%d" % (q, i)
                self.sem[k] = es.enter_context(nc.semaphore(k))
                self.cnt[k] = 0
            self.qn[q] = 0
        self.waited = {e: {} for e in self.engs}

    def _wait(self, eng, deps):
        for k, v in deps.items():
            if v <= 0:
                continue
            if k == "pe" and eng == "pe":
                continue
            if self.waited[eng].get(k, 0) >= v:
                continue
            self.engs[eng].wait_ge(self.sem[k], v)
            self.waited[eng][k] = v

    @staticmethod
    def _merge(dst, src):
        for k, v in src.items():
            if dst.get(k, 0) < v:
                dst[k] = v

    def _deps(self, reads, writes):
        deps = {}
        for b in reads:
            self._merge(deps, b.w)
        for b in writes:
            self._merge(deps, b.w)
            self._merge(deps, b.r)
        return deps

    def _commit(self, key, val, reads, writes):
        for b in reads:
            if b.r.get(key, 0) < val:
                b.r[key] = val
        for b in writes:
            if b.r:
                b.w = {key: val}
                b.r = {}
            else:
                b.w[key] = val

    def op(self, eng, fn, reads=(), writes=()):
        self._wait(eng, self._deps(reads, writes))
        ins = fn(self.engs[eng])
        self.cnt[eng] += 1
        ins.then_inc(self.sem[eng], 1)
        self._commit(eng, self.cnt[eng], reads, writes)

    def dma(self, q, out, in_, reads=(), writes=(), **kw):
        slot = self.qn[q] % self.NSLOT
        self.qn[q] += 1
        key = "d_%s_%d" % (q, slot)
        deps = self._deps(reads, writes)
        if self.cnt[key] > 0:
            deps[key] = max(deps.get(key, 0), self.cnt[key])
        self._wait(q, deps)
        ins = self.engs[q].dma_start(out=out, in_=in_, **kw)
        self.cnt[key] += 16
        ins.then_inc(self.sem[key], 16)
        self._commit(key, self.cnt[key], reads, writes)

    def barrier(self):
        allc = {k: v for k, v in self.cnt.items() if v > 0}
        for e in self.engs:
            self._wait(e, dict(allc))

    def finish(self, bufs):
        deps = {}
        for b in bufs:
            self._merge(deps, b.w)
        self._wait("sp", deps)


def _new_nc():
    return bass.Bass("TRN2", target_bir_lowering=False)


MODC = 3 * D // NCORES


def build_mod():
    nc = _new_nc()
    condT = nc.dram_tensor("condT", [128, KC, 3], F32, kind="ExternalInput").ap()
    mw = nc.dram_tensor("mw", [2, D, MODC], F32, kind="ExternalInput").ap()
    mb = nc.dram_tensor("mb", [2, 3, MODC], F32, kind="ExternalInput").ap()
    out = nc.dram_tensor("out", [2, 3, MODC], F32, kind="ExternalOutput").ap()
    with ExitStack() as es:
        em = Emit(nc, es)
        sc = es.enter_context(nc.sbuf_tensor("sc", [128, KC, 3], F32))
        wt = [es.enter_context(nc.sbuf_tensor("wt%d" % i, [128, KC, 512], F32)) for i in range(2)]
        bt = es.enter_context(nc.sbuf_tensor("bt", [3, 2, MODC], F32))
        ot = es.enter_context(nc.sbuf_tensor("ot", [3, 2, MODC], F32))
        ps = [es.enter_context(nc.psum_tensor("ps%d" % i, [128, 512], F32)) for i in range(2)]
        b_sc, b_bt, b_ot = Buf("sc"), Buf("bt"), Buf("ot")
        b_wt = [Buf("wt0"), Buf("wt1")]
        b_ps = [Buf("ps0"), Buf("ps1")]
        em.dma("sp", sc[:], condT[:, :, :], writes=[b_sc])
        em.dma("sp", bt[:], mb.rearrange("l r n -> r l n"), writes=[b_bt])
        em.op("act", lambda e: e.activation(out=sc[:], in_=sc[:], func=AF.Silu), reads=[b_sc], writes=[b_sc])
        it = 0
        for l in range(2):
            for nt in range(MODC // 512):
                j = it % 2
                it += 1
                em.dma("sp" if j == 0 else "pool", wt[j][:],
                       mw[l, :, nt * 512:(nt + 1) * 512].rearrange("(k p) n -> p k n", p=128), writes=[b_wt[j]])
                for k in range(KC):
                    em.op("pe", lambda e, k=k, j=j: e.matmul(ps[j][0:3, :], lhsT=sc[:, k, :], rhs=wt[j][:, k, :],
                                                              start=(k == 0), stop=(k == KC - 1)),
                          reads=[b_sc, b_wt[j]], writes=[b_ps[j]])
                em.op("dve", lambda e, j=j, l=l, nt=nt: e.tensor_tensor(
                    out=ot[:, l, nt * 512:(nt + 1) * 512], in0=ps[j][0:3, :], in1=bt[:, l, nt * 512:(nt + 1) * 512],
                    op=ALU.add), reads=[b_ps[j], b_bt], writes=[b_ot])
        em.dma("sp", out.rearrange("l r n -> r l n"), ot[:], reads=[b_ot], writes=[b_ot])
        em.finish([b_ot])
    return nc


def run_mod(c, c_ctx, mod_w, mod_b, ncores=NCORES):
    cond = np.concatenate([c, c_ctx[None, :]], axis=0).astype(np.float32)
    condT = np.ascontiguousarray(cond.reshape(3, KC, 128).transpose(2, 1, 0))
    in_maps = []
    for i in range(ncores):
        sl = slice(i * MODC, (i + 1) * MODC)
        in_maps.append({
            "condT": condT,
            "mw": np.ascontiguousarray(mod_w[:, :, sl]),
            "mb": np.ascontiguousarray(np.broadcast_to(mod_b[:, None, sl], (2, 3, MODC))),
        })
    nc = build_mod()
    res = run_bass_kernel_spmd(nc, in_maps, core_ids=list(range(ncores)))
    return np.concatenate([r["out"] for r in res.results], axis=2)


_UID = [0]


def _u(name):
    _UID[0] += 1
    return "%s_u%d" % (name, _UID[0])


class Ring:
    def __init__(self, nc, es, name, shape, dtype, n, psum=False):
        alloc = nc.psum_tensor if psum else nc.sbuf_tensor
        name = _u(name)
        self.t = [es.enter_context(alloc("%s_%d" % (name, i), shape, dtype)) for i in range(n)]
        self.b = [Buf("%s%d" % (name, i)) for i in range(n)]
        self.i = 0

    def next(self):
        j = self.i % len(self.t)
        self.i += 1
        return self.t[j], self.b[j]


def mm(em, out, lhsT, rhs, start, stop, reads, writes):
    em.op("pe", lambda e: e.matmul(out, lhsT=lhsT, rhs=rhs, start=start, stop=stop), reads=reads, writes=writes)


def act(em, out, in_, func, reads, writes, scale=None, bias=None, accum_out=None):
    kw = {}
    if scale is not None:
        kw["scale"] = scale
    if bias is not None:
        kw["bias"] = bias
    if accum_out is not None:
        kw["accum_out"] = accum_out
    em.op("act", lambda e: e.activation(out=out, in_=in_, func=func, **kw), reads=reads, writes=writes)


def tt(em, out, in0, in1, op, reads, writes, eng="dve"):
    em.op(eng, lambda e: e.tensor_tensor(out=out, in0=in0, in1=in1, op=op), reads=reads, writes=writes)


def stt(em, out, in0, scalar, in1, op0, op1, reads, writes):
    em.op("dve", lambda e: e.scalar_tensor_tensor(out=out, in0=in0, scalar=scalar, in1=in1, op0=op0, op1=op1),
          reads=reads, writes=writes)


def ts(em, out, in0, s1, s2, op0, op1, reads, writes, eng="dve"):
    if s2 is None:
        em.op(eng, lambda e: e.tensor_scalar(out=out, in0=in0, scalar1=s1, scalar2=None, op0=op0),
              reads=reads, writes=writes)
    else:
        em.op(eng, lambda e: e.tensor_scalar(out=out, in0=in0, scalar1=s1, scalar2=s2, op0=op0, op1=op1),
              reads=reads, writes=writes)


def cast(em, out, in_, reads, writes, eng="pool"):
    em.op(eng, lambda e: e.tensor_copy(out=out, in_=in_), reads=reads, writes=writes)


GRP_W = 768
NORM_EPS = 1e-6
LN_EPS = 1e-5


def build_l3(T, NG, TP):
    CW = NG * GRP_W
    NCC = CW // 128
    NCT = CW // 512
    NTB = TP // 128
    NTS = TP // 512
    NPASS = T // TP
    TS = 256
    NS = TP // TS
    nc = _new_nc()
    xT = nc.dram_tensor("xT", [128, KC, T], F32, kind="ExternalInput").ap()
    oT = nc.dram_tensor("oT", [128, KC, T], BF16, kind="ExternalInput").ap()
    w_out = nc.dram_tensor("w_out", [D, D], F32, kind="ExternalInput").ap()
    gw_in = nc.dram_tensor("gw_in", [D, 3 * CW], F32, kind="ExternalInput").ap()
    gw_out = nc.dram_tensor("gw_out", [CW, D], F32, kind="ExternalInput").ap()
    vec32 = nc.dram_tensor("vec32", [128, 6, KC], F32, kind="ExternalInput").ap()
    lnv = nc.dram_tensor("lnv", [128, 2, NCC], F32, kind="ExternalInput").ap()
    wsT = nc.dram_tensor("wsT", [128, NG, 128], F32, kind="ExternalInput").ap()
    bsb = nc.dram_tensor("bsb", [128, NG, 128], F32, kind="ExternalInput").ap()
    outT = nc.dram_tensor("outT", [128, KC, T], F32, kind="ExternalOutput").ap()
    xnewD = nc.dram_tensor("xnewD", [128, KC, T], F32).ap()
    xfinD = nc.dram_tensor("xfinD", [128, KC, TP], F32).ap()
    gvD = nc.dram_tensor("gvD", [TP, CW], BF16).ap()
    yD = nc.dram_tensor("yD", [NCC, 128, TP], BF16).ap()
    with ExitStack() as es:
        em = Emit(nc, es)
        sb = lambda name, shape, dt: es.enter_context(nc.sbuf_tensor(name, shape, dt))
        PS = Ring(nc, es, "ps", [128, 512], F32, 8, psum=True)
        v32 = sb("v32", [128, 6, KC], F32)
        lnt = sb("lnt", [128, 2, NCC], F32)
        gs1 = sb("gs1", [128, KC], F32)
        ones = sb("ones", [128, 128], F32)
        wsf = sb("wsf", [128, NG, 128], F32)
        wsb = sb("wsb", [128, NG, 128], BF16)
        rsb = sb("rsb", [128, NG, 128], F32)
        bsbt = sb("bsbt", [128, NG, 128], F32)
        stat = sb("stat", [128, 4, NTB], F32)
        h1T = sb("h1T", [128, KC, TP], BF16)
        b_c, b_out, b_st = Buf("consts"), Buf("out"), Buf("stats")
        em.dma("sp", v32[:], vec32[:, :, :], writes=[b_c])
        em.dma("sp", lnt[:], lnv[:, :, :], writes=[b_c])
        em.dma("sp", wsf[:], wsT[:, :, :], writes=[b_c])
        em.dma("sp", bsbt[:], bsb[:, :, :], writes=[b_c])
        em.op("dve", lambda e: e.memset(ones[:], 1.0), writes=[b_c])
        ts(em, gs1[:], v32[:, 2, :], 1.0, None, ALU.add, None, [b_c], [b_c])
        tt(em, gs1[:], gs1[:], v32[:, 1, :], ALU.mult, [b_c], [b_c])
        cast(em, wsb[:], wsf[:], [b_c], [b_c], eng="dve")
        wsf2 = wsf[:].rearrange("p g i -> p (g i)")
        rsb2 = rsb[:].rearrange("p g i -> p (g i)")
        for q in range((NG * 128 + 511) // 512):
            c0, c1 = q * 512, min(NG * 128, (q + 1) * 512)
            pt, pb = PS.next()
            mm(em, pt[:, 0:c1 - c0], ones[:, :], wsf2[:, c0:c1], True, True, [b_c], [pb])
            cast(em, rsb2[:, c0:c1], pt[:, 0:c1 - c0], [pb], [b_c], eng="dve")
        gate0, shift1, gate1, fing = v32[:, 0, :], v32[:, 3, :], v32[:, 4, :], v32[:, 5, :]
        h1b = [[Buf("h1_%d_%d" % (s, n)) for n in range(KC)] for s in range(NS)]
        xnew_b = [Buf("xnewD%d" % p) for p in range(NPASS)]
        gvb = [Buf("gvD%d" % tb) for tb in range(NTB)]
        yb = [Buf("yD%d" % cc) for cc in range(NCC)]
        xfb = [Buf("xfin%d" % n) for n in range(KC)]

        def rmsnorm(sqr, tmpr, rs_t, rs_b, xt, xb, emit_chunk):
            pt, pb = PS.next()
            for n in range(KC):
                sq, sqb = sqr.next()
                act(em, sq[:], xt[:, n, :], AF.Square, [xb[n]], [sqb])
                mm(em, pt[:, 0:TS], ones[:, :], sq[:], n == 0, n == KC - 1, [sqb, b_c], [pb])
            act(em, rs_t[:], pt[:, 0:TS], AF.Sqrt, [pb], [rs_b], scale=1.0 / D, bias=NORM_EPS)
            em.op("dve", lambda e: e.reciprocal(out=rs_t[:], in_=rs_t[:]), reads=[rs_b], writes=[rs_b])
            for n in range(KC):
                tmp, tmpb = tmpr.next()
                tt(em, tmp[:], xt[:, n, :], rs_t[:], ALU.mult, [xb[n], rs_b], [tmpb])
                emit_chunk(n, tmp, tmpb)

        for ps_i in range(NPASS):
            P0 = ps_i * TP
            em.barrier()
            with ExitStack() as ph:
                psb = lambda name, shape, dt: ph.enter_context(nc.sbuf_tensor(_u(name), shape, dt))
                xt = psb("p1_xt", [128, KC, TS], F32)
                ot = psb("p1_ot", [128, KC, TS], BF16)
                wfr = Ring(nc, ph, "p1_wf", [128, KC, 128], F32, 2)
                wbr = Ring(nc, ph, "p1_wb", [128, KC, 128], BF16, 2)
                sqr = Ring(nc, ph, "p1_sq", [128, TS], F32, 2)
                tmpr = Ring(nc, ph, "p1_tmp", [128, TS], F32, 2)
                rs_t = psb("p1_rs", [128, TS], F32)
                rs_b = Buf("rs")
                xb = [Buf("xt%d" % n) for n in range(KC)]
                ob = Buf("ot")
                for s in range(NS):
                    t0 = P0 + s * TS
                    em.dma("sp", xt[:], xT[:, :, t0:t0 + TS], writes=xb)
                    em.dma("sp", ot[:], oT[:, :, t0:t0 + TS], writes=[ob])
                    for n in range(KC):
                        wf, wfb = wfr.next()
                        em.dma("sp", wf[:],
                               w_out[:, n * 128:(n + 1) * 128].rearrange("(k p) n -> p k n", p=128), writes=[wfb])
                        wb, wbb = wbr.next()
                        cast(em, wb[:], wf[:], [wfb], [wbb], eng="dve")
                        pt, pb = PS.next()
                        for k in range(KC):
                            mm(em, pt[:, 0:TS], wb[:, k, :], ot[:, k, :], k == 0, k == KC - 1, [wbb, ob], [pb])
                        stt(em, xt[:, n, :], pt[:, 0:TS], gate0[:, n:n + 1], xt[:, n, :], ALU.mult, ALU.add,
                            [pb, b_c, xb[n]], [xb[n]])
                    em.dma("pool", xnewD[:, :, t0:t0 + TS], xt[:], reads=xb, writes=[xnew_b[ps_i]])

                    def emit_h1(n, tmp, tmpb, s=s):
                        act(em, h1T[:, n, s * TS:(s + 1) * TS], tmp[:], AF.Identity, [tmpb, b_c], [h1b[s][n]],
                            scale=gs1[:, n:n + 1], bias=shift1[:, n:n + 1])
                    rmsnorm(sqr, tmpr, rs_t, rs_b, xt, xb, emit_h1)
                em.barrier()
            with ExitStack() as ph:
                psb = lambda name, shape, dt: ph.enter_context(nc.sbuf_tensor(_u(name), shape, dt))
                wfr = Ring(nc, ph, "v_wf", [128, 8, 512], F32, 2)
                wbr = Ring(nc, ph, "v_wb", [128, KC, 512], BF16, 2)
                gvr = Ring(nc, ph, "v_gv", [128, 512], BF16, 3)
                jkr = Ring(nc, ph, "v_jk", [128, 512], BF16, 2)
                ssum = psb("v_ssum", [128, NTB, NCT], F32)
                ssq = psb("v_ssq", [128, NTB, NCT], F32)
                st2 = psb("v_st2", [128, 2, NTB], F32)
                for ct in range(NCT):
                    col0 = CW + ct * 512
                    wb, wbb = wbr.next()
                    for q in range(4):
                        wf, wfb = wfr.next()
                        em.dma("sp", wf[:],
                               gw_in[q * 1024:(q + 1) * 1024, col0:col0 + 512].rearrange("(k p) n -> p k n", p=128),
                               writes=[wfb])
                        cast(em, wb[:, q * 8:(q + 1) * 8, :], wf[:], [wfb], [wbb], eng="dve")
                    for tb in range(NTB):
                        pt, pb = PS.next()
                        s0 = (tb * 128) // TS
                        for k in range(KC):
                            mm(em, pt[:, :], h1T[:, k, tb * 128:(tb + 1) * 128], wb[:, k, :], k == 0, k == KC - 1,
                               [wbb, h1b[s0][k]], [pb])
                        gv, gb = gvr.next()
                        act(em, gv[:], pt[:, :], AF.Gelu_apprx_tanh, [pb], [gb, b_st],
                            accum_out=ssum[:, tb, ct:ct + 1])
                        jk, jb = jkr.next()
                        act(em, jk[:], gv[:], AF.Square, [gb], [jb, b_st], accum_out=ssq[:, tb, ct:ct + 1])
                        em.dma("pool", gvD[tb * 128:(tb + 1) * 128, ct * 512:(ct + 1) * 512], gv[:],
                               reads=[gb], writes=[gvb[tb]])
                em.op("dve", lambda e: e.tensor_reduce(out=stat[:, 0, :], in_=ssum[:], axis=AX.X, op=ALU.add),
                      reads=[b_st], writes=[b_st])
                em.op("dve", lambda e: e.tensor_reduce(out=stat[:, 1, :], in_=ssq[:], axis=AX.X, op=ALU.add),
                      reads=[b_st], writes=[b_st])
                ts(em, stat[:, 0, :], stat[:, 0, :], 1.0 / CW, None, ALU.mult, None, [b_st], [b_st])
                ts(em, stat[:, 1, :], stat[:, 1, :], 1.0 / CW, None, ALU.mult, None, [b_st], [b_st])
                tt(em, st2[:, 0, :], stat[:, 0, :], stat[:, 0, :], ALU.mult, [b_st], [b_st])
                tt(em, st2[:, 1, :], stat[:, 1, :], st2[:, 0, :], ALU.subtract, [b_st], [b_st])
                act(em, stat[:, 2, :], st2[:, 1, :], AF.Sqrt, [b_st], [b_st], scale=1.0, bias=LN_EPS)
                em.op("dve", lambda e: e.reciprocal(out=stat[:, 2, :], in_=stat[:, 2, :]), reads=[b_st], writes=[b_st])
                tt(em, st2[:, 0, :], stat[:, 0, :], stat[:, 2, :], ALU.mult, [b_st], [b_st])
                ts(em, stat[:, 3, :], st2[:, 0, :], -1.0, None, ALU.mult, None, [b_st], [b_st])
                em.barrier()
            with ExitStack() as ph:
                wfr2 = Ring(nc, ph, "u_wf", [128, 8, 256], F32, 2)
                wbr2 = Ring(nc, ph, "u_wb", [128, KC, 256], BF16, 4)
                utr = Ring(nc, ph, "u_ut", [128, TP], F32, 2)
                gtr = Ring(nc, ph, "u_gt", [128, TP], F32, 2)
                ytr = Ring(nc, ph, "u_yt", [128, TP], BF16, 2)
                gvtr = Ring(nc, ph, "u_gvt", [128, 128], BF16, 4)
                ntr = Ring(nc, ph, "u_nt", [128, 128], BF16, 4)
                addr = Ring(nc, ph, "u_add", [128, 128], F32, 2)
                vmr = Ring(nc, ph, "u_vm", [128, 512], F32, 2)
                for cc in range(NCC):
                    g = cc // (GRP_W // 128)
                    sub = cc % 2
                    if sub == 0:
                        wpair = []
                        for which, cbase in ((0, 0), (1, 2 * CW)):
                            wb2, wbb = wbr2.next()
                            for q in range(4):
                                wf, wfb = wfr2.next()
                                em.dma("sp", wf[:],
                                       gw_in[q * 1024:(q + 1) * 1024, cbase + cc * 128:cbase + cc * 128 + 256].rearrange("(k p) n -> p k n", p=128),
                                       writes=[wfb])
                                cast(em, wb2[:, q * 8:(q + 1) * 8, :], wf[:], [wfb], [wbb], eng="dve")
                            wpair.append((wb2, wbb))
                    wbs = [(wpair[0][0][:, :, sub * 128:(sub + 1) * 128], wpair[0][1]),
                           (wpair[1][0][:, :, sub * 128:(sub + 1) * 128], wpair[1][1])]
                    ut, ub = utr.next()
                    gt, gtb = gtr.next()
                    for tsi in range(NTS):
                        for (wb, wbb), dst, dstb, fn in ((wbs[0], ut, ub, AF.Gelu_apprx_tanh), (wbs[1], gt, gtb, AF.Silu)):
                            pt, pb = PS.next()
                            for k in range(KC):
                                rd = [wbb] + [h1b[s_][k] for s_ in range(tsi * 512 // TS, (tsi + 1) * 512 // TS)]
                                mm(em, pt[:, :], wb[:, k, :], h1T[:, k, tsi * 512:(tsi + 1) * 512], k == 0, k == KC - 1,
                                   rd, [pb])
                            act(em, dst[:, tsi * 512:(tsi + 1) * 512], pt[:, :], fn, [pb], [dstb])
                    addt, addb = addr.next()
                    stt(em, addt[:], rsb[:, g, :], lnt[:, 1, cc:cc + 1], bsbt[:, g, :], ALU.mult, ALU.add, [b_c], [addb])
                    yt, ytb = ytr.next()
                    for hf in range(NTS):
                        pt, pb = PS.next()
                        for j4 in range(4):
                            tb = hf * 4 + j4
                            gvt, gvtb = gvtr.next()
                            em.dma("sp", gvt[:],
                                   gvD[tb * 128:(tb + 1) * 128, cc * 128:(cc + 1) * 128], reads=[gvb[tb]], writes=[gvtb])
                            nt, ntb = ntr.next()
                            act(em, nt[:], gvt[:], AF.Identity, [gvtb, b_st], [ntb],
                                scale=stat[:, 2, tb:tb + 1], bias=stat[:, 3, tb:tb + 1])
                            mm(em, pt[:, j4 * 128:(j4 + 1) * 128], nt[:, :], wsb[:, g, :], True, True, [ntb, b_c], [pb])
                        vm, vmb = vmr.next()
                        stt(em, vm[:].rearrange("p (a i) -> p a i", a=4), pt[:, :].rearrange("p (a i) -> p a i", a=4),
                            lnt[:, 0, cc:cc + 1], addt[:, None, :].broadcast_to([128, 4, 128]), ALU.mult, ALU.add,
                            [pb, b_c, addb], [vmb])
                        tt(em, vm[:], vm[:], ut[:, hf * 512:(hf + 1) * 512], ALU.mult, [vmb, ub], [vmb])
                        tt(em, yt[:, hf * 512:(hf + 1) * 512], vm[:], gt[:, hf * 512:(hf + 1) * 512], ALU.mult,
                           [vmb, gtb], [ytb])
                    em.dma("pool", yD[cc], yt[:], reads=[ytb], writes=[yb[cc]])
                em.barrier()
            with ExitStack() as ph:
                w2fr = Ring(nc, ph, "o_wf", [128, 512], F32, 3)
                w2br = Ring(nc, ph, "o_wb", [128, 512], BF16, 3)
                yh = ph.enter_context(nc.sbuf_tensor(_u("o_yh"), [128, NCC, 512], BF16))
                NYG = 8
                yhb = [Buf("yh%d" % j) for j in range((NCC + NYG - 1) // NYG)]
                xnr = Ring(nc, ph, "o_xn", [128, 512], F32, 3)
                for tsi in range(NTS):
                    for j in range(len(yhb)):
                        c0, c1 = j * NYG, min(NCC, (j + 1) * NYG)
                        em.dma("act", yh[:, c0:c1, :], yD[c0:c1, :, tsi * 512:(tsi + 1) * 512].rearrange("c p t -> p c t"),
                               reads=yb[c0:c1], writes=[yhb[j]])
                    for ng in range(KC // 4):
                        pss = [PS.next() for _ in range(4)]
                        for cc in range(NCC):
                            wf, wfb = w2fr.next()
                            em.dma("sp", wf[:], gw_out[cc * 128:(cc + 1) * 128, ng * 512:(ng + 1) * 512], writes=[wfb])
                            wb, wbb = w2br.next()
                            cast(em, wb[:], wf[:], [wfb], [wbb], eng="dve")
                            for nn in range(4):
                                pt, pb = pss[nn]
                                mm(em, pt[:, :], wb[:, nn * 128:(nn + 1) * 128], yh[:, cc, :], cc == 0, cc == NCC - 1,
                                   [wbb, yhb[cc // NYG]], [pb])
                        for nn in range(4):
                            n = ng * 4 + nn
                            pt, pb = pss[nn]
                            xn, xnb = xnr.next()
                            em.dma("sp", xn[:], xnewD[:, n, P0 + tsi * 512:P0 + (tsi + 1) * 512], reads=[xnew_b[ps_i]], writes=[xnb])
                            stt(em, xn[:], pt[:, :], gate1[:, n:n + 1], xn[:], ALU.mult, ALU.add, [pb, b_c, xnb], [xnb])
                            em.dma("pool", xfinD[:, n, tsi * 512:(tsi + 1) * 512], xn[:], reads=[xnb], writes=[xfb[n]])
                em.barrier()
            with ExitStack() as ph:
                psb = lambda name, shape, dt: ph.enter_context(nc.sbuf_tensor(_u(name), shape, dt))
                xt = psb("f_xt", [128, KC, TS], F32)
                ores = psb("f_o", [128, KC, TS], F32)
                sqr = Ring(nc, ph, "f_sq", [128, TS], F32, 2)
                tmpr = Ring(nc, ph, "f_tmp", [128, TS], F32, 2)
                rs_t = psb("f_rs", [128, TS], F32)
                rs_b = Buf("frs")
                xb = [Buf("fx%d" % n) for n in range(KC)]
                orb = [Buf("fo%d" % n) for n in range(KC)]
                for s in range(NS):
                    em.dma("sp", xt[:], xfinD[:, :, s * TS:(s + 1) * TS], reads=xfb, writes=xb)

                    def emit_out(n, tmp, tmpb):
                        ts(em, ores[:, n, :], tmp[:], fing[:, n:n + 1], None, ALU.mult, None, [tmpb, b_c], [orb[n]], eng="pool")
                    rmsnorm(sqr, tmpr, rs_t, rs_b, xt, xb, emit_out)
                    em.dma("sp", outT[:, :, P0 + s * TS:P0 + (s + 1) * TS], ores[:], reads=orb, writes=[b_out])
                em.barrier()
        em.finish([b_out])
    return nc


def fm(v, nchunk):
    return np.ascontiguousarray(np.asarray(v, np.float32).reshape(nchunk, 128).T)


def fmT(a):
    t, f = a.shape
    return np.ascontiguousarray(a.reshape(t, f // 128, 128).transpose(2, 1, 0))


def l3_core_inputs(x_tok, oT_tok, w_out, gw_in, gw_out, gate0, norm_g1, scale1, shift1, gate1, final_g,
                   ln_g, ln_b, w_s, b_s):
    NG = w_s.shape[0]
    NCC = NG * GRP_W // 128
    vec32 = np.stack([fm(v, KC) for v in (gate0, norm_g1, scale1, shift1, gate1, final_g)], axis=1)
    lnv = np.stack([fm(ln_g, NCC), fm(ln_b, NCC)], axis=1)
    return {
        "xT": fmT(x_tok), "oT": oT_tok, "w_out": np.ascontiguousarray(w_out), "gw_in": np.ascontiguousarray(gw_in),
        "gw_out": np.ascontiguousarray(gw_out), "vec32": np.ascontiguousarray(vec32), "lnv": np.ascontiguousarray(lnv),
        "wsT": np.ascontiguousarray(np.transpose(w_s, (2, 0, 1))),
        "bsb": np.ascontiguousarray(np.broadcast_to(b_s[None, :, :], (128,) + b_s.shape)),
    }


CHW = [128] * 31 + [96] * 4
CHO = [sum(CHW[:i]) for i in range(len(CHW) + 1)]
NWC = CHO[-1]
A_SCALE = 128 ** -0.5
NEG = -30000.0
CH = 64


def rmsnorm_fm(em, PS, ones, b_c, sqr, tmpr, rs_t, rs_b, xt, xb, N, emit_chunk):
    pt, pb = PS.next()
    for n in range(KC):
        sq, sqb = sqr.next()
        act(em, sq[:, 0:N], xt[:, n, 0:N], AF.Square, [xb[n]], [sqb])
        mm(em, pt[:, 0:N], ones[:, :], sq[:, 0:N], n == 0, n == KC - 1, [sqb, b_c], [pb])
    act(em, rs_t[:, 0:N], pt[:, 0:N], AF.Sqrt, [pb], [rs_b], scale=1.0 / D, bias=NORM_EPS)
    em.op("dve", lambda e: e.reciprocal(out=rs_t[:, 0:N], in_=rs_t[:, 0:N]), reads=[rs_b], writes=[rs_b])
    for n in range(KC):
        tmp, tmpb = tmpr.next()
        tt(em, tmp[:, 0:N], xt[:, n, 0:N], rs_t[:, 0:N], ALU.mult, [xb[n], rs_b], [tmpb])
        emit_chunk(n, tmp, tmpb)


def build_l2(TL, TC, do_b=True, dbg=False):
    TT = TL + TC
    NQB = TL // 128
    nc = _new_nc()
    inp = lambda name, shape, dt=F32: nc.dram_tensor(name, shape, dt, kind="ExternalInput").ap()
    xT = inp("xT", [128, KC, TL])
    cT = inp("cT", [128, KC, TC])
    wcols = inp("wcols", [D, NWC])
    vec = inp("vec", [128, 6, KC])
    sinkb = inp("sinkb", [128, 4])
    cosT = inp("cosT", [128, TL])
    sinT = inp("sinT", [128, TL])
    ident = inp("ident", [128, 128])
    amask = inp("amask", [128, 3, 384])
    oT = nc.dram_tensor("oT", [128, 8, TL], BF16, kind="ExternalOutput").ap()
    hD = nc.dram_tensor("hD", [128, KC, TT], BF16).ap()
    wD = nc.dram_tensor("wD", [128, KC, NWC], BF16).ap()
    qD = nc.dram_tensor("qD", [4, 128, TL], BF16).ap()
    kD = nc.dram_tensor("kD", [128, TT], BF16).ap()
    vD = nc.dram_tensor("vD", [TT, 128], BF16).ap()
    gaD = nc.dram_tensor("gaD", [4, 128, TL], BF16).ap()
    gbD = nc.dram_tensor("gbD", [4, 128, TL], BF16).ap()
    zD = nc.dram_tensor("zD", [12, 128, TT], F32).ap()
    lD = nc.dram_tensor("lD", [4, 96, TT], F32).ap()
    with ExitStack() as es:
        em = Emit(nc, es)
        sb = lambda name, shape, dt: es.enter_context(nc.sbuf_tensor(name, shape, dt))
        PS = Ring(nc, es, "ps", [128, 512], F32, 6, psum=True)
        PB = Ring(nc, es, "pb", [128, 1024], BF16, 2, psum=True)
        b_c, b_out = Buf("consts"), Buf("out")
        vt = sb("vt", [128, 6, KC], F32)
        gsl = sb("gsl", [128, KC], F32)
        gsc = sb("gsc", [128, KC], F32)
        ones = sb("ones", [128, 128], F32)
        idf = sb("idf", [128, 128], F32)
        idb = sb("idb", [128, 128], BF16)
        skb = sb("skb", [128, 4], F32)
        msk = sb("msk", [128, 3, 384], F32)
        em.dma("sp", vt[:], vec[:, :, :], writes=[b_c])
        em.dma("sp", idf[:], ident[:, :], writes=[b_c])
        em.dma("sp", skb[:], sinkb[:, :], writes=[b_c])
        em.dma("sp", msk[:], amask[:, :, :], writes=[b_c])
        em.op("dve", lambda e: e.memset(ones[:], 1.0), writes=[b_c])
        cast(em, idb[:], idf[:], [b_c], [b_c], eng="dve")
        for dst, sidx in ((gsl, 1), (gsc, 3)):
            ts(em, dst[:], vt[:, sidx, :], 1.0, None, ALU.add, None, [b_c], [b_c])
            tt(em, dst[:], dst[:], vt[:, 0, :], ALU.mult, [b_c], [b_c])
        hb = Buf("hD")
        wDb = Buf("wD")
        with ExitStack() as ph:
            wfr = Ring(nc, ph, "w_wf", [128, 8, 512], F32, 3)
            wbr = Ring(nc, ph, "w_wb", [128, 8, 512], BF16, 3)
            ncol = (NWC + 511) // 512
            for ct in range(ncol):
                c0, c1 = ct * 512, min(NWC, (ct + 1) * 512)
                for q in range(4):
                    wf, wfb = wfr.next()
                    em.dma("sp" if q % 2 == 0 else "act", wf[:, :, 0:c1 - c0],
                           wcols[q * 1024:(q + 1) * 1024, c0:c1].rearrange("(k p) n -> p k n", p=128), writes=[wfb])
                    wb, wbb = wbr.next()
                    cast(em, wb[:, :, 0:c1 - c0], wf[:, :, 0:c1 - c0], [wfb], [wbb], eng="dve")
                    em.dma("pool", wD[:, q * 8:(q + 1) * 8, c0:c1], wb[:, :, 0:c1 - c0], reads=[wbb], writes=[wDb])
            em.barrier()
        with ExitStack() as ph:
            psb = lambda name, shape, dt: ph.enter_context(nc.sbuf_tensor(_u(name), shape, dt))
            TS = 256
            xts = [psb("h_xt%d" % i, [128, KC, TS], F32) for i in range(2)]
            hts = [psb("h_ht%d" % i, [128, KC, TS], BF16) for i in range(2)]
            sqr = Ring(nc, ph, "h_sq", [128, TS], F32, 3)
            tmpr = Ring(nc, ph, "h_tmp", [128, TS], F32, 3)
            rss = [psb("h_rs%d" % i, [128, TS], F32) for i in range(2)]
            rsbs = [Buf("rs0"), Buf("rs1")]
            xbs = [[Buf("hx%d_%d" % (i, n)) for n in range(KC)] for i in range(2)]
            hcbs = [[Buf("hh%d_%d" % (i, n)) for n in range(KC)] for i in range(2)]
            for s in range(TT // TS):
                t0 = s * TS
                i2 = s % 2
                xt, ht, xb, hcb = xts[i2], hts[i2], xbs[i2], hcbs[i2]
                isc = t0 >= TL
                src = cT[:, :, t0 - TL:t0 - TL + TS] if isc else xT[:, :, t0:t0 + TS]
                em.dma("sp", xt[:], src, writes=xb)
                gs_, sh_ = (gsc, vt[:, 4, :]) if isc else (gsl, vt[:, 2, :])

                def emit_h(n, tmp, tmpb, gs_=gs_, sh_=sh_, ht=ht, hcb=hcb):
                    act(em, ht[:, n, :], tmp[:], AF.Identity, [tmpb, b_c], [hcb[n]],
                        scale=gs_[:, n:n + 1], bias=sh_[:, n:n + 1])
                rmsnorm_fm(em, PS, ones, b_c, sqr, tmpr, rss[i2], rsbs[i2], xt, xb, TS, emit_h)
                em.dma("act", hD[:, :, t0:t0 + TS], ht[:], reads=hcb, writes=[hb])
            em.barrier()
        qDb, kDb, vDb, gaDb, gbDb, zDb, lDb = (Buf(n) for n in ("qD", "kD", "vD", "gaD", "gbD", "zD", "lD"))
        with ExitStack() as ph:
            psb = lambda name, shape, dt: ph.enter_context(nc.sbuf_tensor(_u(name), shape, dt))
            htr = Ring(nc, ph, "p_h", [128, KC, 512], BF16, 2)
            wr = Ring(nc, ph, "p_w", [128, KC, 128], BF16, 4)
            csr = Ring(nc, ph, "p_cs", [128, 2, 512], F32, 2)
            e32 = Ring(nc, ph, "p_e32", [128, 512], F32, 4)
            e16 = Ring(nc, ph, "p_e16", [128, 512], BF16, 4)
            tiles = [(t0, 512, False) for t0 in range(0, TL, 512)] + [(TL + t0, min(512, TC - t0), True) for t0 in range(0, TC, 512)]
            for (t0, N, isc) in tiles:
                ht, htb = htr.next()
                em.dma("sp", ht[:, :, 0:N], hD[:, :, t0:t0 + N], reads=[hb], writes=[htb])
                if not isc:
                    cs, csb = csr.next()
                    em.dma("sp", cs[:, 0, 0:N], cosT[:, t0:t0 + N], writes=[csb])
                    em.dma("sp", cs[:, 1, 0:N], sinT[:, t0:t0 + N], writes=[csb])

                def proj(ci, swap_roles=False):
                    w, wb_ = wr.next()
                    wd = CHW[ci]
                    em.dma("sp", w[:, :, 0:wd], wD[:, :, CHO[ci]:CHO[ci] + wd], reads=[wDb], writes=[wb_])
                    pt, pb = PS.next()
                    if not swap_roles:
                        for k in range(KC):
                            mm(em, pt[0:wd, 0:N], w[:, k, 0:wd], ht[:, k, 0:N], k == 0, k == KC - 1, [wb_, htb], [pb])
                    else:
                        for blk in range(N // 128):
                            for k in range(KC):
                                mm(em, pt[:, blk * 128:(blk + 1) * 128], ht[:, k, blk * 128:(blk + 1) * 128], w[:, k, :],
                                   k == 0, k == KC - 1, [wb_, htb], [pb])
                    return pt, pb

                def rope_pair(ci, scale, dst_ap, dstbuf):
                    p1, b1 = proj(ci)
                    p2, b2 = proj(ci + (4 if ci < 4 else 1))
                    t1, t1b = e32.next()
                    t2, t2b = e32.next()
                    stt(em, t1[:, 0:N], p1[:, 0:N], scale, cs[:, 0, 0:N], ALU.mult, ALU.mult, [b1, csb], [t1b])
                    stt(em, t2[:, 0:N], p2[:, 0:N], scale, cs[:, 1, 0:N], ALU.mult, ALU.mult, [b2, csb], [t2b])
                    o16, o16b = e16.next()
                    tt(em, o16[:, 0:N], t1[:, 0:N], t2[:, 0:N], ALU.add, [t1b, t2b], [o16b], eng="pool")
                    em.dma("pool", dst_ap, o16[:, 0:N], reads=[o16b], writes=[dstbuf])

                if not isc:
                    for h in range(4):
                        rope_pair(h, A_SCALE, qD[h, :, t0:t0 + N], qDb)
                    rope_pair(8, 1.0, kD[:, t0:t0 + N], kDb)
                else:
                    p1, b1 = proj(8)
                    o16, o16b = e16.next()
                    act(em, o16[:, 0:N], p1[:, 0:N], AF.Copy, [b1], [o16b])
                    em.dma("pool", kD[:, t0:t0 + N], o16[:, 0:N], reads=[o16b], writes=[kDb])
                p1, b1 = proj(10, swap_roles=True)
                o16, o16b = e16.next()
                act(em, o16[:, 0:N], p1[:, 0:N], AF.Copy, [b1], [o16b])
                em.dma("pool", vD[t0:t0 + N, :].rearrange("(b p) d -> p b d", p=128),
                       o16[:, 0:N].rearrange("p (b d) -> p b d", d=128), reads=[o16b], writes=[vDb])
                if not isc:
                    for h in range(4):
                        for ci, dstD, dstb in ((11 + h, gaD, gaDb), (27 + h, gbD, gbDb)):
                            if ci >= 27 and not do_b:
                                continue
                            p1, b1 = proj(ci)
                            o16, o16b = e16.next()
                            act(em, o16[:, 0:N], p1[:, 0:N], AF.Silu, [b1], [o16b])
                            em.dma("pool", dstD[h, :, t0:t0 + N], o16[:, 0:N], reads=[o16b], writes=[dstb])
                if do_b:
                    for j in range(12):
                        p1, b1 = proj(15 + j)
                        o32, o32b = e32.next()
                        act(em, o32[:, 0:N], p1[:, 0:N], AF.Copy, [b1], [o32b])
                        em.dma("pool", zD[j, :, t0:t0 + N], o32[:, 0:N], reads=[o32b], writes=[zDb])
                    for j in range(4):
                        p1, b1 = proj(31 + j)
                        o32, o32b = e32.next()
                        act(em, o32[0:96, 0:N], p1[0:96, 0:N], AF.Tanh if j < 2 else AF.Copy, [b1], [o32b])
                        em.dma("pool", lD[j, :, t0:t0 + N], o32[0:96, 0:N], reads=[o32b], writes=[lDb])
            em.barrier()
        with ExitStack() as ph:
            psb = lambda name, shape, dt: ph.enter_context(nc.sbuf_tensor(_u(name), shape, dt))
            kc = psb("a_kc", [128, TC], BF16)
            vc = psb("a_vc", [128, TC // 128, 128], BF16)
            em.dma("sp", kc[:], kD[:, TL:TL + TC], reads=[kDb], writes=[b_c])
            em.dma("sp", vc[:], vD[TL:TL + TC, :].rearrange("(b p) d -> p b d", p=128), reads=[vDb], writes=[b_c])
            NCB = TC // 128
            NK = 384 + TC
            kbr = Ring(nc, ph, "a_kb", [128, 384], BF16, 2)
            vbr = Ring(nc, ph, "a_vb", [128, 3, 128], BF16, 2)
            qtr = Ring(nc, ph, "a_q", [128, 4, 128], BF16, 2)
            gar = Ring(nc, ph, "a_ga", [128, 4, 128], BF16, 2)
            sbr = Ring(nc, ph, "a_s", [128, 384], F32, 4)
            pr = Ring(nc, ph, "a_p", [128, NK], BF16, 4)
            ptr_ = Ring(nc, ph, "a_pT", [128, NK], BF16, 4)
            smr = Ring(nc, ph, "a_sm", [128, 8], F32, 8)
            otr = Ring(nc, ph, "a_o", [128, 4, 128], BF16, 2)
            for n in range(NQB):
                kb, kbb = kbr.next()
                vb, vbb = vbr.next()
                blks = [min(max(n - 1 + j, 0), NQB - 1) for j in range(3)]
                for j, bk in enumerate(blks):
                    em.dma("sp", kb[:, j * 128:(j + 1) * 128], kD[:, bk * 128:(bk + 1) * 128], reads=[kDb], writes=[kbb])
                    em.dma("act", vb[:, j, :], vD[bk * 128:(bk + 1) * 128, :], reads=[vDb], writes=[vbb])
                mi = 0 if n == 0 else (2 if n == NQB - 1 else 1)
                qt, qtb = qtr.next()
                ga, gab = gar.next()
                em.dma("sp", qt[:], qD[:, :, n * 128:(n + 1) * 128].rearrange("h d q -> d h q"), reads=[qDb], writes=[qtb])
                em.dma("act", ga[:], gaD[:, :, n * 128:(n + 1) * 128].rearrange("h d q -> d h q"), reads=[gaDb], writes=[gab])
                ot, otb = otr.next()

                def head_gen(h, kb=kb, kbb=kbb, vb=vb, vbb=vbb, qt=qt, qtb=qtb, ga=ga, gab=gab, ot=ot, otb=otb, mi=mi):
                        p1, b1 = PS.next()
                        p2, b2 = PS.next()
                        mm(em, p1[:, 0:384], qt[:, h, :], kb[:, :], True, True, [qtb, kbb], [b1])
                        mm(em, p2[:, 0:TC], qt[:, h, :], kc[:, :], True, True, [qtb, b_c], [b2])
                        s_, s_b = sbr.next()
                        tt(em, s_[:], p1[:, 0:384], msk[:, mi, :], ALU.add, [b1, b_c], [s_b])
                        sm, smb = smr.next()
                        em.op("dve", lambda e: e.tensor_reduce(out=sm[:, 0:1], in_=s_[:], axis=AX.X, op=ALU.max), reads=[s_b], writes=[smb])
                        em.op("dve", lambda e: e.tensor_reduce(out=sm[:, 1:2], in_=p2[:, 0:TC], axis=AX.X, op=ALU.max), reads=[b2], writes=[smb])
                        stt(em, sm[:, 2:3], sm[:, 0:1], skb[:, h:h + 1], sm[:, 1:2], ALU.max, ALU.max, [smb, b_c], [smb])
                        ts(em, sm[:, 3:4], sm[:, 2:3], -1.0, None, ALU.mult, None, [smb], [smb])
                        yield
                        p_, p_b = pr.next()
                        act(em, p_[:, 0:384], s_[:], AF.Exp, [s_b, smb], [p_b, smb], bias=sm[:, 3:4], scale=1.0, accum_out=sm[:, 4:5])
                        act(em, p_[:, 384:NK], p2[:, 0:TC], AF.Exp, [b2, smb], [p_b, smb], bias=sm[:, 3:4], scale=1.0, accum_out=sm[:, 5:6])
                        act(em, sm[:, 6:7], sm[:, 3:4], AF.Exp, [smb, b_c], [smb], bias=skb[:, h:h + 1], scale=1.0)
                        yield
                        tt(em, sm[:, 7:8], sm[:, 4:5], sm[:, 5:6], ALU.add, [smb], [smb])
                        tt(em, sm[:, 7:8], sm[:, 7:8], sm[:, 6:7], ALU.add, [smb], [smb])
                        em.op("dve", lambda e: e.reciprocal(out=sm[:, 7:8], in_=sm[:, 7:8]), reads=[smb], writes=[smb])
                        ts(em, p_[:], p_[:], sm[:, 7:8], None, ALU.mult, None, [p_b, smb], [p_b])
                        yield
                        pT_ps, pTb = PB.next()
                        nb = 3 + NCB
                        for j in range(nb):
                            em.op("pe", lambda e, j=j: e.transpose(out=pT_ps[:, j * 128:(j + 1) * 128], in_=p_[:, j * 128:(j + 1) * 128],
                                                                 identity=idb[:, :]), reads=[p_b, b_c], writes=[pTb])
                        pT, pTsb = ptr_.next()
                        act(em, pT[:], pT_ps[:, 0:NK], AF.Copy, [pTb], [pTsb])
                        yield
                        po, pob = PS.next()
                        for j in range(nb):
                            vv = vb[:, j, :] if j < 3 else vc[:, j - 3, :]
                            mm(em, po[:, 0:128], vv, pT[:, j * 128:(j + 1) * 128], j == 0, j == nb - 1, [vbb, b_c, pTsb], [pob])
                        tt(em, ot[:, h, :], po[:, 0:128], ga[:, h, :], ALU.mult, [pob, gab], [otb])

                for pair in ((0, 1), (2, 3)):
                    gens = [head_gen(h) for h in pair]
                    while gens:
                        gens = [g_ for g_ in gens if next(g_, "done") != "done"]
                em.dma("pool", oT[:, 0:4, n * 128:(n + 1) * 128], ot[:], reads=[otb], writes=[b_out])
            em.barrier()
        if do_b:
            _l2_rwkv(nc, es, em, PS, PB, TL, TC, zD, lD, gbD, oT, (zDb, lDb, gbDb), b_c, b_out, ones, idf, idb, inp, dbg)
        em.finish([b_out])
    return nc


GN_EPS = 64e-5
DECAY_K = -0.6065306597126334


def _l2_rwkv(nc, es, em, PS, PB, TL, TC, zD, lD, gbD, oT, dbufs, b_c, b_out, ones, idf, idb, inp, dbg=False):
    zDb, lDb, gbDb = dbufs
    TT = TL + TC
    NCH = TT // CH
    cv = inp("cv", [128, 3, 3, 4])
    pv = inp("pv", [128, 9, 4])
    w2c = inp("w2c", [96, 2, 512])
    a2c = inp("a2c", [96, 2, 512])
    gnb = inp("gnb", [128, 2, 512])
    rmask = inp("rmask", [128, 4, 128])
    blk1 = inp("blk1", [128, 128])
    kd = {"kind": "ExternalOutput"} if dbg else {}
    opD = nc.dram_tensor("opD", [2, 4, 4, 128, TT], BF16, **kd).ap()
    vcD = nc.dram_tensor("vcD", [4, 128, TT], BF16, **kd).ap()
    bvD = nc.dram_tensor("bvD", [4, 128, TL], F32, **kd).ap()
    gcD = nc.dram_tensor("gcD", [2, 4, 128, NCH], F32, **kd).ap()
    yD = nc.dram_tensor("yD", [2, TL, 8, 64], F32, **kd).ap()
    opDb, vcDb, bvDb, gcDb = Buf("opD"), Buf("vcD"), Buf("bvD"), Buf("gcD")
    yDb = [Buf("yD0"), Buf("yD1")]
    sb = lambda name, shape, dt: es.enter_context(nc.sbuf_tensor(name, shape, dt))
    cvt = sb("r_cv", [128, 3, 3, 4], F32)
    pvt = sb("r_pv", [128, 9, 4], F32)
    w2t = sb("r_w2", [96, 2, 512], F32)
    a2t = sb("r_a2", [96, 2, 512], F32)
    gnt = sb("r_gn", [128, 2, 512], F32)
    mkt = sb("r_mk", [128, 4, 128], F32)
    b1t = sb("r_b1", [128, 128], F32)
    for dst, src_ in ((cvt, cv), (pvt, pv), (w2t, w2c), (a2t, a2c), (gnt, gnb), (mkt, rmask), (b1t, blk1)):
        em.dma("sp", dst[:], src_, writes=[b_c])
    ts(em, pvt[:, 7, :], pvt[:, 5, :], -1.0, 1.0, ALU.mult, ALU.add, [b_c], [b_c])
    with ExitStack() as ph:
        def R_(name, shape, dt, n):
            return Ring(nc, ph, name, shape, dt, n)
        class L_:
            pass
        lanes = []
        for li in range(2):
            l_ = L_()
            l_.ztr = R_("b1_z%d" % li, [128, 514], F32, 3)
            l_.rkv = R_("b1_rkv%d" % li, [128, 3, 512], F32, 2)
            l_.t32 = R_("b1_t%d" % li, [128, 512], F32, 12)
            l_.kkr = R_("b1_kk%d" % li, [128, 512], F32, 2)
            l_.lwr = R_("b1_lw%d" % li, [96, 2, 512], F32, 2)
            l_.gct = R_("b1_gc%d" % li, [128, 8], F32, 4)
            l_.o16r = R_("b1_o16%d" % li, [128, 512], BF16, 6)
            lanes.append(l_)
        tiles = [(t0, 512, 0, TL) for t0 in range(0, TL, 512)] + [(TL + t0, min(512, TC - t0), TL, TT) for t0 in range(0, TC, 512)]

        def b1_iter(c, t0, N, s0, s1, LN):
            ztr, rkv, t32, kkr, lwr, gct, o16r = LN.ztr, LN.rkv, LN.t32, LN.kkr, LN.lwr, LN.gct, LN.o16r
            J = N // CH
            islat = s0 == 0
            x3, x3b = rkv.next()
            for kind in range(3):
                zt, ztb = ztr.next()
                j = kind * 4 + c
                em.dma("sp", zt[:, 1:N + 1], zD[j, :, t0:t0 + N], reads=[zDb], writes=[ztb])
                if t0 > s0:
                    em.dma("act", zt[:, 0:1], zD[j, :, t0 - 1:t0], reads=[zDb], writes=[ztb], allow_slow_non_contiguous=True)
                else:
                    em.op("pool", lambda e: e.memset(zt[:, 0:1], 0.0), writes=[ztb])
                if t0 + N < s1:
                    em.dma("act", zt[:, N + 1:N + 2], zD[j, :, t0 + N:t0 + N + 1], reads=[zDb], writes=[ztb], allow_slow_non_contiguous=True)
                else:
                    em.op("pool", lambda e: e.memset(zt[:, N + 1:N + 2], 0.0), writes=[ztb])
                o = x3[:, kind, 0:N]
                ts(em, o, zt[:, 0:N], cvt[:, kind, 0, c:c + 1], None, ALU.mult, None, [ztb, b_c], [x3b])
                stt(em, o, zt[:, 1:N + 1], cvt[:, kind, 1, c:c + 1], o, ALU.mult, ALU.add, [ztb, b_c, x3b], [x3b])
                stt(em, o, zt[:, 2:N + 2], cvt[:, kind, 2, c:c + 1], o, ALU.mult, ALU.add, [ztb, b_c, x3b], [x3b])
            r_t, k_t, v_t = x3[:, 0, 0:N], x3[:, 1, 0:N], x3[:, 2, 0:N]
            yield
            v16, v16b = o16r.next()
            cast(em, v16[:, 0:N], v_t, [x3b], [v16b], eng="pool")
            em.dma("pool", vcD[c, :, t0:t0 + N], v16[:, 0:N], reads=[v16b], writes=[vcDb])
            sq, sqb = t32.next()
            act(em, sq[:, 0:N], k_t, AF.Square, [x3b, b_c], [sqb], scale=pvt[:, 4, c:c + 1])
            pt, pb = PS.next()
            mm(em, pt[:, 0:N], b1t[:, :], sq[:, 0:N], True, True, [sqb, b_c], [pb])
            nr, nrb = t32.next()
            act(em, nr[:, 0:N], pt[:, 0:N], AF.Sqrt, [pb], [nrb])
            ts(em, nr[:, 0:N], nr[:, 0:N], 1e-12, None, ALU.max, None, [nrb], [nrb])
            em.op("dve", lambda e: e.reciprocal(out=nr[:, 0:N], in_=nr[:, 0:N]), reads=[nrb], writes=[nrb])
            kk, kkb = kkr.next()
            stt(em, kk[:, 0:N], k_t, pvt[:, 4, c:c + 1], nr[:, 0:N], ALU.mult, ALU.mult, [x3b, b_c, nrb], [kkb])
            if islat:
                rk, rkb = t32.next()
                stt(em, rk[:, 0:N], r_t, pvt[:, 6, c:c + 1], k_t, ALU.mult, ALU.mult, [x3b, b_c], [rkb])
                pt, pb = PS.next()
                mm(em, pt[:, 0:N], b1t[:, :], rk[:, 0:N], True, True, [rkb, b_c], [pb])
                bv, bvb = t32.next()
                tt(em, bv[:, 0:N], pt[:, 0:N], v_t, ALU.mult, [pb, x3b], [bvb])
                em.dma("pool", bvD[c, :, t0:t0 + N], bv[:, 0:N], reads=[bvb], writes=[bvDb])
            yield
            lw, lwb = lwr.next()
            la, lab = lwr.next()
            for d in range(2):
                em.dma("sp", lw[:, d, 0:N], lD[d, :, t0:t0 + N], reads=[lDb], writes=[lwb])
                em.dma("act", la[:, d, 0:N], lD[2 + d, :, t0:t0 + N], reads=[lDb], writes=[lab])
            for d in range(2):
                pt, pb = PS.next()
                mm(em, pt[:, 0:N], w2t[:, d, c * 128:(c + 1) * 128], lw[:, d, 0:N], True, True, [lwb, b_c], [pb])
                lg, lgb = t32.next()
                act(em, lg[:, 0:N], pt[:, 0:N], AF.Sigmoid, [pb, b_c], [lgb], bias=pvt[:, d, c:c + 1], scale=1.0)
                ts(em, lg[:, 0:N], lg[:, 0:N], DECAY_K, None, ALU.mult, None, [lgb], [lgb], eng="pool")
                pt, pb = PS.next()
                mm(em, pt[:, 0:N], a2t[:, d, c * 128:(c + 1) * 128], la[:, d, 0:N], True, True, [lab, b_c], [pb])
                a_, a_b = t32.next()
                act(em, a_[:, 0:N], pt[:, 0:N], AF.Sigmoid, [pb, b_c], [a_b], bias=pvt[:, 2 + d, c:c + 1], scale=1.0)
                yield
                P_, P_b = t32.next()
                for jj in range(J):
                    em.op("dve", lambda e, jj=jj: e.tensor_tensor_scan(
                        out=P_[:, jj * CH:(jj + 1) * CH], data0=ones[:, 0:CH], data1=lg[:, jj * CH:(jj + 1) * CH],
                        initial=0.0, op0=ALU.mult, op1=ALU.add), reads=[lgb, b_c], writes=[P_b])
                P3 = P_[:, 0:N].rearrange("p (j i) -> p j i", i=CH)
                tot = P3[:, :, CH - 1:CH]
                cin, cinb = t32.next()
                cex, cexb = t32.next()
                if d == 0:
                    tt(em, cex[:, 0:N], P_[:, 0:N], lg[:, 0:N], ALU.subtract, [P_b, lgb], [cexb], eng="pool")
                    cin_ap, cin_rd = P_[:, 0:N], P_b
                else:
                    tt(em, cex[:, 0:N].rearrange("p (j i) -> p j i", i=CH), tot.broadcast_to([128, J, CH]), P3, ALU.subtract,
                       [P_b], [cexb])
                    tt(em, cin[:, 0:N], cex[:, 0:N], lg[:, 0:N], ALU.add, [cexb, lgb], [cinb], eng="pool")
                    cin_ap, cin_rd = cin[:, 0:N], cinb
                yield
                g8, g8b = gct.next()
                act(em, g8[:, 0:J], P3[:, :, CH - 1], AF.Exp, [P_b], [g8b])
                em.dma("pool", gcD[d, c, :, t0 // CH:t0 // CH + J], g8[:, 0:J], reads=[g8b], writes=[gcDb])
                ein, einb = t32.next()
                eni, enib = t32.next()
                act(em, ein[:, 0:N], cin_ap, AF.Exp, [cin_rd], [einb])
                act(em, eni[:, 0:N], cin_ap, AF.Exp, [cin_rd], [enib], scale=-1.0)
                act(em, cex[:, 0:N], cex[:, 0:N], AF.Exp, [cexb], [cexb])
                yield
                o1, o1b = o16r.next()
                tt(em, o1[:, 0:N], kk[:, 0:N], cex[:, 0:N], ALU.mult, [kkb, cexb], [o1b], eng="pool")
                em.dma("sp", opD[d, 0, c, :, t0:t0 + N], o1[:, 0:N], reads=[o1b], writes=[opDb])
                o3, o3b = o16r.next()
                tt(em, o3[:, 0:N], r_t, ein[:, 0:N], ALU.mult, [x3b, einb], [o3b])
                em.dma("act", opD[d, 1, c, :, t0:t0 + N], o3[:, 0:N], reads=[o3b], writes=[opDb])
                yield
                o2, o2b = t32.next()
                tt(em, o2[:, 0:N], kk[:, 0:N], a_[:, 0:N], ALU.mult, [kkb, a_b], [o2b], eng="pool")
                o4, o4b = o16r.next()
                stt(em, o4[:, 0:N], o2[:, 0:N], -1.0, eni[:, 0:N], ALU.mult, ALU.mult, [o2b, enib], [o4b])
                em.dma("sp", opD[d, 2, c, :, t0:t0 + N], o4[:, 0:N], reads=[o4b], writes=[opDb])
                ts(em, a_[:, 0:N], a_[:, 0:N], pvt[:, 5, c:c + 1], pvt[:, 7, c:c + 1], ALU.mult, ALU.add, [a_b, b_c], [a_b])
                tt(em, a_[:, 0:N], a_[:, 0:N], k_t, ALU.mult, [a_b, x3b], [a_b], eng="pool")
                o5, o5b = o16r.next()
                tt(em, o5[:, 0:N], a_[:, 0:N], eni[:, 0:N], ALU.mult, [a_b, enib], [o5b])
                em.dma("act", opD[d, 3, c, :, t0:t0 + N], o5[:, 0:N], reads=[o5b], writes=[opDb])

        def rr_(gens):
            gens = list(gens)
            while gens:
                nxt = []
                for g_ in gens:
                    if next(g_, "done") != "done":
                        nxt.append(g_)
                gens = nxt
        work = [(c, t0, N, s0, s1) for c in range(4) for (t0, N, s0, s1) in tiles]
        for wi in range(0, len(work), 2):
            rr_([b1_iter(*work[wi + li], lanes[li]) for li in range(2) if wi + li < len(work)])
        em.barrier()
    with ExitStack() as ph:
        def R_(name, n, shape=(128, 128), dt=BF16):
            return Ring(nc, ph, name, list(shape), dt, n)
        gall = ph.enter_context(nc.sbuf_tensor("s_gc", [128, 2, 4, NCH], F32))
        em.dma("sp", gall[:], gcD.rearrange("d c p n -> p d c n"), reads=[gcDb], writes=[b_c])
        PS_full, PB_full = PS, PB

        class BS:
            pass
        sets = {}
        for d in range(2):
            for par in range(2):
                s_ = BS()
                nm = "_%d_%d" % (d, par)
                s_.ops = [R_("s_op%d%s" % (k, nm), 1, (128, 4, 2, 64)) for k in range(5)]
                for r_ in s_.ops:
                    em.op("pool", lambda e, t_=r_.t[0]: e.memset(t_[:], 0.0), writes=[r_.b[0]])
                s_.trs = [R_("s_tr%d%s" % (k, nm), 1, (128, 4, 128)) for k in range(3)]
                s_.mks = [R_("s_mk%d%s" % (k, nm), 1, (128, 4, 128)) for k in range(5)]
                s_.pw = R_("s_pw" + nm, 4, (128, 4, 128))
                s_.wt = R_("s_wt" + nm, 2, (128, 4, 128))
                s_.x1 = R_("s_x1" + nm, 1, (128, 4, 128))
                s_.u = R_("s_u" + nm, 1, (128, 4, 128))
                s_.y = R_("s_y" + nm, 1, (128, 4, 128), dt=F32)
                s_.hn = R_("s_hn" + nm, 1, (128, 4, 128), dt=F32)
                sets[(d, par)] = s_
        H4 = [ph.enter_context(nc.sbuf_tensor("s_H4_%d" % d, [128, 4, 128], F32)) for d in range(2)]
        Hh4 = [ph.enter_context(nc.sbuf_tensor("s_Hh4_%d" % d, [128, 4, 128], BF16)) for d in range(2)]
        H4b = [Buf("H4_0"), Buf("H4_1")]
        for d in range(2):
            em.op("pool", lambda e, d=d: e.memset(H4[d][:], 0.0), writes=[H4b[d]])
            em.op("pool", lambda e, d=d: e.memset(Hh4[d][:], 0.0), writes=[H4b[d]])
        seqs = []
        for d in range(2):
            cc_ = [(TL + j * CH, False) for j in range(TC // CH)]
            ll_ = [(j * CH, True) for j in range(TL // CH)]
            seqs.append(cc_ + ll_ if d == 0 else cc_[::-1] + ll_[::-1])

        def f2(t4, c):
            return t4[:, c].rearrange("p a i -> p (a i)")

        def fl(t3):
            return t3[:].rearrange("p c i -> p (c i)")

        def v3(ap2):
            return ap2.rearrange("p (c i) -> p c i", c=4)

        def bc(ap2):
            return ap2[:, None, :].broadcast_to([128, 4, 128])

        def indep(u):
            d, o, islat, S = u
            st = {}
            ops = []
            for k in range(5):
                t4, tb = S.ops[k].next()
                for hh in range(2):
                    if k == 4:
                        src_ = vcD[:, hh * 64:(hh + 1) * 64, o:o + CH]
                    else:
                        src_ = opD[d, k, :, hh * 64:(hh + 1) * 64, o:o + CH]
                    em.dma("sp", t4[hh * 64:(hh + 1) * 64, :, hh, :], src_.rearrange("c p t -> p c t"),
                           reads=[vcDb if k == 4 else opDb], writes=[tb])
                ops.append((t4, tb))
            st["ops"] = ops
            (At, Atb), (Rt, Rtb), (Bt, Btb), (Kt, Ktb), (Vt, Vtb) = ops
            yield
            trs = []
            for k, (src_, sbuf_) in enumerate(((Bt, Btb), (Kt, Ktb), (Vt, Vtb))):
                pt, pb = PB_full.next()
                for c in range(4):
                    em.op("pe", lambda e, pt=pt, src_=src_, c=c: e.transpose(out=pt[:, c * 128:(c + 1) * 128], in_=f2(src_, c),
                                                                            identity=idb[:, :]), reads=[sbuf_, b_c], writes=[pb])
                t_, tb_ = S.trs[k].next()
                em.op("dve", lambda e, t_=t_, pt=pt: e.tensor_copy(out=fl(t_), in_=pt[:, 0:512]), reads=[pb], writes=[tb_])
                trs.append((t_, tb_))
                yield
            st["trs"] = trs
            ms, mi_, mL = (0, 1, 2) if d == 0 else (2, 3, 0)
            specs = ((Kt, Ktb, At, Atb, ms), (Bt, Btb, At, Atb, ms), (At, Atb, Bt, Btb, mL), (Bt, Btb, Rt, Rtb, mi_), (Kt, Ktb, Rt, Rtb, mi_))
            mks = []
            for k, (l_, lb_, r_, rb_, mi) in enumerate(specs):
                pt, pb = PS_full.next()
                for c in range(4):
                    mm(em, pt[:, c * 128:(c + 1) * 128], f2(l_, c), f2(r_, c), True, True, [lb_, rb_], [pb])
                t_, tb_ = S.mks[k].next()
                tt(em, t_[:], v3(pt[:, :]), bc(mkt[:, mi, :]), ALU.mult, [pb, b_c], [tb_])
                mks.append((t_, tb_))
                yield
            st["mks"] = mks
            (Np, Npb), (Lp, Lpb) = mks[1], mks[2]
            WT, WTb = S.wt.next()
            tt(em, WT[:], Np[:], bc(idf[:, :]), ALU.add, [Npb, b_c], [WTb], eng="pool")
            for lvl in range(5):
                pt, pb = PS_full.next()
                for c in range(4):
                    mm(em, pt[:, c * 128:(c + 1) * 128], Np[:, c, :], Lp[:, c, :], True, True, [Npb, Lpb], [pb])
                L2, L2b = S.pw.next()
                act(em, fl(L2), pt[:, :], AF.Copy, [pb], [L2b])
                if lvl < 4:
                    pt, pb = PS_full.next()
                    for c in range(4):
                        mm(em, pt[:, c * 128:(c + 1) * 128], Lp[:, c, :], Np[:, c, :], True, True, [Npb, Lpb], [pb])
                    N2, N2b = S.pw.next()
                    em.op("dve", lambda e, N2=N2, pt=pt: e.tensor_copy(out=fl(N2), in_=pt[:, :]), reads=[pb], writes=[N2b])
                yield
                pt, pb = PS_full.next()
                for c in range(4):
                    mm(em, pt[:, c * 128:(c + 1) * 128], L2[:, c, :], WT[:, c, :], True, True, [L2b, WTb], [pb])
                WTn, WTnb = S.wt.next()
                tt(em, fl(WTn), pt[:, :], fl(WT), ALU.add, [pb, WTb], [WTnb])
                WT, WTb = WTn, WTnb
                if lvl < 4:
                    Np, Npb, Lp, Lpb = N2, N2b, L2, L2b
                yield
            st["WT"] = (WT, WTb)
            u_state[u[:3]] = st

        def dep(u):
            d, o, islat, S = u
            st = u_state.pop(u[:3])
            (At, Atb), (Rt, Rtb), (Bt, Btb), (Kt, Ktb), (Vt, Vtb) = st["ops"]
            (Bm, Bmb), (Km, Kmb), (Vm, Vmb) = st["trs"]
            (Lak, Lakb), _n, _l, (Trb, Trbb), (Trk, Trkb) = st["mks"]
            WT, WTb = st["WT"]
            H, Hh, Hbuf = H4[d], Hh4[d], H4b[d]
            pt, pb = PS_full.next()
            for c in range(4):
                mm(em, pt[:, c * 128:(c + 1) * 128], f2(At, c), Hh[:, c, :], True, False, [Atb, Hbuf], [pb])
                mm(em, pt[:, c * 128:(c + 1) * 128], Lak[:, c, :], Vm[:, c, :], False, True, [Lakb, Vmb], [pb])
            X1, X1b = S.x1.next()
            act(em, fl(X1), pt[:, :], AF.Copy, [pb], [X1b])
            yield
            pt, pb = PS_full.next()
            for c in range(4):
                mm(em, pt[:, c * 128:(c + 1) * 128], WT[:, c, :], X1[:, c, :], True, True, [WTb, X1b], [pb])
            U, Ub = S.u.next()
            em.op("dve", lambda e: e.tensor_copy(out=fl(U), in_=pt[:, :]), reads=[pb], writes=[Ub])
            yield
            if islat:
                pt, pb = PS_full.next()
                for c in range(4):
                    mm(em, pt[:, c * 128:(c + 1) * 128], f2(Rt, c), Hh[:, c, :], True, False, [Rtb, Hbuf], [pb])
                    mm(em, pt[:, c * 128:(c + 1) * 128], Trb[:, c, :], U[:, c, :], False, False, [Trbb, Ub], [pb])
                    mm(em, pt[:, c * 128:(c + 1) * 128], Trk[:, c, :], Vm[:, c, :], False, True, [Trkb, Vmb], [pb])
                Y, Yb = S.y.next()
                act(em, fl(Y), pt[:, :], AF.Copy, [pb], [Yb])
                for hh in range(2):
                    em.dma("sp", yD[d, o:o + CH, hh:8:2, :], Y[hh * 64:(hh + 1) * 64, :, hh * 64:(hh + 1) * 64],
                           reads=[Yb], writes=[yDb[d]])
            pt, pb = PS_full.next()
            for c in range(4):
                mm(em, pt[:, c * 128:(c + 1) * 128], Bm[:, c, :], U[:, c, :], True, False, [Bmb, Ub], [pb])
                mm(em, pt[:, c * 128:(c + 1) * 128], Km[:, c, :], Vm[:, c, :], False, True, [Kmb, Vmb], [pb])
            Hn, Hnb = S.hn.next()
            tt(em, fl(Hn), pt[:, :], fl(H), ALU.add, [pb, Hbuf], [Hnb])
            ci = o // CH
            gam = gall[:, d, :, ci:ci + 1].broadcast_to([128, 4, 128])
            tt(em, H[:], Hn[:], gam, ALU.mult, [Hnb, b_c], [Hbuf])
            tt(em, Hh[:], Hn[:], gam, ALU.mult, [Hnb, b_c], [Hbuf], eng="pool")
            yield

        def rr(gens):
            gens = list(gens)
            while gens:
                nxt = []
                for g_ in gens:
                    if next(g_, "done") != "done":
                        nxt.append(g_)
                gens = nxt

        steps = [[(d,) + seqs[d][i] + (sets[(d, i & 1)],) for d in range(2)] for i in range(NCH)]
        u_state = {}
        rr([indep(u) for u in steps[0]])
        for si in range(NCH):
            gens = [dep(u) for u in steps[si]]
            if si + 1 < NCH:
                gens += [indep(u) for u in steps[si + 1]]
            rr(gens)
        em.barrier()
    with ExitStack() as ph:
        def R_(name, shape, dt, n):
            return Ring(nc, ph, name, shape, dt, n)
        yfr = R_("o_yf", [128, 8, 64], F32, 2)
        yrr = R_("o_yr", [128, 8, 64], F32, 2)
        ycr = R_("o_yc", [128, 8, 64], F32, 2)
        sqr2 = R_("o_sq", [128, 8, 64], F32, 2)
        str_ = R_("o_st", [128, 2, 8], F32, 2)
        bvr = R_("o_bv", [128, 4, 128], F32, 2)
        gbr = R_("o_gb", [128, 4, 128], BF16, 2)
        obr = R_("o_ob", [128, 4, 128], BF16, 2)
        for n in range(TL // 128):
            t0 = n * 128
            yf, yfb = yfr.next()
            yr2, yrb = yrr.next()
            em.dma("sp", yf[:], yD[0, t0:t0 + 128, :, :], reads=[yDb[0]], writes=[yfb])
            em.dma("act", yr2[:], yD[1, t0:t0 + 128, :, :], reads=[yDb[1]], writes=[yrb])
            tt(em, yf[:], yf[:], yr2[:], ALU.add, [yfb, yrb], [yfb], eng="pool")
            st_, stb = str_.next()
            em.op("dve", lambda e: e.tensor_reduce(out=st_[:, 0, :], in_=yf[:], axis=AX.X, op=ALU.add), reads=[yfb], writes=[stb])
            ts(em, st_[:, 0, :], st_[:, 0, :], 1.0 / 64, None, ALU.mult, None, [stb], [stb])
            yc, ycb = ycr.next()
            tt(em, yc[:], yf[:], st_[:, 0, :][:, :, None].broadcast_to([128, 8, 64]), ALU.subtract, [yfb, stb], [ycb])
            sq, sqb = sqr2.next()
            act(em, sq[:], yc[:], AF.Square, [ycb], [sqb])
            em.op("dve", lambda e: e.tensor_reduce(out=st_[:, 1, :], in_=sq[:], axis=AX.X, op=ALU.add), reads=[sqb], writes=[stb])
            act(em, st_[:, 1, :], st_[:, 1, :], AF.Sqrt, [stb], [stb], scale=1.0 / 64, bias=GN_EPS)
            em.op("dve", lambda e: e.reciprocal(out=st_[:, 1, :], in_=st_[:, 1, :]), reads=[stb], writes=[stb])
            tt(em, yc[:], yc[:], st_[:, 1, :][:, :, None].broadcast_to([128, 8, 64]), ALU.mult, [ycb, stb], [ycb])
            yc2 = yc[:].rearrange("p h v -> p (h v)")
            tt(em, yc2, yc2, gnt[:, 0, :], ALU.mult, [ycb, b_c], [ycb], eng="pool")
            tt(em, yc2, yc2, gnt[:, 1, :], ALU.add, [ycb, b_c], [ycb], eng="pool")
            pt, pb = PS_full.next()
            for c in range(4):
                em.op("pe", lambda e, c=c: e.transpose(out=pt[:, c * 128:(c + 1) * 128], in_=yc2[:, c * 128:(c + 1) * 128],
                                                       identity=idf[:, :]), reads=[ycb, b_c], writes=[pb])
            bv, bvb = bvr.next()
            gb_, gbb = gbr.next()
            em.dma("sp", bv[:], bvD[:, :, t0:t0 + 128].rearrange("c p t -> p c t"), reads=[bvDb], writes=[bvb])
            em.dma("act", gb_[:], gbD[:, :, t0:t0 + 128].rearrange("c p t -> p c t"), reads=[gbDb], writes=[gbb])
            tt(em, bv[:].rearrange("p c t -> p (c t)"), pt[:, :], bv[:].rearrange("p c t -> p (c t)"), ALU.add, [pb, bvb], [bvb])
            ob_, obb = obr.next()
            tt(em, ob_[:], bv[:], gb_[:], ALU.mult, [bvb, gbb], [obb], eng="pool")
            em.dma("pool", oT[:, 4:8, t0:t0 + 128], ob_[:], reads=[obb], writes=[b_out])
        em.barrier()


A_Q, A_KV, B_W = 2048, 512, 2048
OFF = {"q": 0, "k": 2048, "v": 2560, "ga": 3072, "r": 5120, "kb": 7168, "vb": 9216, "gb": 11264, "lw": 13312, "la": 13504}
SWAP = np.concatenate([np.arange(32, 64), np.arange(0, 32), np.arange(96, 128), np.arange(64, 96)])


def l2_wcols(w_in, g):
    cols = []
    for h in range(4):
        cols.append(np.arange(128) + OFF["q"] + (4 * g + h) * 128)
    for h in range(4):
        cols.append(SWAP + OFF["q"] + (4 * g + h) * 128)
    cols.append(np.arange(128) + OFF["k"] + g * 128)
    cols.append(SWAP + OFF["k"] + g * 128)
    cols.append(np.arange(128) + OFF["v"] + g * 128)
    for h in range(4):
        cols.append(np.arange(128) + OFF["ga"] + (4 * g + h) * 128)
    for nm in ("r", "kb", "vb", "gb"):
        for c in range(4):
            cols.append(np.arange(128) + OFF[nm] + g * 512 + c * 128)
    for nm in ("lw", "la"):
        for d in range(2):
            cols.append(np.arange(96) + OFF[nm] + d * 96)
    return np.ascontiguousarray(w_in[:, np.concatenate(cols)])


def rope_tables(TL):
    t = np.arange(TL)
    row = (t // 64).astype(np.float32)
    col = (t % 64).astype(np.float32)
    inv = (np.float32(10000.0) ** (-np.arange(32, dtype=np.float32) / np.float32(32))).astype(np.float32)
    cosT = np.zeros((128, TL), np.float32)
    sinT = np.zeros((128, TL), np.float32)
    for p in range(128):
        idx = p % 64
        ang = (row if p < 64 else col) * inv[idx % 32]
        cosT[p] = np.cos(ang)
        sinT[p] = np.sin(ang) * (-1.0 if idx < 32 else 1.0)
    return cosT, sinT


def attn_masks():
    qi = np.arange(128)[:, None]
    j = np.arange(128)[None, :]
    lo = np.where(j >= qi, 0.0, NEG).astype(np.float32)
    hi = np.where(j <= qi, 0.0, NEG).astype(np.float32)
    z = np.zeros((128, 128), np.float32)
    ng = np.full((128, 128), NEG, np.float32)
    m = np.stack([np.concatenate([ng, z, hi], 1), np.concatenate([lo, z, hi], 1), np.concatenate([lo, z, ng], 1)], axis=1)
    return np.ascontiguousarray(m)


def rwkv_masks():
    i = np.arange(64)[:, None]
    j = np.arange(64)[None, :]
    pats = [(i < j), (i <= j), (i > j), (i >= j)]
    out = np.zeros((128, 4, 128), np.float32)
    for k, p in enumerate(pats):
        out[:, k, :] = np.tile(p.astype(np.float32), (2, 2))
    return out


def l2_rwkv_inputs(g, conv, w0, w2, a0, a2, k_k, k_a, r_k, gn_w, gn_b):
    sl = slice(g * 512, (g + 1) * 512)
    pc = lambda v: np.ascontiguousarray(np.asarray(v[sl], np.float32).reshape(4, 128).T)
    cv = np.zeros((128, 3, 3, 4), np.float32)
    for kind in range(3):
        for tap in range(3):
            cv[:, kind, tap, :] = pc(conv[tap, kind * 2048:(kind + 1) * 2048])
    pv = np.zeros((128, 9, 4), np.float32)
    for i, v in enumerate((w0[0], w0[1], a0[0], a0[1], k_k, k_a, r_k)):
        pv[:, i, :] = pc(v)
    blk1 = np.zeros((128, 128), np.float32)
    blk1[:64, :64] = 1
    blk1[64:, 64:] = 1
    return {
        "cv": cv, "pv": pv,
        "w2c": np.ascontiguousarray(np.transpose(w2[:, :, sl], (1, 0, 2))),
        "a2c": np.ascontiguousarray(np.transpose(a2[:, :, sl], (1, 0, 2))),
        "gnb": np.ascontiguousarray(np.broadcast_to(np.stack([gn_w[sl], gn_b[sl]])[None], (128, 2, 512))).astype(np.float32),
        "rmask": rwkv_masks(), "blk1": blk1,
    }


BATCH, SEQ, CTX = 2, 8192, 256
_BF16_NP = mybir.dt.np(BF16)


def kernel(x, c, ctx, c_ctx, mod_w, mod_b, norm_g, ab_w_in, ab_w_out, attn_sink, rwkv_conv, rwkv_w0, rwkv_w2,
           rwkv_a0, rwkv_a2, rwkv_k_k, rwkv_k_a, rwkv_r_k, rwkv_gn_w, rwkv_gn_b, gm_w_in, gm_ln_g, gm_ln_b,
           gm_w_s, gm_b_s, gm_w_out, final_g):
    f32 = lambda a: np.asarray(a, dtype=np.float32)
    x, c, ctx, c_ctx = f32(x), f32(c), f32(ctx), f32(c_ctx)
    mod = run_mod(c, c_ctx, f32(mod_w), f32(mod_b))
    shift = mod[:, :, 0:D]
    scale = mod[:, :, D:2 * D]
    gate = mod[:, :, 2 * D:3 * D]
    w_in = f32(ab_w_in[0])
    cosT, sinT = rope_tables(SEQ)
    ident = np.eye(128, dtype=np.float32)
    am = attn_masks()
    xTb = [fmT(x[b]) for b in range(BATCH)]
    cTb = [fmT(ctx[b]) for b in range(BATCH)]
    wcs = [l2_wcols(w_in, g) for g in range(4)]
    rw = [l2_rwkv_inputs(g, f32(rwkv_conv[0]), f32(rwkv_w0[0]), f32(rwkv_w2[0]), f32(rwkv_a0[0]), f32(rwkv_a2[0]),
                         f32(rwkv_k_k[0]), f32(rwkv_k_a[0]), f32(rwkv_r_k[0]), f32(rwkv_gn_w[0]), f32(rwkv_gn_b[0]))
          for g in range(4)]
    in_maps = []
    for i in range(NCORES):
        b, g = i // 4, i % 4
        vec = np.stack([fm(v, KC) for v in (norm_g[0], scale[0, b], shift[0, b], scale[0, 2], shift[0, 2], shift[0, 2])], axis=1)
        m = {"xT": xTb[b], "cT": cTb[b], "wcols": wcs[g], "vec": np.ascontiguousarray(vec),
             "sinkb": np.ascontiguousarray(np.broadcast_to(f32(attn_sink[0])[None, 4 * g:4 * g + 4], (128, 4))),
             "cosT": cosT, "sinT": sinT, "ident": ident, "amask": am}
        m.update(rw[g])
        in_maps.append(m)
    nc2 = build_l2(SEQ, CTX, do_b=True)
    res2 = run_bass_kernel_spmd(nc2, in_maps, core_ids=list(range(NCORES)))
    del in_maps, xTb, cTb, wcs
    oTf = np.zeros((BATCH, 128, KC, SEQ), dtype=_BF16_NP)
    for i in range(NCORES):
        b, g = i // 4, i % 4
        o = res2.results[i]["oT"]
        oTf[b, :, g * 4:(g + 1) * 4, :] = o[:, 0:4, :]
        oTf[b, :, 16 + g * 4:16 + (g + 1) * 4, :] = o[:, 4:8, :]
    del res2
    TCORE = BATCH * SEQ // NCORES
    w_out, gw_in, gw_out = f32(ab_w_out[0]), f32(gm_w_in[0]), f32(gm_w_out[0])
    in_maps = []
    for i in range(NCORES):
        b, s = i // 4, (i % 4) * TCORE
        in_maps.append(l3_core_inputs(x[b, s:s + TCORE], np.ascontiguousarray(oTf[b, :, :, s:s + TCORE]), w_out, gw_in, gw_out,
                                      gate[0, b], norm_g[1], scale[1, b], shift[1, b], gate[1, b], final_g,
                                      f32(gm_ln_g[0]), f32(gm_ln_b[0]), f32(gm_w_s[0]), f32(gm_b_s[0])))
    nc3 = build_l3(TCORE, 16, 1024)
    res3 = run_bass_kernel_spmd(nc3, in_maps, core_ids=list(range(NCORES)))
    del in_maps
    out = np.empty((BATCH, SEQ, D), np.float32)
    for i in range(NCORES):
        b, s = i // 4, (i % 4) * TCORE
        out[b, s:s + TCORE] = res3.results[i]["outT"].transpose(2, 1, 0).reshape(TCORE, D)
    return out
```

```python
import numpy as np
from contextlib import ExitStack
import concourse.bass as bass
import concourse.mybir as mybir
from concourse.bass_utils import run_bass_kernel_spmd

F32 = mybir.dt.float32
BF16 = mybir.dt.bfloat16
AF = mybir.ActivationFunctionType
ALU = mybir.AluOpType
AX = mybir.AxisListType

D = 4096
KC = D // 128
NCORES = 8


class Buf:
    __slots__ = ("w", "r", "name")

    def __init__(self, name=""):
        self.w = {}
        self.r = {}
        self.name = name


class Emit:
    NSLOT = 12

    def __init__(self, nc, es):
        self.nc = nc
        self.engs = {"pe": nc.tensor, "act": nc.scalar, "dve": nc.vector, "pool": nc.gpsimd, "sp": nc.sync}
        self.sem = {}
        self.cnt = {}
        for e in ("pe", "act", "dve", "pool"):
            self.sem[e] = es.enter_context(nc.semaphore("s_" + e))
            self.cnt[e] = 0
        self.qn = {}
        for q in ("sp", "act", "pool"):
            for i in range(self.NSLOT):
                k = "d_%s_%d" % (q, i)
                self.sem[k] = es.enter_context(nc.semaphore(k))
                self.cnt[k] = 0
            self.qn[q] = 0
        self.waited = {e: {} for e in self.engs}

    def _wait(self, eng, deps):
        for k, v in deps.items():
            if v <= 0:
                continue
            if k == "pe" and eng == "pe":
                continue
            if self.waited[eng].get(k, 0) >= v:
                continue
            self.engs[eng].wait_ge(self.sem[k], v)
            self.waited[eng][k] = v

    @staticmethod
    def _merge(dst, src):
        for k, v in src.items():
            if dst.get(k, 0) < v:
                dst[k] = v

    def _deps(self, reads, writes):
        deps = {}
        for b in reads:
            self._merge(deps, b.w)
        for b in writes:
            self._merge(deps, b.w)
            self._merge(deps, b.r)
        return deps

    def _commit(self, key, val, reads, writes):
        for b in reads:
            if b.r.get(key, 0) < val:
                b.r[key] = val
        for b in writes:
            if b.r:
                b.w = {key: val}
                b.r = {}
            else:
                b.w[key] = val

    def op(self, eng, fn, reads=(), writes=()):
        self._wait(eng, self._deps(reads, writes))
        ins = fn(self.engs[eng])
        self.cnt[eng] += 1
        ins.then_inc(self.sem[eng], 1)
        self._commit(eng, self.cnt[eng], reads, writes)

    def dma(self, q, out, in_, reads=(), writes=(), **kw):
        slot = self.qn[q] % self.NSLOT
        self.qn[q] += 1
        key = "d_%s_%d" % (q, slot)
        deps = self._deps(reads, writes)
        if self.cnt[key] > 0:
            deps[key] = max(deps.get(key, 0), self.cnt[key])
        self._wait(q, deps)
        ins = self.engs[q].dma_start(out=out, in_=in_, **kw)
        self.cnt[key] += 16
        ins.then_inc(self.sem[key], 16)
        self._commit(key, self.cnt[key], reads, writes)

    def barrier(self):
        allc = {k: v for k, v in self.cnt.items() if v > 0}
        for e in self.engs:
            self._wait(e, dict(allc))

    def finish(self, bufs):
        deps = {}
        for b in bufs:
            self._merge(deps, b.w)
        self._wait("sp", deps)


def _new_nc():
    return bass.Bass("TRN2", target_bir_lowering=False)


MODC = 3 * D // NCORES


def build_mod():
    nc = _new_nc()
    condT = nc.dram_tensor("condT", [128, KC, 3], F32, kind="ExternalInput").ap()
    mw = nc.dram_tensor("mw", [2, D, MODC], F32, kind="ExternalInput").ap()
    mb = nc.dram_tensor("mb", [2, 3, MODC], F32, kind="ExternalInput").ap()
    out = nc.dram_tensor("out", [2, 3, MODC], F32, kind="ExternalOutput").ap()
    with ExitStack() as es:
        em = Emit(nc, es)
        sc = es.enter_context(nc.sbuf_tensor("sc", [128, KC, 3], F32))
        wt = [es.enter_context(nc.sbuf_tensor("wt%d" % i, [128, KC, 512], F32)) for i in range(2)]
        bt = es.enter_context(nc.sbuf_tensor("bt", [3, 2, MODC], F32))
        ot = es.enter_context(nc.sbuf_tensor("ot", [3, 2, MODC], F32))
        ps = [es.enter_context(nc.psum_tensor("ps%d" % i, [128, 512], F32)) for i in range(2)]
        b_sc, b_bt, b_ot = Buf("sc"), Buf("bt"), Buf("ot")
        b_wt = [Buf("wt0"), Buf("wt1")]
        b_ps = [Buf("ps0"), Buf("ps1")]
        em.dma("sp", sc[:], condT[:, :, :], writes=[b_sc])
        em.dma("sp", bt[:], mb.rearrange("l r n -> r l n"), writes=[b_bt])
        em.op("act", lambda e: e.activation(out=sc[:], in_=sc[:], func=AF.Silu), reads=[b_sc], writes=[b_sc])
        it = 0
        for l in range(2):
            for nt in range(MODC // 512):
                j = it % 2
                it += 1
                em.dma("sp" if j == 0 else "pool", wt[j][:],
                       mw[l, :, nt * 512:(nt + 1) * 512].rearrange("(k p) n -> p k n", p=128), writes=[b_wt[j]])
                for k in range(KC):
                    em.op("pe", lambda e, k=k, j=j: e.matmul(ps[j][0:3, :], lhsT=sc[:, k, :], rhs=wt[j][:, k, :],
                                                              start=(k == 0), stop=(k == KC - 1)),
                          reads=[b_sc, b_wt[j]], writes=[b_ps[j]])
                em.op("dve", lambda e, j=j, l=l, nt=nt: e.tensor_tensor(
                    out=ot[:, l, nt * 512:(nt + 1) * 512], in0=ps[j][0:3, :], in1=bt[:, l, nt * 512:(nt + 1) * 512],
                    op=ALU.add), reads=[b_ps[j], b_bt], writes=[b_ot])
        em.dma("sp", out.rearrange("l r n -> r l n"), ot[:], reads=[b_ot], writes=[b_ot])
        em.finish([b_ot])
    return nc


def run_mod(c, c_ctx, mod_w, mod_b, ncores=NCORES):
    cond = np.concatenate([c, c_ctx[None, :]], axis=0).astype(np.float32)
    condT = np.ascontiguousarray(cond.reshape(3, KC, 128).transpose(2, 1, 0))
    in_maps = []
    for i in range(ncores):
        sl = slice(i * MODC, (i + 1) * MODC)
        in_maps.append({
            "condT": condT,
            "mw": np.ascontiguousarray(mod_w[:, :, sl]),
            "mb": np.ascontiguousarray(np.broadcast_to(mod_b[:, None, sl], (2, 3, MODC))),
        })
    nc = build_mod()
    res = run_bass_kernel_spmd(nc, in_maps, core_ids=list(range(ncores)))
    return np.concatenate([r["out"] for r in res.results], axis=2)


_UID = [0]


def _u(name):
    _UID[0] += 1
    return "%s_u%d" % (name, _UID[0])


class Ring:
    def __init__(self, nc, es, name, shape, dtype, n, psum=False):
        alloc = nc.psum_tensor if psum else nc.sbuf_tensor
        name = _u(name)
        self.t = [es.enter_context(alloc("%s_%d" % (name, i), shape, dtype)) for i in range(n)]
        self.b = [Buf("%s%d" % (name, i)) for i in range(n)]
        self.i = 0

    def next(self):
        j = self.i % len(self.t)
        self.i += 1
        return self.t[j], self.b[j]


def mm(em, out, lhsT, rhs, start, stop, reads, writes):
    em.op("pe", lambda e: e.matmul(out, lhsT=lhsT, rhs=rhs, start=start, stop=stop), reads=reads, writes=writes)


def act(em, out, in_, func, reads, writes, scale=None, bias=None, accum_out=None):
    kw = {}
    if scale is not None:
        kw["scale"] = scale
    if bias is not None:
        kw["bias"] = bias
    if accum_out is not None:
        kw["accum_out"] = accum_out
    em.op("act", lambda e: e.activation(out=out, in_=in_, func=func, **kw), reads=reads, writes=writes)


def tt(em, out, in0, in1, op, reads, writes, eng="dve"):
    em.op(eng, lambda e: e.tensor_tensor(out=out, in0=in0, in1=in1, op=op), reads=reads, writes=writes)


def stt(em, out, in0, scalar, in1, op0, op1, reads, writes):
    em.op("dve", lambda e: e.scalar_tensor_tensor(out=out, in0=in0, scalar=scalar, in1=in1, op0=op0, op1=op1),
          reads=reads, writes=writes)


def ts(em, out, in0, s1, s2, op0, op1, reads, writes, eng="dve"):
    if s2 is None:
        em.op(eng, lambda e: e.tensor_scalar(out=out, in0=in0, scalar1=s1, scalar2=None, op0=op0),
              reads=reads, writes=writes)
    else:
        em.op(eng, lambda e: e.tensor_scalar(out=out, in0=in0, scalar1=s1, scalar2=s2, op0=op0, op1=op1),
              reads=reads, writes=writes)


def cast(em, out, in_, reads, writes, eng="pool"):
    em.op(eng, lambda e: e.tensor_copy(out=out, in_=in_), reads=reads, writes=writes)


GRP_W = 768
NORM_EPS = 1e-6
LN_EPS = 1e-5


def build_l3(T, NG, TP):
    CW = NG * GRP_W
    NCC = CW // 128
    NCT = CW // 512
    NTB = TP // 128
    NTS = TP // 512
    NPASS = T // TP
    TS = 256
    NS = TP // TS
    nc = _new_nc()
    xT = nc.dram_tensor("xT", [128, KC, T], F32, kind="ExternalInput").ap()
    oT = nc.dram_tensor("oT", [128, KC, T], BF16, kind="ExternalInput").ap()
    w_out = nc.dram_tensor("w_out", [D, D], F32, kind="ExternalInput").ap()
    gw_in = nc.dram_tensor("gw_in", [D, 3 * CW], F32, kind="ExternalInput").ap()
    gw_out = nc.dram_tensor("gw_out", [CW, D], F32, kind="ExternalInput").ap()
    vec32 = nc.dram_tensor("vec32", [128, 6, KC], F32, kind="ExternalInput").ap()
    lnv = nc.dram_tensor("lnv", [128, 2, NCC], F32, kind="ExternalInput").ap()
    wsT = nc.dram_tensor("wsT", [128, NG, 128], F32, kind="ExternalInput").ap()
    bsb = nc.dram_tensor("bsb", [128, NG, 128], F32, kind="ExternalInput").ap()
    outT = nc.dram_tensor("outT", [128, KC, T], F32, kind="ExternalOutput").ap()
    xnewD = nc.dram_tensor("xnewD", [128, KC, T], F32).ap()
    xfinD = nc.dram_tensor("xfinD", [128, KC, TP], F32).ap()
    gvD = nc.dram_tensor("gvD", [TP, CW], BF16).ap()
    yD = nc.dram_tensor("yD", [NCC, 128, TP], BF16).ap()
    with ExitStack() as es:
        em = Emit(nc, es)
        sb = lambda name, shape, dt: es.enter_context(nc.sbuf_tensor(name, shape, dt))
        PS = Ring(nc, es, "ps", [128, 512], F32, 8, psum=True)
        v32 = sb("v32", [128, 6, KC], F32)
        lnt = sb("lnt", [128, 2, NCC], F32)
        gs1 = sb("gs1", [128, KC], F32)
        ones = sb("ones", [128, 128], F32)
        ones16 = sb("ones16", [128, 128], BF16)
        wsf = sb("wsf", [128, NG, 128], F32)
        wsb = sb("wsb", [128, NG, 128], BF16)
        rsb = sb("rsb", [128, NG, 128], F32)
        bsbt = sb("bsbt", [128, NG, 128], F32)
        stat = sb("stat", [128, 4, NTB], F32)
        h1T = sb("h1T", [128, KC, TP], BF16)
        b_c, b_out, b_st = Buf("consts"), Buf("out"), Buf("stats")
        em.dma("sp", v32[:], vec32[:, :, :], writes=[b_c])
        em.dma("sp", lnt[:], lnv[:, :, :], writes=[b_c])
        em.dma("sp", wsf[:], wsT[:, :, :], writes=[b_c])
        em.dma("sp", bsbt[:], bsb[:, :, :], writes=[b_c])
        em.op("dve", lambda e: e.memset(ones[:], 1.0), writes=[b_c])
        em.op("dve", lambda e: e.memset(ones16[:], 1.0), writes=[b_c])
        ts(em, gs1[:], v32[:, 2, :], 1.0, None, ALU.add, None, [b_c], [b_c])
        tt(em, gs1[:], gs1[:], v32[:, 1, :], ALU.mult, [b_c], [b_c])
        cast(em, wsb[:], wsf[:], [b_c], [b_c], eng="dve")
        wsf2 = wsf[:].rearrange("p g i -> p (g i)")
        rsb2 = rsb[:].rearrange("p g i -> p (g i)")
        for q in range((NG * 128 + 511) // 512):
            c0, c1 = q * 512, min(NG * 128, (q + 1) * 512)
            pt, pb = PS.next()
            mm(em, pt[:, 0:c1 - c0], ones[:, :], wsf2[:, c0:c1], True, True, [b_c], [pb])
            cast(em, rsb2[:, c0:c1], pt[:, 0:c1 - c0], [pb], [b_c], eng="dve")
        gate0, shift1, gate1, fing = v32[:, 0, :], v32[:, 3, :], v32[:, 4, :], v32[:, 5, :]
        h1b = [[Buf("h1_%d_%d" % (s, n)) for n in range(KC)] for s in range(NS)]
        xnew_b = [Buf("xnewD%d" % p) for p in range(NPASS)]
        gvb = [Buf("gvD%d" % tb) for tb in range(NTB)]
        yb = [Buf("yD%d" % cc) for cc in range(NCC)]
        xfb = [Buf("xfin%d" % n) for n in range(KC)]

        def rmsnorm(sqr, tmpr, rs_t, rs_b, xt, xb, emit_chunk):
            pt, pb = PS.next()
            for n in range(KC):
                sq, sqb = sqr.next()
                act(em, sq[:], xt[:, n, :], AF.Square, [xb[n]], [sqb])
                mm(em, pt[:, 0:TS], ones16[:, :], sq[:], n == 0, n == KC - 1, [sqb, b_c], [pb])
            act(em, rs_t[:], pt[:, 0:TS], AF.Sqrt, [pb], [rs_b], scale=1.0 / D, bias=NORM_EPS)
            em.op("dve", lambda e: e.reciprocal(out=rs_t[:], in_=rs_t[:]), reads=[rs_b], writes=[rs_b])
            for n in range(KC):
                tmp, tmpb = tmpr.next()
                tt(em, tmp[:], xt[:, n, :], rs_t[:], ALU.mult, [xb[n], rs_b], [tmpb])
                emit_chunk(n, tmp, tmpb)

        for ps_i in range(NPASS):
            P0 = ps_i * TP
            em.barrier()
            with ExitStack() as ph:
                psb = lambda name, shape, dt: ph.enter_context(nc.sbuf_tensor(_u(name), shape, dt))
                xt = psb("p1_xt", [128, KC, TS], F32)
                ot = psb("p1_ot", [128, KC, TS], BF16)
                wfr = Ring(nc, ph, "p1_wf", [128, KC, 128], F32, 2)
                wbr = Ring(nc, ph, "p1_wb", [128, KC, 128], BF16, 2)
                sqr = Ring(nc, ph, "p1_sq", [128, TS], BF16, 3)
                tmpr = Ring(nc, ph, "p1_tmp", [128, TS], F32, 2)
                rs_t = psb("p1_rs", [128, TS], F32)
                rs_b = Buf("rs")
                xb = [Buf("xt%d" % n) for n in range(KC)]
                ob = Buf("ot")
                for s in range(NS):
                    t0 = P0 + s * TS
                    em.dma("sp", xt[:], xT[:, :, t0:t0 + TS], writes=xb)
                    em.dma("sp", ot[:], oT[:, :, t0:t0 + TS], writes=[ob])
                    for n in range(KC):
                        wf, wfb = wfr.next()
                        em.dma("sp", wf[:],
                               w_out[:, n * 128:(n + 1) * 128].rearrange("(k p) n -> p k n", p=128), writes=[wfb])
                        wb, wbb = wbr.next()
                        cast(em, wb[:], wf[:], [wfb], [wbb], eng="dve")
                        pt, pb = PS.next()
                        for k in range(KC):
                            mm(em, pt[:, 0:TS], wb[:, k, :], ot[:, k, :], k == 0, k == KC - 1, [wbb, ob], [pb])
                        stt(em, xt[:, n, :], pt[:, 0:TS], gate0[:, n:n + 1], xt[:, n, :], ALU.mult, ALU.add,
                            [pb, b_c, xb[n]], [xb[n]])
                    em.dma("pool", xnewD[:, :, t0:t0 + TS], xt[:], reads=xb, writes=[xnew_b[ps_i]])

                    def emit_h1(n, tmp, tmpb, s=s):
                        act(em, h1T[:, n, s * TS:(s + 1) * TS], tmp[:], AF.Identity, [tmpb, b_c], [h1b[s][n]],
                            scale=gs1[:, n:n + 1], bias=shift1[:, n:n + 1])
                    rmsnorm(sqr, tmpr, rs_t, rs_b, xt, xb, emit_h1)
                em.barrier()
            with ExitStack() as ph:
                psb = lambda name, shape, dt: ph.enter_context(nc.sbuf_tensor(_u(name), shape, dt))
                wfr = Ring(nc, ph, "v_wf", [128, 8, 512], F32, 2)
                wbr = Ring(nc, ph, "v_wb", [128, KC, 512], BF16, 2)
                gvr = Ring(nc, ph, "v_gv", [128, 512], BF16, 3)
                jkr = Ring(nc, ph, "v_jk", [128, 512], BF16, 2)
                ssum = psb("v_ssum", [128, NTB, NCT], F32)
                ssq = psb("v_ssq", [128, NTB, NCT], F32)
                st2 = psb("v_st2", [128, 2, NTB], F32)
                for ct in range(NCT):
                    col0 = CW + ct * 512
                    wb, wbb = wbr.next()
                    for q in range(4):
                        wf, wfb = wfr.next()
                        em.dma("sp", wf[:],
                               gw_in[q * 1024:(q + 1) * 1024, col0:col0 + 512].rearrange("(k p) n -> p k n", p=128),
                               writes=[wfb])
                        cast(em, wb[:, q * 8:(q + 1) * 8, :], wf[:], [wfb], [wbb], eng="dve")
                    for tb in range(NTB):
                        pt, pb = PS.next()
                        s0 = (tb * 128) // TS
                        for k in range(KC):
                            mm(em, pt[:, :], h1T[:, k, tb * 128:(tb + 1) * 128], wb[:, k, :], k == 0, k == KC - 1,
                               [wbb, h1b[s0][k]], [pb])
                        gv, gb = gvr.next()
                        act(em, gv[:], pt[:, :], AF.Gelu_apprx_tanh, [pb], [gb, b_st],
                            accum_out=ssum[:, tb, ct:ct + 1])
                        jk, jb = jkr.next()
                        act(em, jk[:], gv[:], AF.Square, [gb], [jb, b_st], accum_out=ssq[:, tb, ct:ct + 1])
                        em.dma("pool", gvD[tb * 128:(tb + 1) * 128, ct * 512:(ct + 1) * 512], gv[:],
                               reads=[gb], writes=[gvb[tb]])
                em.op("dve", lambda e: e.tensor_reduce(out=stat[:, 0, :], in_=ssum[:], axis=AX.X, op=ALU.add),
                      reads=[b_st], writes=[b_st])
                em.op("dve", lambda e: e.tensor_reduce(out=stat[:, 1, :], in_=ssq[:], axis=AX.X, op=ALU.add),
                      reads=[b_st], writes=[b_st])
                ts(em, stat[:, 0, :], stat[:, 0, :], 1.0 / CW, None, ALU.mult, None, [b_st], [b_st])
                ts(em, stat[:, 1, :], stat[:, 1, :], 1.0 / CW, None, ALU.mult, None, [b_st], [b_st])
                tt(em, st2[:, 0, :], stat[:, 0, :], stat[:, 0, :], ALU.mult, [b_st], [b_st])
                tt(em, st2[:, 1, :], stat[:, 1, :], st2[:, 0, :], ALU.subtract, [b_st], [b_st])
                act(em, stat[:, 2, :], st2[:, 1, :], AF.Sqrt, [b_st], [b_st], scale=1.0, bias=LN_EPS)
                em.op("dve", lambda e: e.reciprocal(out=stat[:, 2, :], in_=stat[:, 2, :]), reads=[b_st], writes=[b_st])
                tt(em, st2[:, 0, :], stat[:, 0, :], stat[:, 2, :], ALU.mult, [b_st], [b_st])
                ts(em, stat[:, 3, :], st2[:, 0, :], -1.0, None, ALU.mult, None, [b_st], [b_st])
                em.barrier()
            with ExitStack() as ph:
                wfr2 = Ring(nc, ph, "u_wf", [128, 8, 256], F32, 2)
                wbr2 = Ring(nc, ph, "u_wb", [128, KC, 256], BF16, 4)
                utr = Ring(nc, ph, "u_ut", [128, TP], F32, 2)
                gtr = Ring(nc, ph, "u_gt", [128, TP], F32, 2)
                ytr = Ring(nc, ph, "u_yt", [128, TP], BF16, 2)
                gvtr = Ring(nc, ph, "u_gvt", [128, 128], BF16, 4)
                ntr = Ring(nc, ph, "u_nt", [128, 128], BF16, 4)
                addr = Ring(nc, ph, "u_add", [128, 128], F32, 2)
                vmr = Ring(nc, ph, "u_vm", [128, 512], F32, 2)
                for cc in range(NCC):
                    g = cc // (GRP_W // 128)
                    sub = cc % 2
                    if sub == 0:
                        wpair = []
                        for which, cbase in ((0, 0), (1, 2 * CW)):
                            wb2, wbb = wbr2.next()
                            for q in range(4):
                                wf, wfb = wfr2.next()
                                em.dma("sp", wf[:],
                                       gw_in[q * 1024:(q + 1) * 1024, cbase + cc * 128:cbase + cc * 128 + 256].rearrange("(k p) n -> p k n", p=128),
                                       writes=[wfb])
                                cast(em, wb2[:, q * 8:(q + 1) * 8, :], wf[:], [wfb], [wbb], eng="dve")
                            wpair.append((wb2, wbb))
                    wbs = [(wpair[0][0][:, :, sub * 128:(sub + 1) * 128], wpair[0][1]),
                           (wpair[1][0][:, :, sub * 128:(sub + 1) * 128], wpair[1][1])]
                    ut, ub = utr.next()
                    gt, gtb = gtr.next()
                    for tsi in range(NTS):
                        for (wb, wbb), dst, dstb, fn in ((wbs[0], ut, ub, AF.Gelu_apprx_tanh), (wbs[1], gt, gtb, AF.Silu)):
                            pt, pb = PS.next()
                            for k in range(KC):
                                rd = [wbb] + [h1b[s_][k] for s_ in range(tsi * 512 // TS, (tsi + 1) * 512 // TS)]
                                mm(em, pt[:, :], wb[:, k, :], h1T[:, k, tsi * 512:(tsi + 1) * 512], k == 0, k == KC - 1,
                                   rd, [pb])
                            act(em, dst[:, tsi * 512:(tsi + 1) * 512], pt[:, :], fn, [pb], [dstb])
                    addt, addb = addr.next()
                    stt(em, addt[:], rsb[:, g, :], lnt[:, 1, cc:cc + 1], bsbt[:, g, :], ALU.mult, ALU.add, [b_c], [addb])
                    yt, ytb = ytr.next()
                    for hf in range(NTS):
                        pt, pb = PS.next()
                        for j4 in range(4):
                            tb = hf * 4 + j4
                            gvt, gvtb = gvtr.next()
                            em.dma("sp", gvt[:],
                                   gvD[tb * 128:(tb + 1) * 128, cc * 128:(cc + 1) * 128], reads=[gvb[tb]], writes=[gvtb])
                            nt, ntb = ntr.next()
                            act(em, nt[:], gvt[:], AF.Identity, [gvtb, b_st], [ntb],
                                scale=stat[:, 2, tb:tb + 1], bias=stat[:, 3, tb:tb + 1])
                            mm(em, pt[:, j4 * 128:(j4 + 1) * 128], nt[:, :], wsb[:, g, :], True, True, [ntb, b_c], [pb])
                        vm, vmb = vmr.next()
                        stt(em, vm[:].rearrange("p (a i) -> p a i", a=4), pt[:, :].rearrange("p (a i) -> p a i", a=4),
                            lnt[:, 0, cc:cc + 1], addt[:, None, :].broadcast_to([128, 4, 128]), ALU.mult, ALU.add,
                            [pb, b_c, addb], [vmb])
                        tt(em, vm[:], vm[:], ut[:, hf * 512:(hf + 1) * 512], ALU.mult, [vmb, ub], [vmb])
                        tt(em, yt[:, hf * 512:(hf + 1) * 512], vm[:], gt[:, hf * 512:(hf + 1) * 512], ALU.mult,
                           [vmb, gtb], [ytb])
                    em.dma("pool", yD[cc], yt[:], reads=[ytb], writes=[yb[cc]])
                em.barrier()
            with ExitStack() as ph:
                w2fr = Ring(nc, ph, "o_wf", [128, 512], F32, 3)
                w2br = Ring(nc, ph, "o_wb", [128, 512], BF16, 3)
                yh = ph.enter_context(nc.sbuf_tensor(_u("o_yh"), [128, NCC, 512], BF16))
                NYG = 8
                yhb = [Buf("yh%d" % j) for j in range((NCC + NYG - 1) // NYG)]
                xnr = Ring(nc, ph, "o_xn", [128, 512], F32, 3)
                for tsi in range(NTS):
                    for j in range(len(yhb)):
                        c0, c1 = j * NYG, min(NCC, (j + 1) * NYG)
                        em.dma("act", yh[:, c0:c1, :], yD[c0:c1, :, tsi * 512:(tsi + 1) * 512].rearrange("c p t -> p c t"),
                               reads=yb[c0:c1], writes=[yhb[j]])
                    for ng in range(KC // 4):
                        pss = [PS.next() for _ in range(4)]
                        for cc in range(NCC):
                            wf, wfb = w2fr.next()
                            em.dma("sp", wf[:], gw_out[cc * 128:(cc + 1) * 128, ng * 512:(ng + 1) * 512], writes=[wfb])
                            wb, wbb = w2br.next()
                            cast(em, wb[:], wf[:], [wfb], [wbb], eng="dve")
                            for nn in range(4):
                                pt, pb = pss[nn]
                                mm(em, pt[:, :], wb[:, nn * 128:(nn + 1) * 128], yh[:, cc, :], cc == 0, cc == NCC - 1,
                                   [wbb, yhb[cc // NYG]], [pb])
                        for nn in range(4):
                            n = ng * 4 + nn
                            pt, pb = pss[nn]
                            xn, xnb = xnr.next()
                            em.dma("sp", xn[:], xnewD[:, n, P0 + tsi * 512:P0 + (tsi + 1) * 512], reads=[xnew_b[ps_i]], writes=[xnb])
                            stt(em, xn[:], pt[:, :], gate1[:, n:n + 1], xn[:], ALU.mult, ALU.add, [pb, b_c, xnb], [xnb])
                            em.dma("pool", xfinD[:, n, tsi * 512:(tsi + 1) * 512], xn[:], reads=[xnb], writes=[xfb[n]])
                em.barrier()
            with ExitStack() as ph:
                psb = lambda name, shape, dt: ph.enter_context(nc.sbuf_tensor(_u(name), shape, dt))
                xt = psb("f_xt", [128, KC, TS], F32)
                ores = psb("f_o", [128, KC, TS], F32)
                sqr = Ring(nc, ph, "f_sq", [128, TS], BF16, 3)
                tmpr = Ring(nc, ph, "f_tmp", [128, TS], F32, 2)
                rs_t = psb("f_rs", [128, TS], F32)
                rs_b = Buf("frs")
                xb = [Buf("fx%d" % n) for n in range(KC)]
                orb = [Buf("fo%d" % n) for n in range(KC)]
                for s in range(NS):
                    em.dma("sp", xt[:], xfinD[:, :, s * TS:(s + 1) * TS], reads=xfb, writes=xb)

                    def emit_out(n, tmp, tmpb):
                        act(em, ores[:, n, :], tmp[:], AF.Identity, [tmpb, b_c], [orb[n]], scale=fing[:, n:n + 1])
                    rmsnorm(sqr, tmpr, rs_t, rs_b, xt, xb, emit_out)
                    em.dma("sp", outT[:, :, P0 + s * TS:P0 + (s + 1) * TS], ores[:], reads=orb, writes=[b_out])
                em.barrier()
        em.finish([b_out])
    return nc


def fm(v, nchunk):
    return np.ascontiguousarray(np.asarray(v, np.float32).reshape(nchunk, 128).T)


def fmT(a):
    t, f = a.shape
    return np.ascontiguousarray(a.reshape(t, f // 128, 128).transpose(2, 1, 0))


def l3_core_inputs(x_tok, oT_tok, w_out, gw_in, gw_out, gate0, norm_g1, scale1, shift1, gate1, final_g,
                   ln_g, ln_b, w_s, b_s):
    NG = w_s.shape[0]
    NCC = NG * GRP_W // 128
    vec32 = np.stack([fm(v, KC) for v in (gate0, norm_g1, scale1, shift1, gate1, final_g)], axis=1)
    lnv = np.stack([fm(ln_g, NCC), fm(ln_b, NCC)], axis=1)
    return {
        "xT": fmT(x_tok), "oT": oT_tok, "w_out": np.ascontiguousarray(w_out), "gw_in": np.ascontiguousarray(gw_in),
        "gw_out": np.ascontiguousarray(gw_out), "vec32": np.ascontiguousarray(vec32), "lnv": np.ascontiguousarray(lnv),
        "wsT": np.ascontiguousarray(np.transpose(w_s, (2, 0, 1))),
        "bsb": np.ascontiguousarray(np.broadcast_to(b_s[None, :, :], (128,) + b_s.shape)),
    }


CHW = [128] * 31 + [96] * 4
CHO = [sum(CHW[:i]) for i in range(len(CHW) + 1)]
NWC = CHO[-1]
A_SCALE = 128 ** -0.5
NEG = -30000.0
CH = 64


def rmsnorm_fm(em, PS, ones, b_c, sqr, tmpr, rs_t, rs_b, xt, xb, N, emit_chunk):
    pt, pb = PS.next()
    for n in range(KC):
        sq, sqb = sqr.next()
        act(em, sq[:, 0:N], xt[:, n, 0:N], AF.Square, [xb[n]], [sqb])
        mm(em, pt[:, 0:N], ones[:, :], sq[:, 0:N], n == 0, n == KC - 1, [sqb, b_c], [pb])
    act(em, rs_t[:, 0:N], pt[:, 0:N], AF.Sqrt, [pb], [rs_b], scale=1.0 / D, bias=NORM_EPS)
    em.op("dve", lambda e: e.reciprocal(out=rs_t[:, 0:N], in_=rs_t[:, 0:N]), reads=[rs_b], writes=[rs_b])
    for n in range(KC):
        tmp, tmpb = tmpr.next()
        tt(em, tmp[:, 0:N], xt[:, n, 0:N], rs_t[:, 0:N], ALU.mult, [xb[n], rs_b], [tmpb])
        emit_chunk(n, tmp, tmpb)


def build_l2(TL, TC, do_b=True, dbg=False):
    TT = TL + TC
    NQB = TL // 128
    nc = _new_nc()
    inp = lambda name, shape, dt=F32: nc.dram_tensor(name, shape, dt, kind="ExternalInput").ap()
    xT = inp("xT", [128, KC, TL])
    cT = inp("cT", [128, KC, TC])
    wcols = inp("wcols", [D, NWC])
    vec = inp("vec", [128, 6, KC])
    sinkb = inp("sinkb", [128, 4])
    cosT = inp("cosT", [128, TL])
    sinT = inp("sinT", [128, TL])
    ident = inp("ident", [128, 128])
    amask = inp("amask", [128, 3, 384])
    oT = nc.dram_tensor("oT", [128, 8, TL], BF16, kind="ExternalOutput").ap()
    hD = nc.dram_tensor("hD", [128, KC, TT], BF16).ap()
    wD = nc.dram_tensor("wD", [128, KC, NWC], BF16).ap()
    qD = nc.dram_tensor("qD", [4, 128, TL], BF16).ap()
    kD = nc.dram_tensor("kD", [128, TT], BF16).ap()
    vD = nc.dram_tensor("vD", [TT, 128], BF16).ap()
    gaD = nc.dram_tensor("gaD", [4, 128, TL], BF16).ap()
    gbD = nc.dram_tensor("gbD", [4, 128, TL], BF16).ap()
    zD = nc.dram_tensor("zD", [12, 128, TT], F32).ap()
    lD = nc.dram_tensor("lD", [4, 96, TT], F32).ap()
    with ExitStack() as es:
        em = Emit(nc, es)
        sb = lambda name, shape, dt: es.enter_context(nc.sbuf_tensor(name, shape, dt))
        PS = Ring(nc, es, "ps", [128, 512], F32, 6, psum=True)
        PB = Ring(nc, es, "pb", [128, 1024], BF16, 2, psum=True)
        b_c, b_out = Buf("consts"), Buf("out")
        vt = sb("vt", [128, 6, KC], F32)
        gsl = sb("gsl", [128, KC], F32)
        gsc = sb("gsc", [128, KC], F32)
        ones = sb("ones", [128, 128], F32)
        ones16 = sb("ones16", [128, 128], BF16)
        idf = sb("idf", [128, 128], F32)
        idb = sb("idb", [128, 128], BF16)
        skb = sb("skb", [128, 4], F32)
        msk = sb("msk", [128, 3, 384], F32)
        em.dma("sp", vt[:], vec[:, :, :], writes=[b_c])
        em.dma("sp", idf[:], ident[:, :], writes=[b_c])
        em.dma("sp", skb[:], sinkb[:, :], writes=[b_c])
        em.dma("sp", msk[:], amask[:, :, :], writes=[b_c])
        em.op("dve", lambda e: e.memset(ones[:], 1.0), writes=[b_c])
        em.op("dve", lambda e: e.memset(ones16[:], 1.0), writes=[b_c])
        cast(em, idb[:], idf[:], [b_c], [b_c], eng="dve")
        for dst, sidx in ((gsl, 1), (gsc, 3)):
            ts(em, dst[:], vt[:, sidx, :], 1.0, None, ALU.add, None, [b_c], [b_c])
            tt(em, dst[:], dst[:], vt[:, 0, :], ALU.mult, [b_c], [b_c])
        hb = Buf("hD")
        wDb = Buf("wD")
        with ExitStack() as ph:
            wfr = Ring(nc, ph, "w_wf", [128, 8, 512], F32, 3)
            wbr = Ring(nc, ph, "w_wb", [128, 8, 512], BF16, 3)
            ncol = (NWC + 511) // 512
            for ct in range(ncol):
                c0, c1 = ct * 512, min(NWC, (ct + 1) * 512)
                for q in range(4):
                    wf, wfb = wfr.next()
                    em.dma("sp" if q % 2 == 0 else "act", wf[:, :, 0:c1 - c0],
                           wcols[q * 1024:(q + 1) * 1024, c0:c1].rearrange("(k p) n -> p k n", p=128), writes=[wfb])
                    wb, wbb = wbr.next()
                    cast(em, wb[:, :, 0:c1 - c0], wf[:, :, 0:c1 - c0], [wfb], [wbb], eng="dve")
                    em.dma("pool", wD[:, q * 8:(q + 1) * 8, c0:c1], wb[:, :, 0:c1 - c0], reads=[wbb], writes=[wDb])
            em.barrier()
        with ExitStack() as ph:
            psb = lambda name, shape, dt: ph.enter_context(nc.sbuf_tensor(_u(name), shape, dt))
            TS = 256
            xts = [psb("h_xt%d" % i, [128, KC, TS], F32) for i in range(2)]
            hts = [psb("h_ht%d" % i, [128, KC, TS], BF16) for i in range(2)]
            sqr = Ring(nc, ph, "h_sq", [128, TS], BF16, 3)
            tmpr = Ring(nc, ph, "h_tmp", [128, TS], F32, 3)
            rss = [psb("h_rs%d" % i, [128, TS], F32) for i in range(2)]
            rsbs = [Buf("rs0"), Buf("rs1")]
            xbs = [[Buf("hx%d_%d" % (i, n)) for n in range(KC)] for i in range(2)]
            hcbs = [[Buf("hh%d_%d" % (i, n)) for n in range(KC)] for i in range(2)]
            for s in range(TT // TS):
                t0 = s * TS
                i2 = s % 2
                xt, ht, xb, hcb = xts[i2], hts[i2], xbs[i2], hcbs[i2]
                isc = t0 >= TL
                src = cT[:, :, t0 - TL:t0 - TL + TS] if isc else xT[:, :, t0:t0 + TS]
                em.dma("sp", xt[:], src, writes=xb)
                gs_, sh_ = (gsc, vt[:, 4, :]) if isc else (gsl, vt[:, 2, :])

                def emit_h(n, tmp, tmpb, gs_=gs_, sh_=sh_, ht=ht, hcb=hcb):
                    act(em, ht[:, n, :], tmp[:], AF.Identity, [tmpb, b_c], [hcb[n]],
                        scale=gs_[:, n:n + 1], bias=sh_[:, n:n + 1])
                rmsnorm_fm(em, PS, ones16, b_c, sqr, tmpr, rss[i2], rsbs[i2], xt, xb, TS, emit_h)
                em.dma("act", hD[:, :, t0:t0 + TS], ht[:], reads=hcb, writes=[hb])
            em.barrier()
        qDb, kDb, vDb, gaDb, gbDb, zDb, lDb = (Buf(n) for n in ("qD", "kD", "vD", "gaD", "gbD", "zD", "lD"))
        with ExitStack() as ph:
            psb = lambda name, shape, dt: ph.enter_context(nc.sbuf_tensor(_u(name), shape, dt))
            htr = Ring(nc, ph, "p_h", [128, KC, 512], BF16, 2)
            wr = Ring(nc, ph, "p_w", [128, KC, 128], BF16, 4)
            csr = Ring(nc, ph, "p_cs", [128, 2, 512], F32, 2)
            e32 = Ring(nc, ph, "p_e32", [128, 512], F32, 4)
            e16 = Ring(nc, ph, "p_e16", [128, 512], BF16, 4)
            tiles = [(t0, 512, False) for t0 in range(0, TL, 512)] + [(TL + t0, min(512, TC - t0), True) for t0 in range(0, TC, 512)]
            for (t0, N, isc) in tiles:
                ht, htb = htr.next()
                em.dma("sp", ht[:, :, 0:N], hD[:, :, t0:t0 + N], reads=[hb], writes=[htb])
                if not isc:
                    cs, csb = csr.next()
                    em.dma("sp", cs[:, 0, 0:N], cosT[:, t0:t0 + N], writes=[csb])
                    em.dma("sp", cs[:, 1, 0:N], sinT[:, t0:t0 + N], writes=[csb])

                def proj(ci, swap_roles=False):
                    w, wb_ = wr.next()
                    wd = CHW[ci]
                    em.dma("sp", w[:, :, 0:wd], wD[:, :, CHO[ci]:CHO[ci] + wd], reads=[wDb], writes=[wb_])
                    pt, pb = PS.next()
                    if not swap_roles:
                        for k in range(KC):
                            mm(em, pt[0:wd, 0:N], w[:, k, 0:wd], ht[:, k, 0:N], k == 0, k == KC - 1, [wb_, htb], [pb])
                    else:
                        for blk in range(N // 128):
                            for k in range(KC):
                                mm(em, pt[:, blk * 128:(blk + 1) * 128], ht[:, k, blk * 128:(blk + 1) * 128], w[:, k, :],
                                   k == 0, k == KC - 1, [wb_, htb], [pb])
                    return pt, pb

                def rope_pair(ci, scale, dst_ap, dstbuf):
                    p1, b1 = proj(ci)
                    p2, b2 = proj(ci + (4 if ci < 4 else 1))
                    t1, t1b = e32.next()
                    t2, t2b = e32.next()
                    stt(em, t1[:, 0:N], p1[:, 0:N], scale, cs[:, 0, 0:N], ALU.mult, ALU.mult, [b1, csb], [t1b])
                    stt(em, t2[:, 0:N], p2[:, 0:N], scale, cs[:, 1, 0:N], ALU.mult, ALU.mult, [b2, csb], [t2b])
                    o16, o16b = e16.next()
                    tt(em, o16[:, 0:N], t1[:, 0:N], t2[:, 0:N], ALU.add, [t1b, t2b], [o16b], eng="pool")
                    em.dma("pool", dst_ap, o16[:, 0:N], reads=[o16b], writes=[dstbuf])

                if not isc:
                    for h in range(4):
                        rope_pair(h, A_SCALE, qD[h, :, t0:t0 + N], qDb)
                    rope_pair(8, 1.0, kD[:, t0:t0 + N], kDb)
                else:
                    p1, b1 = proj(8)
                    o16, o16b = e16.next()
                    act(em, o16[:, 0:N], p1[:, 0:N], AF.Copy, [b1], [o16b])
                    em.dma("pool", kD[:, t0:t0 + N], o16[:, 0:N], reads=[o16b], writes=[kDb])
                p1, b1 = proj(10, swap_roles=True)
                o16, o16b = e16.next()
                act(em, o16[:, 0:N], p1[:, 0:N], AF.Copy, [b1], [o16b])
                em.dma("pool", vD[t0:t0 + N, :].rearrange("(b p) d -> p b d", p=128),
                       o16[:, 0:N].rearrange("p (b d) -> p b d", d=128), reads=[o16b], writes=[vDb])
                if not isc:
                    for h in range(4):
                        for ci, dstD, dstb in ((11 + h, gaD, gaDb), (27 + h, gbD, gbDb)):
                            if ci >= 27 and not do_b:
                                continue
                            p1, b1 = proj(ci)
                            o16, o16b = e16.next()
                            act(em, o16[:, 0:N], p1[:, 0:N], AF.Silu, [b1], [o16b])
                            em.dma("pool", dstD[h, :, t0:t0 + N], o16[:, 0:N], reads=[o16b], writes=[dstb])
                if do_b:
                    for j in range(12):
                        p1, b1 = proj(15 + j)
                        o32, o32b = e32.next()
                        act(em, o32[:, 0:N], p1[:, 0:N], AF.Copy, [b1], [o32b])
                        em.dma("pool", zD[j, :, t0:t0 + N], o32[:, 0:N], reads=[o32b], writes=[zDb])
                    for j in range(4):
                        p1, b1 = proj(31 + j)
                        o32, o32b = e32.next()
                        act(em, o32[0:96, 0:N], p1[0:96, 0:N], AF.Tanh if j < 2 else AF.Copy, [b1], [o32b])
                        em.dma("pool", lD[j, :, t0:t0 + N], o32[0:96, 0:N], reads=[o32b], writes=[lDb])
            em.barrier()
        with ExitStack() as ph:
            psb = lambda name, shape, dt: ph.enter_context(nc.sbuf_tensor(_u(name), shape, dt))
            kc = psb("a_kc", [128, TC], BF16)
            vc = psb("a_vc", [128, TC // 128, 128], BF16)
            em.dma("sp", kc[:], kD[:, TL:TL + TC], reads=[kDb], writes=[b_c])
            em.dma("sp", vc[:], vD[TL:TL + TC, :].rearrange("(b p) d -> p b d", p=128), reads=[vDb], writes=[b_c])
            NCB = TC // 128
            NK = 384 + TC
            kbr = Ring(nc, ph, "a_kb", [128, 384], BF16, 2)
            vbr = Ring(nc, ph, "a_vb", [128, 3, 128], BF16, 2)
            qtr = Ring(nc, ph, "a_q", [128, 4, 128], BF16, 2)
            gar = Ring(nc, ph, "a_ga", [128, 4, 128], BF16, 2)
            sbr = Ring(nc, ph, "a_s", [128, 384], F32, 4)
            pr = Ring(nc, ph, "a_p", [128, NK], BF16, 4)
            ptr_ = Ring(nc, ph, "a_pT", [128, NK], BF16, 4)
            smr = Ring(nc, ph, "a_sm", [128, 8], F32, 8)
            otr = Ring(nc, ph, "a_o", [128, 4, 128], BF16, 2)
            for n in range(NQB):
                kb, kbb = kbr.next()
                vb, vbb = vbr.next()
                blks = [min(max(n - 1 + j, 0), NQB - 1) for j in range(3)]
                for j, bk in enumerate(blks):
                    em.dma("sp", kb[:, j * 128:(j + 1) * 128], kD[:, bk * 128:(bk + 1) * 128], reads=[kDb], writes=[kbb])
                    em.dma("act", vb[:, j, :], vD[bk * 128:(bk + 1) * 128, :], reads=[vDb], writes=[vbb])
                mi = 0 if n == 0 else (2 if n == NQB - 1 else 1)
                qt, qtb = qtr.next()
                ga, gab = gar.next()
                em.dma("sp", qt[:], qD[:, :, n * 128:(n + 1) * 128].rearrange("h d q -> d h q"), reads=[qDb], writes=[qtb])
                em.dma("act", ga[:], gaD[:, :, n * 128:(n + 1) * 128].rearrange("h d q -> d h q"), reads=[gaDb], writes=[gab])
                ot, otb = otr.next()

                def head_gen(h, kb=kb, kbb=kbb, vb=vb, vbb=vbb, qt=qt, qtb=qtb, ga=ga, gab=gab, ot=ot, otb=otb, mi=mi):
                        p1, b1 = PS.next()
                        p2, b2 = PS.next()
                        mm(em, p1[:, 0:384], qt[:, h, :], kb[:, :], True, True, [qtb, kbb], [b1])
                        mm(em, p2[:, 0:TC], qt[:, h, :], kc[:, :], True, True, [qtb, b_c], [b2])
                        s_, s_b = sbr.next()
                        tt(em, s_[:], p1[:, 0:384], msk[:, mi, :], ALU.add, [b1, b_c], [s_b])
                        sm, smb = smr.next()
                        em.op("dve", lambda e: e.tensor_reduce(out=sm[:, 0:1], in_=s_[:], axis=AX.X, op=ALU.max), reads=[s_b], writes=[smb])
                        em.op("dve", lambda e: e.tensor_reduce(out=sm[:, 1:2], in_=p2[:, 0:TC], axis=AX.X, op=ALU.max), reads=[b2], writes=[smb])
                        stt(em, sm[:, 2:3], sm[:, 0:1], skb[:, h:h + 1], sm[:, 1:2], ALU.max, ALU.max, [smb, b_c], [smb])
                        ts(em, sm[:, 3:4], sm[:, 2:3], -1.0, None, ALU.mult, None, [smb], [smb])
                        yield
                        p_, p_b = pr.next()
                        act(em, p_[:, 0:384], s_[:], AF.Exp, [s_b, smb], [p_b, smb], bias=sm[:, 3:4], scale=1.0, accum_out=sm[:, 4:5])
                        act(em, p_[:, 384:NK], p2[:, 0:TC], AF.Exp, [b2, smb], [p_b, smb], bias=sm[:, 3:4], scale=1.0, accum_out=sm[:, 5:6])
                        act(em, sm[:, 6:7], sm[:, 3:4], AF.Exp, [smb, b_c], [smb], bias=skb[:, h:h + 1], scale=1.0)
                        yield
                        tt(em, sm[:, 7:8], sm[:, 4:5], sm[:, 5:6], ALU.add, [smb], [smb])
                        tt(em, sm[:, 7:8], sm[:, 7:8], sm[:, 6:7], ALU.add, [smb], [smb])
                        em.op("dve", lambda e: e.reciprocal(out=sm[:, 7:8], in_=sm[:, 7:8]), reads=[smb], writes=[smb])
                        ts(em, p_[:], p_[:], sm[:, 7:8], None, ALU.mult, None, [p_b, smb], [p_b])
                        yield
                        pT_ps, pTb = PB.next()
                        nb = 3 + NCB
                        for j in range(nb):
                            em.op("pe", lambda e, j=j: e.transpose(out=pT_ps[:, j * 128:(j + 1) * 128], in_=p_[:, j * 128:(j + 1) * 128],
                                                                 identity=idb[:, :]), reads=[p_b, b_c], writes=[pTb])
                        pT, pTsb = ptr_.next()
                        act(em, pT[:], pT_ps[:, 0:NK], AF.Copy, [pTb], [pTsb])
                        yield
                        po, pob = PS.next()
                        for j in range(nb):
                            vv = vb[:, j, :] if j < 3 else vc[:, j - 3, :]
                            mm(em, po[:, 0:128], vv, pT[:, j * 128:(j + 1) * 128], j == 0, j == nb - 1, [vbb, b_c, pTsb], [pob])
                        tt(em, ot[:, h, :], po[:, 0:128], ga[:, h, :], ALU.mult, [pob, gab], [otb])

                for pair in ((0, 1), (2, 3)):
                    gens = [head_gen(h) for h in pair]
                    while gens:
                        gens = [g_ for g_ in gens if next(g_, "done") != "done"]
                em.dma("pool", oT[:, 0:4, n * 128:(n + 1) * 128], ot[:], reads=[otb], writes=[b_out])
            em.barrier()
        if do_b:
            _l2_rwkv(nc, es, em, PS, PB, TL, TC, zD, lD, gbD, oT, (zDb, lDb, gbDb), b_c, b_out, ones, idf, idb, inp, dbg)
        em.finish([b_out])
    return nc


GN_EPS = 64e-5
DECAY_K = -0.6065306597126334


def _l2_rwkv(nc, es, em, PS, PB, TL, TC, zD, lD, gbD, oT, dbufs, b_c, b_out, ones, idf, idb, inp, dbg=False):
    zDb, lDb, gbDb = dbufs
    TT = TL + TC
    NCH = TT // CH
    cv = inp("cv", [128, 3, 3, 4])
    pv = inp("pv", [128, 9, 4])
    w2c = inp("w2c", [96, 2, 512])
    a2c = inp("a2c", [96, 2, 512])
    gnb = inp("gnb", [128, 2, 512])
    rmask = inp("rmask", [128, 4, 128])
    blk1 = inp("blk1", [128, 128])
    kd = {"kind": "ExternalOutput"} if dbg else {}
    opD = nc.dram_tensor("opD", [2, 4, 4, 128, TT], BF16, **kd).ap()
    vcD = nc.dram_tensor("vcD", [4, 128, TT], BF16, **kd).ap()
    bvD = nc.dram_tensor("bvD", [4, 128, TL], F32, **kd).ap()
    gcD = nc.dram_tensor("gcD", [2, 4, 128, NCH], F32, **kd).ap()
    yD = nc.dram_tensor("yD", [2, TL, 8, 64], F32, **kd).ap()
    opDb, vcDb, bvDb, gcDb = Buf("opD"), Buf("vcD"), Buf("bvD"), Buf("gcD")
    yDb = [Buf("yD0"), Buf("yD1")]
    sb = lambda name, shape, dt: es.enter_context(nc.sbuf_tensor(name, shape, dt))
    cvt = sb("r_cv", [128, 3, 3, 4], F32)
    pvt = sb("r_pv", [128, 9, 4], F32)
    w2t = sb("r_w2", [96, 2, 512], F32)
    a2t = sb("r_a2", [96, 2, 512], F32)
    gnt = sb("r_gn", [128, 2, 512], F32)
    mkt = sb("r_mk", [128, 4, 128], F32)
    b1t = sb("r_b1", [128, 128], F32)
    for dst, src_ in ((cvt, cv), (pvt, pv), (w2t, w2c), (a2t, a2c), (gnt, gnb), (mkt, rmask), (b1t, blk1)):
        em.dma("sp", dst[:], src_, writes=[b_c])
    ts(em, pvt[:, 7, :], pvt[:, 5, :], -1.0, 1.0, ALU.mult, ALU.add, [b_c], [b_c])
    with ExitStack() as ph:
        def R_(name, shape, dt, n):
            return Ring(nc, ph, name, shape, dt, n)
        class L_:
            pass
        lanes = []
        for li in range(2):
            l_ = L_()
            l_.ztr = R_("b1_z%d" % li, [128, 514], F32, 3)
            l_.rkv = R_("b1_rkv%d" % li, [128, 3, 512], F32, 2)
            l_.t32 = R_("b1_t%d" % li, [128, 512], F32, 12)
            l_.kkr = R_("b1_kk%d" % li, [128, 512], F32, 2)
            l_.lwr = R_("b1_lw%d" % li, [96, 2, 512], F32, 2)
            l_.gct = R_("b1_gc%d" % li, [128, 8], F32, 4)
            l_.o16r = R_("b1_o16%d" % li, [128, 512], BF16, 6)
            lanes.append(l_)
        tiles = [(t0, 512, 0, TL) for t0 in range(0, TL, 512)] + [(TL + t0, min(512, TC - t0), TL, TT) for t0 in range(0, TC, 512)]

        def b1_iter(c, t0, N, s0, s1, LN):
            ztr, rkv, t32, kkr, lwr, gct, o16r = LN.ztr, LN.rkv, LN.t32, LN.kkr, LN.lwr, LN.gct, LN.o16r
            J = N // CH
            islat = s0 == 0
            x3, x3b = rkv.next()
            for kind in range(3):
                zt, ztb = ztr.next()
                j = kind * 4 + c
                em.dma("sp", zt[:, 1:N + 1], zD[j, :, t0:t0 + N], reads=[zDb], writes=[ztb])
                if t0 > s0:
                    em.dma("act", zt[:, 0:1], zD[j, :, t0 - 1:t0], reads=[zDb], writes=[ztb], allow_slow_non_contiguous=True)
                else:
                    em.op("pool", lambda e: e.memset(zt[:, 0:1], 0.0), writes=[ztb])
                if t0 + N < s1:
                    em.dma("act", zt[:, N + 1:N + 2], zD[j, :, t0 + N:t0 + N + 1], reads=[zDb], writes=[ztb], allow_slow_non_contiguous=True)
                else:
                    em.op("pool", lambda e: e.memset(zt[:, N + 1:N + 2], 0.0), writes=[ztb])
                o = x3[:, kind, 0:N]
                ts(em, o, zt[:, 0:N], cvt[:, kind, 0, c:c + 1], None, ALU.mult, None, [ztb, b_c], [x3b])
                stt(em, o, zt[:, 1:N + 1], cvt[:, kind, 1, c:c + 1], o, ALU.mult, ALU.add, [ztb, b_c, x3b], [x3b])
                stt(em, o, zt[:, 2:N + 2], cvt[:, kind, 2, c:c + 1], o, ALU.mult, ALU.add, [ztb, b_c, x3b], [x3b])
            r_t, k_t, v_t = x3[:, 0, 0:N], x3[:, 1, 0:N], x3[:, 2, 0:N]
            yield
            v16, v16b = o16r.next()
            cast(em, v16[:, 0:N], v_t, [x3b], [v16b], eng="pool")
            em.dma("pool", vcD[c, :, t0:t0 + N], v16[:, 0:N], reads=[v16b], writes=[vcDb])
            sq, sqb = t32.next()
            act(em, sq[:, 0:N], k_t, AF.Square, [x3b, b_c], [sqb], scale=pvt[:, 4, c:c + 1])
            pt, pb = PS.next()
            mm(em, pt[:, 0:N], b1t[:, :], sq[:, 0:N], True, True, [sqb, b_c], [pb])
            nr, nrb = t32.next()
            act(em, nr[:, 0:N], pt[:, 0:N], AF.Sqrt, [pb], [nrb])
            ts(em, nr[:, 0:N], nr[:, 0:N], 1e-12, None, ALU.max, None, [nrb], [nrb])
            em.op("dve", lambda e: e.reciprocal(out=nr[:, 0:N], in_=nr[:, 0:N]), reads=[nrb], writes=[nrb])
            kk, kkb = kkr.next()
            stt(em, kk[:, 0:N], k_t, pvt[:, 4, c:c + 1], nr[:, 0:N], ALU.mult, ALU.mult, [x3b, b_c, nrb], [kkb])
            if islat:
                rk, rkb = t32.next()
                stt(em, rk[:, 0:N], r_t, pvt[:, 6, c:c + 1], k_t, ALU.mult, ALU.mult, [x3b, b_c], [rkb])
                pt, pb = PS.next()
                mm(em, pt[:, 0:N], b1t[:, :], rk[:, 0:N], True, True, [rkb, b_c], [pb])
                bv, bvb = t32.next()
                tt(em, bv[:, 0:N], pt[:, 0:N], v_t, ALU.mult, [pb, x3b], [bvb])
                em.dma("pool", bvD[c, :, t0:t0 + N], bv[:, 0:N], reads=[bvb], writes=[bvDb])
            yield
            lw, lwb = lwr.next()
            la, lab = lwr.next()
            for d in range(2):
                em.dma("sp", lw[:, d, 0:N], lD[d, :, t0:t0 + N], reads=[lDb], writes=[lwb])
                em.dma("act", la[:, d, 0:N], lD[2 + d, :, t0:t0 + N], reads=[lDb], writes=[lab])
            for d in range(2):
                pt, pb = PS.next()
                mm(em, pt[:, 0:N], w2t[:, d, c * 128:(c + 1) * 128], lw[:, d, 0:N], True, True, [lwb, b_c], [pb])
                lg, lgb = t32.next()
                act(em, lg[:, 0:N], pt[:, 0:N], AF.Sigmoid, [pb, b_c], [lgb], bias=pvt[:, d, c:c + 1], scale=1.0)
                ts(em, lg[:, 0:N], lg[:, 0:N], DECAY_K, None, ALU.mult, None, [lgb], [lgb], eng="pool")
                pt, pb = PS.next()
                mm(em, pt[:, 0:N], a2t[:, d, c * 128:(c + 1) * 128], la[:, d, 0:N], True, True, [lab, b_c], [pb])
                a_, a_b = t32.next()
                act(em, a_[:, 0:N], pt[:, 0:N], AF.Sigmoid, [pb, b_c], [a_b], bias=pvt[:, 2 + d, c:c + 1], scale=1.0)
                yield
                P_, P_b = t32.next()
                for jj in range(J):
                    em.op("dve", lambda e, jj=jj: e.tensor_tensor_scan(
                        out=P_[:, jj * CH:(jj + 1) * CH], data0=ones[:, 0:CH], data1=lg[:, jj * CH:(jj + 1) * CH],
                        initial=0.0, op0=ALU.mult, op1=ALU.add), reads=[lgb, b_c], writes=[P_b])
                P3 = P_[:, 0:N].rearrange("p (j i) -> p j i", i=CH)
                tot = P3[:, :, CH - 1:CH]
                cin, cinb = t32.next()
                cex, cexb = t32.next()
                if d == 0:
                    tt(em, cex[:, 0:N], P_[:, 0:N], lg[:, 0:N], ALU.subtract, [P_b, lgb], [cexb], eng="pool")
                    cin_ap, cin_rd = P_[:, 0:N], P_b
                else:
                    tt(em, cex[:, 0:N].rearrange("p (j i) -> p j i", i=CH), tot.broadcast_to([128, J, CH]), P3, ALU.subtract,
                       [P_b], [cexb])
                    tt(em, cin[:, 0:N], cex[:, 0:N], lg[:, 0:N], ALU.add, [cexb, lgb], [cinb], eng="pool")
                    cin_ap, cin_rd = cin[:, 0:N], cinb
                yield
                g8, g8b = gct.next()
                act(em, g8[:, 0:J], P3[:, :, CH - 1], AF.Exp, [P_b], [g8b])
                em.dma("pool", gcD[d, c, :, t0 // CH:t0 // CH + J], g8[:, 0:J], reads=[g8b], writes=[gcDb])
                ein, einb = t32.next()
                eni, enib = t32.next()
                act(em, ein[:, 0:N], cin_ap, AF.Exp, [cin_rd], [einb])
                act(em, eni[:, 0:N], cin_ap, AF.Exp, [cin_rd], [enib], scale=-1.0)
                act(em, cex[:, 0:N], cex[:, 0:N], AF.Exp, [cexb], [cexb])
                yield
                o1, o1b = o16r.next()
                tt(em, o1[:, 0:N], kk[:, 0:N], cex[:, 0:N], ALU.mult, [kkb, cexb], [o1b], eng="pool")
                em.dma("sp", opD[d, 0, c, :, t0:t0 + N], o1[:, 0:N], reads=[o1b], writes=[opDb])
                o3, o3b = o16r.next()
                tt(em, o3[:, 0:N], r_t, ein[:, 0:N], ALU.mult, [x3b, einb], [o3b])
                em.dma("act", opD[d, 1, c, :, t0:t0 + N], o3[:, 0:N], reads=[o3b], writes=[opDb])
                yield
                o2, o2b = t32.next()
                tt(em, o2[:, 0:N], kk[:, 0:N], a_[:, 0:N], ALU.mult, [kkb, a_b], [o2b], eng="pool")
                o4, o4b = o16r.next()
                stt(em, o4[:, 0:N], o2[:, 0:N], -1.0, eni[:, 0:N], ALU.mult, ALU.mult, [o2b, enib], [o4b])
                em.dma("sp", opD[d, 2, c, :, t0:t0 + N], o4[:, 0:N], reads=[o4b], writes=[opDb])
                ts(em, a_[:, 0:N], a_[:, 0:N], pvt[:, 5, c:c + 1], pvt[:, 7, c:c + 1], ALU.mult, ALU.add, [a_b, b_c], [a_b])
                tt(em, a_[:, 0:N], a_[:, 0:N], k_t, ALU.mult, [a_b, x3b], [a_b], eng="pool")
                o5, o5b = o16r.next()
                tt(em, o5[:, 0:N], a_[:, 0:N], eni[:, 0:N], ALU.mult, [a_b, enib], [o5b])
                em.dma("act", opD[d, 3, c, :, t0:t0 + N], o5[:, 0:N], reads=[o5b], writes=[opDb])

        def rr_(gens):
            gens = list(gens)
            while gens:
                nxt = []
                for g_ in gens:
                    if next(g_, "done") != "done":
                        nxt.append(g_)
                gens = nxt
        work = [(c, t0, N, s0, s1) for c in range(4) for (t0, N, s0, s1) in tiles]
        for wi in range(0, len(work), 2):
            rr_([b1_iter(*work[wi + li], lanes[li]) for li in range(2) if wi + li < len(work)])
        em.barrier()
    with ExitStack() as ph:
        def R_(name, n, shape=(128, 128), dt=BF16):
            return Ring(nc, ph, name, list(shape), dt, n)
        gall = ph.enter_context(nc.sbuf_tensor("s_gc", [128, 2, 4, NCH], F32))
        em.dma("sp", gall[:], gcD.rearrange("d c p n -> p d c n"), reads=[gcDb], writes=[b_c])
        PS_full, PB_full = PS, PB

        class BS:
            pass
        sets = {}
        for d in range(2):
            for par in range(2):
                s_ = BS()
                nm = "_%d_%d" % (d, par)
                s_.ops = [R_("s_op%d%s" % (k, nm), 1, (128, 4, 2, 64)) for k in range(5)]
                for r_ in s_.ops:
                    em.op("pool", lambda e, t_=r_.t[0]: e.memset(t_[:], 0.0), writes=[r_.b[0]])
                s_.trs = [R_("s_tr%d%s" % (k, nm), 1, (128, 4, 128)) for k in range(3)]
                s_.mks = [R_("s_mk%d%s" % (k, nm), 1, (128, 4, 128)) for k in range(5)]
                s_.pw = R_("s_pw" + nm, 4, (128, 4, 128))
                s_.wt = R_("s_wt" + nm, 2, (128, 4, 128))
                s_.x1 = R_("s_x1" + nm, 1, (128, 4, 128))
                s_.u = R_("s_u" + nm, 1, (128, 4, 128))
                s_.y = R_("s_y" + nm, 1, (128, 4, 128), dt=F32)
                s_.hn = R_("s_hn" + nm, 1, (128, 4, 128), dt=F32)
                sets[(d, par)] = s_
        H4 = [ph.enter_context(nc.sbuf_tensor("s_H4_%d" % d, [128, 4, 128], F32)) for d in range(2)]
        Hh4 = [ph.enter_context(nc.sbuf_tensor("s_Hh4_%d" % d, [128, 4, 128], BF16)) for d in range(2)]
        H4b = [Buf("H4_0"), Buf("H4_1")]
        for d in range(2):
            em.op("pool", lambda e, d=d: e.memset(H4[d][:], 0.0), writes=[H4b[d]])
            em.op("pool", lambda e, d=d: e.memset(Hh4[d][:], 0.0), writes=[H4b[d]])
        seqs = []
        for d in range(2):
            cc_ = [(TL + j * CH, False) for j in range(TC // CH)]
            ll_ = [(j * CH, True) for j in range(TL // CH)]
            seqs.append(cc_ + ll_ if d == 0 else cc_[::-1] + ll_[::-1])

        def f2(t4, c):
            return t4[:, c].rearrange("p a i -> p (a i)")

        def fl(t3):
            return t3[:].rearrange("p c i -> p (c i)")

        def v3(ap2):
            return ap2.rearrange("p (c i) -> p c i", c=4)

        def bc(ap2):
            return ap2[:, None, :].broadcast_to([128, 4, 128])

        def indep(u):
            d, o, islat, S = u
            st = {}
            ops = []
            for k in range(5):
                t4, tb = S.ops[k].next()
                for hh in range(2):
                    if k == 4:
                        src_ = vcD[:, hh * 64:(hh + 1) * 64, o:o + CH]
                    else:
                        src_ = opD[d, k, :, hh * 64:(hh + 1) * 64, o:o + CH]
                    em.dma("sp", t4[hh * 64:(hh + 1) * 64, :, hh, :], src_.rearrange("c p t -> p c t"),
                           reads=[vcDb if k == 4 else opDb], writes=[tb])
                ops.append((t4, tb))
            st["ops"] = ops
            (At, Atb), (Rt, Rtb), (Bt, Btb), (Kt, Ktb), (Vt, Vtb) = ops
            yield
            trs = []
            for k, (src_, sbuf_) in enumerate(((Bt, Btb), (Kt, Ktb), (Vt, Vtb))):
                pt, pb = PB_full.next()
                for c in range(4):
                    em.op("pe", lambda e, pt=pt, src_=src_, c=c: e.transpose(out=pt[:, c * 128:(c + 1) * 128], in_=f2(src_, c),
                                                                            identity=idb[:, :]), reads=[sbuf_, b_c], writes=[pb])
                t_, tb_ = S.trs[k].next()
                em.op("dve", lambda e, t_=t_, pt=pt: e.tensor_copy(out=fl(t_), in_=pt[:, 0:512]), reads=[pb], writes=[tb_])
                trs.append((t_, tb_))
                yield
            st["trs"] = trs
            ms, mi_, mL = (0, 1, 2) if d == 0 else (2, 3, 0)
            specs = ((Kt, Ktb, At, Atb, ms), (Bt, Btb, At, Atb, ms), (At, Atb, Bt, Btb, mL), (Bt, Btb, Rt, Rtb, mi_), (Kt, Ktb, Rt, Rtb, mi_))
            mks = []
            for k, (l_, lb_, r_, rb_, mi) in enumerate(specs):
                pt, pb = PS_full.next()
                for c in range(4):
                    mm(em, pt[:, c * 128:(c + 1) * 128], f2(l_, c), f2(r_, c), True, True, [lb_, rb_], [pb])
                t_, tb_ = S.mks[k].next()
                tt(em, t_[:], v3(pt[:, :]), bc(mkt[:, mi, :]), ALU.mult, [pb, b_c], [tb_])
                mks.append((t_, tb_))
                yield
            st["mks"] = mks
            (Np, Npb), (Lp, Lpb) = mks[1], mks[2]
            WT, WTb = S.wt.next()
            tt(em, WT[:], Np[:], bc(idf[:, :]), ALU.add, [Npb, b_c], [WTb], eng="pool")
            for lvl in range(5):
                pt, pb = PS_full.next()
                for c in range(4):
                    mm(em, pt[:, c * 128:(c + 1) * 128], Np[:, c, :], Lp[:, c, :], True, True, [Npb, Lpb], [pb])
                L2, L2b = S.pw.next()
                act(em, fl(L2), pt[:, :], AF.Copy, [pb], [L2b])
                if lvl < 4:
                    pt, pb = PS_full.next()
                    for c in range(4):
                        mm(em, pt[:, c * 128:(c + 1) * 128], Lp[:, c, :], Np[:, c, :], True, True, [Npb, Lpb], [pb])
                    N2, N2b = S.pw.next()
                    em.op("dve", lambda e, N2=N2, pt=pt: e.tensor_copy(out=fl(N2), in_=pt[:, :]), reads=[pb], writes=[N2b])
                yield
                pt, pb = PS_full.next()
                for c in range(4):
                    mm(em, pt[:, c * 128:(c + 1) * 128], L2[:, c, :], WT[:, c, :], True, True, [L2b, WTb], [pb])
                WTn, WTnb = S.wt.next()
                tt(em, fl(WTn), pt[:, :], fl(WT), ALU.add, [pb, WTb], [WTnb])
                WT, WTb = WTn, WTnb
                if lvl < 4:
                    Np, Npb, Lp, Lpb = N2, N2b, L2, L2b
                yield
            st["WT"] = (WT, WTb)
            u_state[u[:3]] = st

        def dep(u):
            d, o, islat, S = u
            st = u_state.pop(u[:3])
            (At, Atb), (Rt, Rtb), (Bt, Btb), (Kt, Ktb), (Vt, Vtb) = st["ops"]
            (Bm, Bmb), (Km, Kmb), (Vm, Vmb) = st["trs"]
            (Lak, Lakb), _n, _l, (Trb, Trbb), (Trk, Trkb) = st["mks"]
            WT, WTb = st["WT"]
            H, Hh, Hbuf = H4[d], Hh4[d], H4b[d]
            pt, pb = PS_full.next()
            for c in range(4):
                mm(em, pt[:, c * 128:(c + 1) * 128], f2(At, c), Hh[:, c, :], True, False, [Atb, Hbuf], [pb])
                mm(em, pt[:, c * 128:(c + 1) * 128], Lak[:, c, :], Vm[:, c, :], False, True, [Lakb, Vmb], [pb])
            X1, X1b = S.x1.next()
            act(em, fl(X1), pt[:, :], AF.Copy, [pb], [X1b])
            yield
            pt, pb = PS_full.next()
            for c in range(4):
                mm(em, pt[:, c * 128:(c + 1) * 128], WT[:, c, :], X1[:, c, :], True, True, [WTb, X1b], [pb])
            U, Ub = S.u.next()
            em.op("dve", lambda e: e.tensor_copy(out=fl(U), in_=pt[:, :]), reads=[pb], writes=[Ub])
            yield
            if islat:
                pt, pb = PS_full.next()
                for c in range(4):
                    mm(em, pt[:, c * 128:(c + 1) * 128], f2(Rt, c), Hh[:, c, :], True, False, [Rtb, Hbuf], [pb])
                    mm(em, pt[:, c * 128:(c + 1) * 128], Trb[:, c, :], U[:, c, :], False, False, [Trbb, Ub], [pb])
                    mm(em, pt[:, c * 128:(c + 1) * 128], Trk[:, c, :], Vm[:, c, :], False, True, [Trkb, Vmb], [pb])
                Y, Yb = S.y.next()
                act(em, fl(Y), pt[:, :], AF.Copy, [pb], [Yb])
                for hh in range(2):
                    em.dma("sp", yD[d, o:o + CH, hh:8:2, :], Y[hh * 64:(hh + 1) * 64, :, hh * 64:(hh + 1) * 64],
                           reads=[Yb], writes=[yDb[d]])
            pt, pb = PS_full.next()
            for c in range(4):
                mm(em, pt[:, c * 128:(c + 1) * 128], Bm[:, c, :], U[:, c, :], True, False, [Bmb, Ub], [pb])
                mm(em, pt[:, c * 128:(c + 1) * 128], Km[:, c, :], Vm[:, c, :], False, True, [Kmb, Vmb], [pb])
            Hn, Hnb = S.hn.next()
            tt(em, fl(Hn), pt[:, :], fl(H), ALU.add, [pb, Hbuf], [Hnb])
            ci = o // CH
            gam = gall[:, d, :, ci:ci + 1].broadcast_to([128, 4, 128])
            tt(em, H[:], Hn[:], gam, ALU.mult, [Hnb, b_c], [Hbuf])
            tt(em, Hh[:], Hn[:], gam, ALU.mult, [Hnb, b_c], [Hbuf], eng="pool")
            yield

        def rr(gens):
            gens = list(gens)
            while gens:
                nxt = []
                for g_ in gens:
                    if next(g_, "done") != "done":
                        nxt.append(g_)
                gens = nxt

        steps = [[(d,) + seqs[d][i] + (sets[(d, i & 1)],) for d in range(2)] for i in range(NCH)]
        u_state = {}
        rr([indep(u) for u in steps[0]])
        for si in range(NCH):
            gens = [dep(u) for u in steps[si]]
            if si + 1 < NCH:
                gens += [indep(u) for u in steps[si + 1]]
            rr(gens)
        em.barrier()
    with ExitStack() as ph:
        def R_(name, shape, dt, n):
            return Ring(nc, ph, name, shape, dt, n)
        yfr = R_("o_yf", [128, 8, 64], F32, 2)
        yrr = R_("o_yr", [128, 8, 64], F32, 2)
        ycr = R_("o_yc", [128, 8, 64], F32, 2)
        sqr2 = R_("o_sq", [128, 8, 64], F32, 2)
        str_ = R_("o_st", [128, 2, 8], F32, 2)
        bvr = R_("o_bv", [128, 4, 128], F32, 2)
        gbr = R_("o_gb", [128, 4, 128], BF16, 2)
        obr = R_("o_ob", [128, 4, 128], BF16, 2)
        for n in range(TL // 128):
            t0 = n * 128
            yf, yfb = yfr.next()
            yr2, yrb = yrr.next()
            em.dma("sp", yf[:], yD[0, t0:t0 + 128, :, :], reads=[yDb[0]], writes=[yfb])
            em.dma("act", yr2[:], yD[1, t0:t0 + 128, :, :], reads=[yDb[1]], writes=[yrb])
            tt(em, yf[:], yf[:], yr2[:], ALU.add, [yfb, yrb], [yfb], eng="pool")
            st_, stb = str_.next()
            em.op("dve", lambda e: e.tensor_reduce(out=st_[:, 0, :], in_=yf[:], axis=AX.X, op=ALU.add), reads=[yfb], writes=[stb])
            ts(em, st_[:, 0, :], st_[:, 0, :], 1.0 / 64, None, ALU.mult, None, [stb], [stb])
            yc, ycb = ycr.next()
            tt(em, yc[:], yf[:], st_[:, 0, :][:, :, None].broadcast_to([128, 8, 64]), ALU.subtract, [yfb, stb], [ycb])
            sq, sqb = sqr2.next()
            act(em, sq[:], yc[:], AF.Square, [ycb], [sqb])
            em.op("dve", lambda e: e.tensor_reduce(out=st_[:, 1, :], in_=sq[:], axis=AX.X, op=ALU.add), reads=[sqb], writes=[stb])
            act(em, st_[:, 1, :], st_[:, 1, :], AF.Sqrt, [stb], [stb], scale=1.0 / 64, bias=GN_EPS)
            em.op("dve", lambda e: e.reciprocal(out=st_[:, 1, :], in_=st_[:, 1, :]), reads=[stb], writes=[stb])
            tt(em, yc[:], yc[:], st_[:, 1, :][:, :, None].broadcast_to([128, 8, 64]), ALU.mult, [ycb, stb], [ycb])
            yc2 = yc[:].rearrange("p h v -> p (h v)")
            tt(em, yc2, yc2, gnt[:, 0, :], ALU.mult, [ycb, b_c], [ycb], eng="pool")
            tt(em, yc2, yc2, gnt[:, 1, :], ALU.add, [ycb, b_c], [ycb], eng="pool")
            pt, pb = PS_full.next()
            for c in range(4):
                em.op("pe", lambda e, c=c: e.transpose(out=pt[:, c * 128:(c + 1) * 128], in_=yc2[:, c * 128:(c + 1) * 128],
                                                       identity=idf[:, :]), reads=[ycb, b_c], writes=[pb])
            bv, bvb = bvr.next()
            gb_, gbb = gbr.next()
            em.dma("sp", bv[:], bvD[:, :, t0:t0 + 128].rearrange("c p t -> p c t"), reads=[bvDb], writes=[bvb])
            em.dma("act", gb_[:], gbD[:, :, t0:t0 + 128].rearrange("c p t -> p c t"), reads=[gbDb], writes=[gbb])
            tt(em, bv[:].rearrange("p c t -> p (c t)"), pt[:, :], bv[:].rearrange("p c t -> p (c t)"), ALU.add, [pb, bvb], [bvb])
            ob_, obb = obr.next()
            tt(em, ob_[:], bv[:], gb_[:], ALU.mult, [bvb, gbb], [obb], eng="pool")
            em.dma("pool", oT[:, 4:8, t0:t0 + 128], ob_[:], reads=[obb], writes=[b_out])
        em.barrier()


A_Q, A_KV, B_W = 2048, 512, 2048
OFF = {"q": 0, "k": 2048, "v": 2560, "ga": 3072, "r": 5120, "kb": 7168, "vb": 9216, "gb": 11264, "lw": 13312, "la": 13504}
SWAP = np.concatenate([np.arange(32, 64), np.arange(0, 32), np.arange(96, 128), np.arange(64, 96)])


def l2_wcols(w_in, g):
    cols = []
    for h in range(4):
        cols.append(np.arange(128) + OFF["q"] + (4 * g + h) * 128)
    for h in range(4):
        cols.append(SWAP + OFF["q"] + (4 * g + h) * 128)
    cols.append(np.arange(128) + OFF["k"] + g * 128)
    cols.append(SWAP + OFF["k"] + g * 128)
    cols.append(np.arange(128) + OFF["v"] + g * 128)
    for h in range(4):
        cols.append(np.arange(128) + OFF["ga"] + (4 * g + h) * 128)
    for nm in ("r", "kb", "vb", "gb"):
        for c in range(4):
            cols.append(np.arange(128) + OFF[nm] + g * 512 + c * 128)
    for nm in ("lw", "la"):
        for d in range(2):
            cols.append(np.arange(96) + OFF[nm] + d * 96)
    return np.ascontiguousarray(w_in[:, np.concatenate(cols)])


def rope_tables(TL):
    t = np.arange(TL)
    row = (t // 64).astype(np.float32)
    col = (t % 64).astype(np.float32)
    inv = (np.float32(10000.0) ** (-np.arange(32, dtype=np.float32) / np.float32(32))).astype(np.float32)
    cosT = np.zeros((128, TL), np.float32)
    sinT = np.zeros((128, TL), np.float32)
    for p in range(128):
        idx = p % 64
        ang = (row if p < 64 else col) * inv[idx % 32]
        cosT[p] = np.cos(ang)
        sinT[p] = np.sin(ang) * (-1.0 if idx < 32 else 1.0)
    return cosT, sinT


def attn_masks():
    qi = np.arange(128)[:, None]
    j = np.arange(128)[None, :]
    lo = np.where(j >= qi, 0.0, NEG).astype(np.float32)
    hi = np.where(j <= qi, 0.0, NEG).astype(np.float32)
    z = np.zeros((128, 128), np.float32)
    ng = np.full((128, 128), NEG, np.float32)
    m = np.stack([np.concatenate([ng, z, hi], 1), np.concatenate([lo, z, hi], 1), np.concatenate([lo, z, ng], 1)], axis=1)
    return np.ascontiguousarray(m)


def rwkv_masks():
    i = np.arange(64)[:, None]
    j = np.arange(64)[None, :]
    pats = [(i < j), (i <= j), (i > j), (i >= j)]
    out = np.zeros((128, 4, 128), np.float32)
    for k, p in enumerate(pats):
        out[:, k, :] = np.tile(p.astype(np.float32), (2, 2))
    return out


def l2_rwkv_inputs(g, conv, w0, w2, a0, a2, k_k, k_a, r_k, gn_w, gn_b):
    sl = slice(g * 512, (g + 1) * 512)
    pc = lambda v: np.ascontiguousarray(np.asarray(v[sl], np.float32).reshape(4, 128).T)
    cv = np.zeros((128, 3, 3, 4), np.float32)
    for kind in range(3):
        for tap in range(3):
            cv[:, kind, tap, :] = pc(conv[tap, kind * 2048:(kind + 1) * 2048])
    pv = np.zeros((128, 9, 4), np.float32)
    for i, v in enumerate((w0[0], w0[1], a0[0], a0[1], k_k, k_a, r_k)):
        pv[:, i, :] = pc(v)
    blk1 = np.zeros((128, 128), np.float32)
    blk1[:64, :64] = 1
    blk1[64:, 64:] = 1
    return {
        "cv": cv, "pv": pv,
        "w2c": np.ascontiguousarray(np.transpose(w2[:, :, sl], (1, 0, 2))),
        "a2c": np.ascontiguousarray(np.transpose(a2[:, :, sl], (1, 0, 2))),
        "gnb": np.ascontiguousarray(np.broadcast_to(np.stack([gn_w[sl], gn_b[sl]])[None], (128, 2, 512))).astype(np.float32),
        "rmask": rwkv_masks(), "blk1": blk1,
    }


BATCH, SEQ, CTX = 2, 8192, 256
_BF16_NP = mybir.dt.np(BF16)


def kernel(x, c, ctx, c_ctx, mod_w, mod_b, norm_g, ab_w_in, ab_w_out, attn_sink, rwkv_conv, rwkv_w0, rwkv_w2,
           rwkv_a0, rwkv_a2, rwkv_k_k, rwkv_k_a, rwkv_r_k, rwkv_gn_w, rwkv_gn_b, gm_w_in, gm_ln_g, gm_ln_b,
           gm_w_s, gm_b_s, gm_w_out, final_g):
    f32 = lambda a: np.asarray(a, dtype=np.float32)
    x, c, ctx, c_ctx = f32(x), f32(c), f32(ctx), f32(c_ctx)
    mod = run_mod(c, c_ctx, f32(mod_w), f32(mod_b))
    shift = mod[:, :, 0:D]
    scale = mod[:, :, D:2 * D]
    gate = mod[:, :, 2 * D:3 * D]
    w_in = f32(ab_w_in[0])
    cosT, sinT = rope_tables(SEQ)
    ident = np.eye(128, dtype=np.float32)
    am = attn_masks()
    xTb = [fmT(x[b]) for b in range(BATCH)]
    cTb = [fmT(ctx[b]) for b in range(BATCH)]
    wcs = [l2_wcols(w_in, g) for g in range(4)]
    rw = [l2_rwkv_inputs(g, f32(rwkv_conv[0]), f32(rwkv_w0[0]), f32(rwkv_w2[0]), f32(rwkv_a0[0]), f32(rwkv_a2[0]),
                         f32(rwkv_k_k[0]), f32(rwkv_k_a[0]), f32(rwkv_r_k[0]), f32(rwkv_gn_w[0]), f32(rwkv_gn_b[0]))
          for g in range(4)]
    in_maps = []
    for i in range(NCORES):
        b, g = i // 4, i % 4
        vec = np.stack([fm(v, KC) for v in (norm_g[0], scale[0, b], shift[0, b], scale[0, 2], shift[0, 2], shift[0, 2])], axis=1)
        m = {"xT": xTb[b], "cT": cTb[b], "wcols": wcs[g], "vec": np.ascontiguousarray(vec),
             "sinkb": np.ascontiguousarray(np.broadcast_to(f32(attn_sink[0])[None, 4 * g:4 * g + 4], (128, 4))),
             "cosT": cosT, "sinT": sinT, "ident": ident, "amask": am}
        m.update(rw[g])
        in_maps.append(m)
    nc2 = build_l2(SEQ, CTX, do_b=True)
    res2 = run_bass_kernel_spmd(nc2, in_maps, core_ids=list(range(NCORES)))
    del in_maps, xTb, cTb, wcs
    oTf = np.zeros((BATCH, 128, KC, SEQ), dtype=_BF16_NP)
    for i in range(NCORES):
        b, g = i // 4, i % 4
        o = res2.results[i]["oT"]
        oTf[b, :, g * 4:(g + 1) * 4, :] = o[:, 0:4, :]
        oTf[b, :, 16 + g * 4:16 + (g + 1) * 4, :] = o[:, 4:8, :]
    del res2
    TCORE = BATCH * SEQ // NCORES
    w_out, gw_in, gw_out = f32(ab_w_out[0]), f32(gm_w_in[0]), f32(gm_w_out[0])
    in_maps = []
    for i in range(NCORES):
        b, s = i // 4, (i % 4) * TCORE
        in_maps.append(l3_core_inputs(x[b, s:s + TCORE], np.ascontiguousarray(oTf[b, :, :, s:s + TCORE]), w_out, gw_in, gw_out,
                                      gate[0, b], norm_g[1], scale[1, b], shift[1, b], gate[1, b], final_g,
                                      f32(gm_ln_g[0]), f32(gm_ln_b[0]), f32(gm_w_s[0]), f32(gm_b_s[0])))
    nc3 = build_l3(TCORE, 16, 1024)
    res3 = run_bass_kernel_spmd(nc3, in_maps, core_ids=list(range(NCORES)))
    del in_maps
    out = np.empty((BATCH, SEQ, D), np.float32)
    for i in range(NCORES):
        b, s = i // 4, (i % 4) * TCORE
        out[b, s:s + TCORE] = res3.results[i]["outT"].transpose(2, 1, 0).reshape(TCORE, D)
    return out
```
